# Optimizing a Trainium2 kernel written in Bass

```python
import math
import jax, jax.numpy as jnp
from jax import lax
import numpy as np

D_MODEL = 1024
BATCH = 8
SEQ = 4096
DEPTH = 1

PLE_DIM = 256
N_HEADS = 8
HEAD_DIM = 64
N_KV_HEADS = 2
HEADS_PER_GROUP = N_HEADS // N_KV_HEADS
L_CMP = 32
D_STRIDE = 16
CMP_HIDDEN = 256
L_SLC = 64
N_SEL = 16
WINDOW = 512
Q_BLOCK = 128
N_BUCKETS = 32
MAX_EXACT = N_BUCKETS // 2
MAX_DISTANCE = 128
D_CONV = D_MODEL // 2
CONV_WIDTH = 31
D_FF = 2816
FFN_CONV_WIDTH = 3

EPS = 1e-6
NEG = -1e30
FORCE = 1e4

N_CONV_IN = 2 * D_CONV
N_Q = N_HEADS * HEAD_DIM
N_KV = 6 * N_KV_HEADS * HEAD_DIM
N_NSA_GATE = 3 * N_HEADS
N_MERGE = 2 * D_MODEL
N_IN = N_CONV_IN + N_Q + N_KV + N_NSA_GATE + N_MERGE

kernel_name = "hybrid_conformer_nsa_gated_block"


def rmsnorm(x, g):
    xf = x.astype(jnp.float32)
    y = xf * lax.rsqrt(jnp.mean(xf * xf, axis=-1, keepdims=True) + EPS)
    return (y * g.astype(jnp.float32)).astype(x.dtype)


def layernorm(x, g, b):
    xf = x.astype(jnp.float32)
    mu = jnp.mean(xf, axis=-1, keepdims=True)
    var = jnp.mean(jnp.square(xf - mu), axis=-1, keepdims=True)
    y = (xf - mu) * lax.rsqrt(var + EPS)
    return (y * g.astype(jnp.float32) + b.astype(jnp.float32)).astype(x.dtype)


def causal_dwconv(x, w, b):
    k, c = w.shape
    y = lax.conv_general_dilated(x, w[:, None, :].astype(x.dtype), (1,), [(k - 1, 0)],
                                 dimension_numbers=('NWC', 'WIO', 'NWC'),
                                 feature_group_count=c)
    return y + b.astype(x.dtype)


def t5_bucket(dist):
    n = jnp.maximum(dist, 0)
    nf = jnp.maximum(n, MAX_EXACT).astype(jnp.float32)
    large = MAX_EXACT + (jnp.log(nf / MAX_EXACT) / math.log(MAX_DISTANCE / MAX_EXACT)
                         * (N_BUCKETS - MAX_EXACT)).astype(jnp.int32)
    large = jnp.minimum(large, N_BUCKETS - 1)
    return jnp.where(n < MAX_EXACT, n, large)


def masked_softmax(s, mask):
    p = jax.nn.softmax(jnp.where(mask, s, NEG), axis=-1)
    return jnp.where(mask, p, 0.0)


def compress(kv, pe, w1, w2):
    b, s, g, dh = kv.shape
    nc = (s - L_CMP) // D_STRIDE + 1
    idx = jnp.arange(nc)[:, None] * D_STRIDE + jnp.arange(L_CMP)[None, :]
    blocks = kv[:, idx] + pe[None, None, :, None, :]
    blocks = blocks.transpose(0, 1, 3, 2, 4).reshape(b, nc, g, L_CMP * dh)
    return jax.nn.gelu(blocks @ w1) @ w2


def nsa_attention(q, k_cmp, v_cmp, k_slc, v_slc, k_win, v_win, gates, rel_bias):
    b, s, g, hpg, dh = q.shape
    nc = k_cmp.shape[1]
    ns = s // L_SLC
    n_sel = min(N_SEL, ns)
    nqb = s // Q_BLOCK
    scale = 1.0 / math.sqrt(dh)

    c_start = jnp.arange(nc) * D_STRIDE
    c_end = c_start + L_CMP - 1
    s_start = jnp.arange(ns) * L_SLC
    agg = ((c_end[:, None] >= s_start[None, :]) &
           (c_start[:, None] <= s_start[None, :] + L_SLC - 1)).astype(jnp.float32)

    k_blocks = k_slc.reshape(b, ns, L_SLC, g, dh).transpose(0, 3, 1, 2, 4)
    v_blocks = v_slc.reshape(b, ns, L_SLC, g, dh).transpose(0, 3, 1, 2, 4)
    k_pad = jnp.pad(k_win, ((0, 0), (WINDOW, 0), (0, 0), (0, 0)))
    v_pad = jnp.pad(v_win, ((0, 0), (WINDOW, 0), (0, 0), (0, 0)))
    bias_grouped = rel_bias.reshape(N_BUCKETS, g, hpg).transpose(1, 0, 2)
    bi = jnp.arange(b)[:, None, None, None]
    gi = jnp.arange(g)[None, :, None, None]

    def head_bias(dist):
        bb = rel_bias[t5_bucket(dist)].astype(jnp.float32)
        return bb.reshape(dist.shape + (g, hpg)).transpose(2, 3, 0, 1)

    def block(qi):
        t0 = qi * Q_BLOCK
        qb = lax.dynamic_slice_in_dim(q, t0, Q_BLOCK, axis=1)
        gb = lax.dynamic_slice_in_dim(gates, t0, Q_BLOCK, axis=1)
        tpos = t0 + jnp.arange(Q_BLOCK)

        dist_c = tpos[:, None] - c_end[None, :]
        s_c = jnp.einsum('bqghd,bcgd->bghqc', qb, k_cmp).astype(jnp.float32) * scale
        p_c = masked_softmax(s_c + head_bias(dist_c), dist_c >= 0)
        o_c = jnp.einsum('bghqc,bcgd->bqghd', p_c.astype(v_cmp.dtype), v_cmp)

        imp = jnp.einsum('bghqc,cs->bgqs', p_c, agg)
        blk = jnp.arange(ns)[None, :]
        cur = (tpos // L_SLC)[:, None]
        valid = blk * L_SLC <= tpos[:, None]
        forced = (blk == 0) | (blk == cur) | (blk == cur - 1)
        imp = jnp.where(forced, FORCE, jnp.where(valid, imp, -FORCE))
        _, idx = lax.top_k(imp, n_sel)
        kb = k_blocks[bi, gi, idx].reshape(b, g, Q_BLOCK, n_sel * L_SLC, dh)
        vb = v_blocks[bi, gi, idx].reshape(b, g, Q_BLOCK, n_sel * L_SLC, dh)
        kpos = (idx[..., None] * L_SLC + jnp.arange(L_SLC)).reshape(b, g, Q_BLOCK, n_sel * L_SLC)
        dist_s = tpos[None, None, :, None] - kpos
        bias_s = bias_grouped[gi, t5_bucket(dist_s)].astype(jnp.float32).transpose(0, 1, 4, 2, 3)
        s_s = jnp.einsum('bqghd,bgqkd->bghqk', qb, kb).astype(jnp.float32) * scale + bias_s
        p_s = masked_softmax(s_s, (dist_s >= 0)[:, :, None])
        o_s = jnp.einsum('bghqk,bgqkd->bqghd', p_s.astype(vb.dtype), vb)

        kw = lax.dynamic_slice_in_dim(k_pad, t0, Q_BLOCK + WINDOW, axis=1)
        vw = lax.dynamic_slice_in_dim(v_pad, t0, Q_BLOCK + WINDOW, axis=1)
        kwpos = t0 - WINDOW + jnp.arange(Q_BLOCK + WINDOW)
        dist_w = tpos[:, None] - kwpos[None, :]
        mask_w = (dist_w >= 0) & (dist_w < WINDOW) & (kwpos[None, :] >= 0)
        s_w = jnp.einsum('bqghd,bkgd->bghqk', qb, kw).astype(jnp.float32) * scale
        p_w = masked_softmax(s_w + head_bias(dist_w), mask_w)
        o_w = jnp.einsum('bghqk,bkgd->bqghd', p_w.astype(vw.dtype), vw)

        return gb[..., 0:1] * o_c + gb[..., 1:2] * o_s + gb[..., 2:3] * o_w

    out = lax.map(block, jnp.arange(nqb))
    return jnp.moveaxis(out, 0, 1).reshape(b, s, g * hpg * dh)


def setup_inputs(seed: int = 0) -> dict:
    key = jax.random.key(seed)
    ks = jax.random.split(key, 27)
    f32 = jnp.float32

    def nrm(k, shape, scale):
        return jax.random.normal(k, shape, f32) * scale

    def gain(k, shape):
        return 1.0 + 0.01 * jax.random.normal(k, shape, f32)

    L = DEPTH
    return {
        "x": nrm(ks[0], (BATCH, SEQ, D_MODEL), 1.0),
        "p": nrm(ks[1], (DEPTH, BATCH, SEQ, PLE_DIM), 1.0),
        "rel_bias": nrm(ks[2], (N_BUCKETS, N_HEADS), 0.1),
        "norm_mix": gain(ks[3], (L, D_MODEL)),
        "w_in": nrm(ks[4], (L, D_MODEL, N_IN), D_MODEL ** -0.5),
        "conv_dw_w": nrm(ks[5], (L, CONV_WIDTH, D_CONV), CONV_WIDTH ** -0.5),
        "conv_dw_b": nrm(ks[6], (L, D_CONV), 0.01),
        "conv_ln_g": gain(ks[7], (L, D_CONV)),
        "conv_ln_b": nrm(ks[8], (L, D_CONV), 0.01),
        "w_conv_out": nrm(ks[9], (L, D_CONV, D_MODEL), D_CONV ** -0.5),
        "cmp_pe_k": nrm(ks[10], (L, L_CMP, HEAD_DIM), 0.1),
        "cmp_pe_v": nrm(ks[11], (L, L_CMP, HEAD_DIM), 0.1),
        "w_ck1": nrm(ks[12], (L, L_CMP * HEAD_DIM, CMP_HIDDEN), (L_CMP * HEAD_DIM) ** -0.5),
        "w_ck2": nrm(ks[13], (L, CMP_HIDDEN, HEAD_DIM), CMP_HIDDEN ** -0.5),
        "w_cv1": nrm(ks[14], (L, L_CMP * HEAD_DIM, CMP_HIDDEN), (L_CMP * HEAD_DIM) ** -0.5),
        "w_cv2": nrm(ks[15], (L, CMP_HIDDEN, HEAD_DIM), CMP_HIDDEN ** -0.5),
        "w_attn_out": nrm(ks[16], (L, N_HEADS * HEAD_DIM, D_MODEL), (N_HEADS * HEAD_DIM) ** -0.5),
        "w_out": nrm(ks[17], (L, D_MODEL, D_MODEL), D_MODEL ** -0.5),
        "norm_ffn": gain(ks[18], (L, D_MODEL)),
        "w_up": nrm(ks[19], (L, D_MODEL, 2 * D_FF), D_MODEL ** -0.5),
        "ffn_dw_w": nrm(ks[20], (L, FFN_CONV_WIDTH, 2 * D_FF), FFN_CONV_WIDTH ** -0.5),
        "ffn_dw_b": nrm(ks[21], (L, 2 * D_FF), 0.01),
        "w_down": nrm(ks[22], (L, D_FF, D_MODEL), D_FF ** -0.5),
        "norm_ple": gain(ks[23], (L, D_MODEL)),
        "w_ple_gate": nrm(ks[24], (L, D_MODEL, D_MODEL), D_MODEL ** -0.5),
        "w_ple": nrm(ks[25], (L, PLE_DIM, D_MODEL), PLE_DIM ** -0.5),
        "norm_final": gain(ks[26], (D_MODEL,)),
    }


def reference(x, p, rel_bias, norm_mix, w_in, conv_dw_w, conv_dw_b, conv_ln_g, conv_ln_b,
              w_conv_out, cmp_pe_k, cmp_pe_v, w_ck1, w_ck2, w_cv1, w_cv2, w_attn_out, w_out,
              norm_ffn, w_up, ffn_dw_w, ffn_dw_b, w_down, norm_ple, w_ple_gate, w_ple, norm_final):
    b, s, d = x.shape
    g, hpg, dh = N_KV_HEADS, HEADS_PER_GROUP, HEAD_DIM
    splits = [N_CONV_IN, N_CONV_IN + N_Q, N_CONV_IN + N_Q + N_KV,
              N_CONV_IN + N_Q + N_KV + N_NSA_GATE]
    for i in range(DEPTH):
        h = rmsnorm(x, norm_mix[i])
        z = h @ w_in[i]
        u, qf, kvf, ng, mg = jnp.split(z, splits, axis=-1)

        a = u[..., :D_CONV] * jax.nn.sigmoid(u[..., D_CONV:])
        a = causal_dwconv(a, conv_dw_w[i], conv_dw_b[i])
        a = jax.nn.silu(layernorm(a, conv_ln_g[i], conv_ln_b[i]))
        y_conv = a @ w_conv_out[i]

        q = qf.reshape(b, s, g, hpg, dh)
        kv = kvf.reshape(b, s, 6, g, dh)
        k_c, v_c, k_s, v_s, k_w, v_w = (kv[:, :, 0], kv[:, :, 1], kv[:, :, 2],
                                        kv[:, :, 3], kv[:, :, 4], kv[:, :, 5])
        k_cmp = compress(k_c, cmp_pe_k[i], w_ck1[i], w_ck2[i])
        v_cmp = compress(v_c, cmp_pe_v[i], w_cv1[i], w_cv2[i])
        nsa_gates = jax.nn.sigmoid(ng).reshape(b, s, g, hpg, 3)
        o = nsa_attention(q, k_cmp, v_cmp, k_s, v_s, k_w, v_w, nsa_gates, rel_bias)
        y_attn = o @ w_attn_out[i]

        y = jax.nn.sigmoid(mg[..., :d]) * y_conv + jax.nn.sigmoid(mg[..., d:]) * y_attn
        x = x + y @ w_out[i]

        hf = rmsnorm(x, norm_ffn[i])
        up = causal_dwconv(hf @ w_up[i], ffn_dw_w[i], ffn_dw_b[i])
        gate, val = jnp.split(up, 2, axis=-1)
        x = x + (jax.nn.gelu(gate) * val) @ w_down[i]

        pg = jax.nn.sigmoid(rmsnorm(x, norm_ple[i]) @ w_ple_gate[i])
        x = x + pg * (p[i] @ w_ple[i])
    return rmsnorm(x, norm_final)
```

```python
import math
import numpy as np
import concourse.bass as bass
import concourse.mybir as mybir
from concourse.bass_utils import run_bass_kernel_spmd
from contextlib import ExitStack

F32 = mybir.dt.float32
BF16 = mybir.dt.bfloat16
AF = mybir.ActivationFunctionType
ALU = mybir.AluOpType

D = 1024
SEQ = 4096
TS = 512
NT = SEQ // TS
TT = TS // 128
D_FF = 2816
NFF = D_FF // 128
NEGM = -240.0
EPS = 1e-6
SLABW = 1024
NSLOT = 12
PREF = 7
NCAST = 24
DEBUG = False
MARKS = []


class Buf:
    __slots__ = ("name", "w", "r")

    def __init__(self, name=""):
        self.name = name
        self.w = None
        self.r = {}


class Sync:
    def __init__(self, nc, es):
        self.nc = nc
        self.es = es
        self.sems = {}
        self.count = {}
        self.waited = {}
        self.nissued = {}
        self.engs = {"pe": nc.tensor, "act": nc.scalar, "dve": nc.vector,
                     "pool": nc.gpsimd, "sp": nc.sync}
        for k in self.engs:
            self.new_sem(k)
            self.waited[k] = {}

    def new_sem(self, key):
        h = self.es.enter_context(self.nc.semaphore("s_" + key))
        self.sems[key] = h
        self.count[key] = 0
        return key

    def _wait(self, ename, deps):
        e = self.engs[ename]
        wd = self.waited[ename]
        for key, val in deps.items():
            assert val <= self.count[key], ("wait on unissued increment", ename, key, val, self.count[key])
            if wd.get(key, 0) < val:
                e.wait_ge(self.sems[key], val)
                wd[key] = val

    @staticmethod
    def _deps(reads, writes, selfkey=None, acc=False):
        deps = {}

        def add(k, v):
            if deps.get(k, 0) < v:
                deps[k] = v
        for b in reads:
            if b.w is not None:
                add(*b.w)
        for b in writes:
            if b.w is not None and not (acc and b.w[0] == selfkey):
                add(*b.w)
            for k, v in b.r.items():
                if acc and k == selfkey:
                    continue
                add(k, v)
        return deps

    def op(self, ename, fn, reads=(), writes=(), acc=False, inc=True):
        deps = self._deps(reads, writes, ename, acc)
        self._wait(ename, deps)
        ins = fn(self.engs[ename])
        self.nissued[ename] = self.nissued.get(ename, 0) + 1
        if inc:
            self.count[ename] += 1
            v = self.count[ename]
            ins.then_inc(self.sems[ename], 1)
        else:
            v = self.count[ename] + 1
        for b in reads:
            if b.r.get(ename, 0) < v:
                b.r[ename] = v
        for b in writes:
            b.w = (ename, v)
            b.r = {}
        return ins

    def dma(self, qname, semkey, out, in_, reads=(), writes=(), **kw):
        deps = self._deps(reads, writes)
        self._wait(qname, deps)
        ins = self.engs[qname].dma_start(out=out, in_=in_, **kw)
        self.count[semkey] += 16
        v = self.count[semkey]
        ins.then_inc(self.sems[semkey], 16)
        for b in reads:
            if b.r.get(semkey, 0) < v:
                b.r[semkey] = v
        for b in writes:
            b.w = (semkey, v)
            b.r = {}
        return ins

    def wait_all(self, ename, bufs):
        deps = {}
        for b in bufs:
            items = list(b.r.items())
            if b.w is not None:
                items.append(b.w)
            for k, v in items:
                if deps.get(k, 0) < v:
                    deps[k] = v
        self._wait(ename, deps)


def slab_list():
    L = []
    for j in range(4):
        L.append(("u", j))
        L.append(("u", j + 4))
    for hp in range(4):
        L.append(("q", hp))
    for idx in (0, 1, 2, 4):
        L.append(("kv", idx))
    for idx in (3, 5):
        L.append(("vtok", idx))
    L.append(("ng", 0))
    for kv in range(2):
        for s in range(8):
            L.append(("w1", kv, s))
    L.append(("w2", 0))
    for j in range(8):
        L.append(("ca", j))
        L.append(("mgA", j))
        L.append(("mgB", j))
    for half in range(2):
        for s in range(4):
            L.append(("out", half, s))
    for i in range(NFF):
        L.append(("upg", i))
        L.append(("upv", i))
    for half in range(2):
        for s in range(NFF // 2):
            L.append(("down", half, s))
    for half in range(2):
        L.append(("ple", half))
        for s in range(4):
            L.append(("pgate", half, s))
    return L


SLABS = slab_list()
NSLAB = len(SLABS)
assert NSLAB % NCAST == 0, NSLAB


def _pack_fm(W, c0, ncols, KC):
    out = np.zeros((128, SLABW), np.float32)
    blk = W[:KC * 128, c0:c0 + ncols].reshape(KC, 128, ncols).transpose(1, 0, 2)
    out[:, :KC * ncols] = blk.reshape(128, KC * ncols)
    return out


def _pack_tm(W, half, s):
    out = np.zeros((128, SLABW), np.float32)
    blk = W[s * 256:(s + 1) * 256, half * 512:(half + 1) * 512].reshape(2, 128, 512).transpose(1, 0, 2)
    out[:, :] = blk.reshape(128, 1024)
    return out


def build_wflat(w_in, w_conv_out, w_ck1, w_ck2, w_cv1, w_cv2, w_attn_out, w_out, w_up, w_down,
                w_ple_gate, w_ple):
    wf = np.zeros((128, NSLAB * SLABW), np.float32)
    for i, sl in enumerate(SLABS):
        k = sl[0]
        if k == "u":
            a = _pack_fm(w_in, sl[1] * 128, 128, 8)
        elif k == "q":
            a = _pack_fm(w_in, 1024 + sl[1] * 128, 128, 8)
        elif k in ("kv", "vtok"):
            a = _pack_fm(w_in, 1536 + sl[1] * 128, 128, 8)
        elif k == "ng":
            a = _pack_fm(w_in, 2304, 24, 8)
        elif k == "w1":
            w1 = w_ck1 if sl[1] == 0 else w_cv1
            a = np.zeros((128, SLABW), np.float32)
            blk = w1[sl[2] * 256:(sl[2] + 1) * 256, :].reshape(4, 64, 256).transpose(1, 0, 2)
            a[:64, :] = blk.reshape(64, 1024)
        elif k == "w2":
            a = np.zeros((128, SLABW), np.float32)
            for kv, w2 in enumerate((w_ck2, w_cv2)):
                for hc in range(2):
                    a[:, (kv * 2 + hc) * 64:(kv * 2 + hc + 1) * 64] = w2[hc * 128:(hc + 1) * 128, :]
        elif k == "ca":
            a = np.zeros((128, SLABW), np.float32)
            a[:, 0:512] = _pack_fm(w_conv_out, sl[1] * 128, 128, 4)[:, 0:512]
            a[:, 512:1024] = _pack_fm(w_attn_out, sl[1] * 128, 128, 4)[:, 0:512]
        elif k == "mgA":
            a = _pack_fm(w_in, 2328 + sl[1] * 128, 128, 8)
        elif k == "mgB":
            a = _pack_fm(w_in, 2328 + 1024 + sl[1] * 128, 128, 8)
        elif k == "out":
            a = _pack_tm(w_out, sl[1], sl[2])
        elif k == "upg":
            a = _pack_fm(w_up, sl[1] * 128, 128, 8)
        elif k == "upv":
            a = _pack_fm(w_up, D_FF + sl[1] * 128, 128, 8)
        elif k == "down":
            a = _pack_tm(w_down, sl[1], sl[2])
        elif k == "ple":
            a = _pack_tm(w_ple, sl[1], 0)
        elif k == "pgate":
            a = _pack_tm(w_ple_gate, sl[1], sl[2])
        else:
            raise ValueError(k)
        wf[:, i * SLABW:(i + 1) * SLABW] = a
    return wf


def t5_bucket_np(n):
    n = np.maximum(n, 0)
    nf = np.maximum(n, 16).astype(np.float32)
    large = 16 + (np.log(nf / np.float32(16)) / np.float32(math.log(128 / 16)) * np.float32(16)).astype(np.int32)
    large = np.minimum(large, 31)
    return np.where(n < 16, n, large)


def build_consts():
    c = {}
    n = np.arange(-256, 256)
    oh = np.zeros((33, 512), np.float32)
    bk = t5_bucket_np(n)
    for i, nn in enumerate(n):
        if nn >= 0:
            oh[bk[i], i] = 1.0
        else:
            oh[32, i] = 1.0
    c["oh"] = oh
    k = np.arange(4096)
    c["efull"] = (k[None, :] // 64 == np.arange(64)[:, None]).astype(np.float32)
    x = np.arange(504)
    j = x - 248
    sm = np.zeros((17, 504), np.float32)
    for r in range(16):
        sm[r, j == r - 9] = 1.0
    sm[16, j >= 7] = 1.0
    c["selm"] = sm
    cc = np.arange(256)
    cs, ce = cc * 16, cc * 16 + 31
    ss = np.arange(64) * 64
    agg = ((ce[:, None] >= ss[None, :]) & (cs[:, None] <= ss[None, :] + 63)).astype(np.float32)
    agg[255, :] = 0.0
    aa = np.zeros((128, 2, 65), np.float32)
    aa[:, :, :64] = agg.reshape(2, 128, 64).transpose(1, 0, 2)
    aa[:, :, 64] = 1.0
    c["aggaug"] = aa.reshape(128, 130)
    p = np.arange(128)
    c["w4"] = np.where(p[:, None] > p[None, :], 0.0, NEGM).astype(np.float32)
    i = np.arange(128)[:, None]
    ci = (i >= 64).astype(np.int64)
    rel = (np.arange(126) - 62)[None, :]
    vm = (rel < ci - 1).astype(np.float32)
    am = np.zeros((128, 126), np.float32)
    am = np.where(rel == ci, 2e4, am)
    am = np.where(rel == ci - 1, 3e4, am)
    am = np.where(rel > ci, -1e4 - 8.0 * np.arange(126)[None, :], am)
    c["vm"] = vm.astype(np.float32)
    c["am"] = am.astype(np.float32)
    c["ident"] = np.eye(128, dtype=np.float32)
    return c


def build_program():
    nc = bass.Bass("TRN2", target_bir_lowering=False)

    def din(name, shape, dt=F32):
        return nc.dram_tensor(name, list(shape), dt, kind="ExternalInput").ap()

    x_d = din("x", [SEQ, D])
    p_d = din("p", [SEQ, 256])
    wflat_d = din("wflat", [128, NSLAB * SLABW])
    gains_d = din("gains", [4, D])
    relb_d = din("relb", [32, 8])
    cvw_d = din("cvw", [128, 4 * 31])
    cvp_d = din("cvp", [128, 12])
    ffw_d = din("ffw", [128, 44 * 4])
    pet_d = din("pet", [64, 64])
    oh_d = din("oh", [33, 512])
    efull_d = din("efull", [64, 4096])
    selm_d = din("selm", [17, 504])
    aggaug_d = din("aggaug", [128, 130])
    w4_d = din("w4", [128, 128])
    vm_d = din("vm", [128, 126])
    am_d = din("am", [128, 126])
    ident_d = din("ident", [128, 128])
    y_d = nc.dram_tensor("y", [SEQ, D], F32, kind="ExternalOutput").ap()
    wb_d = nc.dram_tensor("wb", [128, NSLAB * SLABW], BF16, kind="Internal").ap()
    tv_d = nc.dram_tensor("tvd", [8, 128 * 512], F32, kind="Internal").ap()
    dbg = {}
    if DEBUG:
        dbg["x1"] = nc.dram_tensor("dbg_x1", [SEQ, D], F32, kind="ExternalOutput").ap()
        dbg["oT"] = nc.dram_tensor("dbg_oT", [128, 4 * SEQ], BF16, kind="ExternalOutput").ap()
        dbg["aT"] = nc.dram_tensor("dbg_aT", [128, 4 * SEQ], BF16, kind="ExternalOutput").ap()

    with ExitStack() as es:
        S = Sync(nc, es)
        used = [0]

        def sb(name, shape, dt):
            n = 1
            for s_ in shape[1:]:
                n *= s_
            used[0] += n * (4 if dt == F32 else 2)
            return es.enter_context(nc.sbuf_tensor("sb_" + name, list(shape), dt))

        banks = [es.enter_context(nc.psum_tensor("bank%d" % i, [128, 512], F32)) for i in range(8)]
        bbank = [Buf("bank%d" % i) for i in range(8)]

        for k in range(NSLOT):
            S.new_sem("ws%d" % k)
            S.new_sem("wbk%d" % k)
        for k in range(NCAST):
            S.new_sem("cast%d" % k)
        for k in ("xl0", "xl1", "xl2", "xl3", "xa0", "xa1", "xa2", "xa3", "pl0", "pl1", "pl2", "pl3", "gl", "ld", "ld1", "ld2", "ld3", "ld4", "ld5", "ld6", "ld7", "ld8", "ld9", "ld10", "ld11", "ld12", "ld13", "st0", "st1", "st2", "st3", "st4", "st5", "st6", "st7", "vc",
                  "tvs", "dbg"):
            S.new_sem(k)

        xres = sb("xres", [128, TT, D], F32)
        bx = [Buf("xres%d" % t) for t in range(TT)]
        gbc = sb("gbc", [128, D], F32); bgbc = Buf("gbc")
        hT = sb("hT", [128, 8, TS], BF16); bhT = Buf("hT")
        hn = sb("hn", [128, 2, D], BF16); bhn = [Buf("hn0"), Buf("hn1")]
        abuf = sb("abuf", [128, 4, 30 + TS], BF16); babuf = [Buf("abuf%d" % j) for j in range(4)]
        ysb = sb("ysb", [128, 4, TS], F32); bysb = [Buf("ysb%d" % j) for j in range(4)]
        big = sb("big", [128, NFF, TS], BF16); bbig = [Buf("big%d" % j) for j in range(NFF)]
        Q = sb("Q", [128, 8, TS], BF16)
        bQlo = [[Buf("Qlo%d_%d" % (g, t)) for t in range(TT)] for g in range(2)]
        bQhi = [[Buf("Qhi%d_%d" % (g, t)) for t in range(TT)] for g in range(2)]
        KsT = [sb("KsT%d" % g, [128, SEQ], BF16) for g in range(2)]
        bKs = [[Buf("Ks%d_%d" % (g, t)) for t in range(NT)] for g in range(2)]
        KwT = [sb("KwT%d" % g, [65, 1024], BF16) for g in range(2)]
        bKw = [[Buf("Kw%d_%d" % (g, t)) for t in range(2)] for g in range(2)]
        Vs = sb("Vs", [128, 32, 2, 65], BF16); bVs = [Buf("Vs%d" % t) for t in range(NT)]
        Vw = sb("Vw", [128, 8, 2, 65], BF16); bVw = [Buf("Vw%d" % t) for t in range(2)]
        kcT = [[sb("kcT%d_%d" % (kv, g), [64, 16 + TS], BF16) for g in range(2)] for kv in range(2)]
        bkcT = [[Buf() for g in range(2)] for kv in range(2)]
        kcmpT = [sb("kcmpT%d" % g, [65, 256], BF16) for g in range(2)]
        bkcmp = [Buf() for g in range(2)]
        vcmp = sb("vcmp", [128, 2, 2, 65], BF16); bvcmp = Buf("vcmp")
        vst = sb("vst", [32, 2, 65], BF16); bvst = Buf("vst")
        hid = sb("hid", [128, 2, 2, 2, 32], BF16); bhid = [Buf(), Buf()]
        cconst = sb("cconst", [128, 2, 2], F32); bcconst = Buf("cconst")
        dg = sb("dg", [128, 4, 31, 128], BF16); bdgc = [Buf("dg%d" % c) for c in range(4)]
        gates = sb("gates", [128, TT, 24], F32); bgates = [Buf() for t in range(TT)]
        Pc = sb("Pc", [128, 2, 512], BF16); bPc = [Buf(), Buf()]
        NPS = 4
        Ps = sb("Ps", [128, NPS, 512], BF16); bPs = [Buf() for _ in range(NPS)]
        osc = sb("osc", [128, 4, 256], BF16); bosc = [Buf() for _ in range(4)]
        uhalo = sb("uhalo", [128, 2 * NFF, 2], BF16); buhalo = [Buf() for _ in range(2 * NFF)]
        dgf = sb("dgf", [128, 2, 6, 128], BF16); bdgf = [Buf(), Buf()]
        pT = sb("pT", [128, 2, TS], BF16); bpT = Buf("pT")
        pbf = sb("pbf", [128, TT, 256], BF16); bpbf = [Buf("pbf%d" % t) for t in range(TT)]
        biasD = sb("biasD", [128, 8, 128], BF16)
        biasO = sb("biasO", [128, 8, 128], BF16)
        mstack = sb("mstack", [17, 8, 128], BF16)
        btab = Buf("tables"); btab2 = Buf("tables2")
        selm = sb("selm", [17, 504], BF16)
        aggaug = sb("aggaug", [128, 2, 65], BF16)
        w4 = sb("w4", [128, 128], BF16)
        identb = sb("identb", [128, 128], BF16)
        onesf = sb("onesf", [128, 128], F32)
        vm = sb("vm", [128, 126], F32)
        am = sb("am", [128, 126], F32)
        cvp = sb("cvp", [128, 12], F32)
        ffw = sb("ffw", [128, 44, 4], F32)
        peT = sb("peT", [64, 64], BF16)
        b31bc = sb("b31bc", [128, 8], F32)
        small = sb("small", [128, 64], F32); bsmall = Buf("small")
        tk = sb("tk", [128, 4, 64], F32); btk = Buf("tk")
        m8 = sb("m8", [128, 16], F32)
        selpad2 = sb("selpad", [128, 2, 128], BF16); bselpad2 = [Buf("selpad0"), Buf("selpad1")]
        f4 = sb("f4", [128, 4, 8], F32); bf4 = [Buf() for _ in range(4)]
        NSCR = 7
        scr = [sb("scr%d" % i, [128, 516], F32) for i in range(NSCR)]
        bscr = [Buf("scr%d" % i) for i in range(NSCR)]
        wring = sb("wring", [128, NSLOT, SLABW], BF16)
        bslot = [Buf("slot%d" % k) for k in range(NSLOT)]
        stage = scr[5]; bstage = bscr[5]
        print("SBUF bytes/partition used:", used[0])

        yT = big
        aT_off = 8
        oT_off = 12

        bcast = [Buf("cast%d" % k) for k in range(NCAST)]
        per_cast = NSLAB // NCAST

        wstate = {"issued": 0, "next": 0}
        total_slabs = NSLAB * NT

        bwbk = [Buf("wbk%d" % i) for i in range(NSLAB)]

        def w_issue(upto):
            while wstate["issued"] < min(upto, total_slabs):
                gi = wstate["issued"]
                i = gi % NSLAB
                k = gi % NSLOT
                if gi < NSLAB:
                    S.dma("pool", "ws%d" % k, wring[:, k, :], wflat_d[:, i * SLABW:(i + 1) * SLABW], writes=[bslot[k]])
                    S.dma("sp", "wbk%d" % k, wb_d[:, i * SLABW:(i + 1) * SLABW], wring[:, k, :],
                          reads=[bslot[k]], writes=[bwbk[i]])
                else:
                    S.dma("sp", "ws%d" % k, wring[:, k, :], wb_d[:, i * SLABW:(i + 1) * SLABW],
                          reads=[bwbk[i]], writes=[bslot[k]])
                wstate["issued"] += 1

        def w_next(kind):
            gi = wstate["next"]
            assert SLABS[gi % NSLAB][0] == kind, (SLABS[gi % NSLAB], kind)
            w_issue(gi + 1 + PREF)
            wstate["next"] += 1
            k = gi % NSLOT
            return wring[:, k, :], bslot[k]

        def mm(out, lhsT, rhs, start, stop, reads, writes, sgc=False, inc=True):
            S.op("pe", lambda e: e.matmul(out, lhsT=lhsT, rhs=rhs, start=start, stop=stop, skip_group_check=sgc),
                 reads=reads, writes=writes, acc=True, inc=inc)

        def act(out, in_, func, reads, writes, bias=None, scale=None, accum_out=None):
            kw = {}
            if bias is not None:
                kw["bias"] = bias
            if scale is not None:
                kw["scale"] = scale
            if accum_out is not None:
                kw["accum_out"] = accum_out
            S.op("act", lambda e: e.activation(out=out, in_=in_, func=func, **kw), reads=reads, writes=writes)

        def tt_(eng, out, in0, in1, op, reads, writes):
            S.op(eng, lambda e: e.tensor_tensor(out=out, in0=in0, in1=in1, op=op), reads=reads, writes=writes)

        def ts_(eng, out, in0, s1, s2, op0, op1, reads, writes):
            if op1 is None:
                S.op(eng, lambda e: e.tensor_scalar(out=out, in0=in0, scalar1=s1, scalar2=None, op0=op0),
                     reads=reads, writes=writes)
            else:
                S.op(eng, lambda e: e.tensor_scalar(out=out, in0=in0, scalar1=s1, scalar2=s2, op0=op0, op1=op1),
                     reads=reads, writes=writes)

        def stt_(eng, out, in0, scalar, in1, op0, op1, reads, writes):
            S.op(eng, lambda e: e.scalar_tensor_tensor(out=out, in0=in0, scalar=scalar, in1=in1, op0=op0, op1=op1),
                 reads=reads, writes=writes)

        def cp_(eng, out, in_, reads, writes):
            S.op(eng, lambda e: e.tensor_copy(out=out, in_=in_), reads=reads, writes=writes)

        def bank3(b, a, n):
            return banks[b][:, 0:a * n].rearrange("p (a n) -> p a n", a=a)

        def bank_bf(b):
            return banks[b][:, :].bitcast(BF16)

        bsetup = Buf("setup")

        def ld(out, in_, writes=(bsetup,), sem="ld", **kw):
            S.dma("pool", sem, out, in_, writes=list(writes), **kw)

        for tt in range(TT):
            S.dma("sp", "xl%d" % tt, xres[:, tt, :], x_d[tt * 128:(tt + 1) * 128, :], writes=[bx[tt]])
        S.dma("sp", "gl", gbc[:], gains_d[0:1, :].partition_broadcast(128), writes=[bgbc])
        cvw = tk[:].rearrange("p a b -> p (a b)")
        ld(cvw[:, 0:124], cvw_d, writes=[btk], sem="ld1")
        rbx = scr[1]; ohx = scr[2]
        ld(rbx[0:32, 0:8], relb_d, writes=[bscr[1]], sem="ld2")
        ld(rbx[0:32, 8:16], relb_d[31:32, :].partition_broadcast(32), writes=[bscr[1]], sem="ld3")
        ld(ohx[0:33, 0:512], oh_d, writes=[bscr[2]], sem="ld4")
        S.dma("pool", "ld5", stage[64:65, 0:8], relb_d[31:32, :], writes=[bstage])
        bident = Buf("ident")
        ld(identb[:], ident_d, writes=[bident], sem="ld7")
        ld(cvp[:], cvp_d)
        ld(vm[:], vm_d)
        ld(am[:], am_d)
        ld(ffw[:].rearrange("p a b -> p (a b)"), ffw_d)
        ld(b31bc[:], relb_d[31:32, :].partition_broadcast(128))
        ld(selm[:], selm_d)
        ld(aggaug[:].rearrange("p a b -> p (a b)"), aggaug_d)
        ld(w4[:], w4_d)
        ld(peT[:], pet_d)
        bsetup.w = ("ld", S.count["ld"])
        w_issue(1 + PREF)
        befull = [Buf("efull0"), Buf("efull1")]
        for g in range(2):
            ld(KsT[g][64:128, :], efull_d, writes=[befull[g]] + bKs[g], sem="ld%d" % (12 + g))

        def cast_chunk(k):
            c0, c1 = k * per_cast * SLABW, (k + 1) * per_cast * SLABW
            S.dma("pool", "cast%d" % k, wb_d[:, c0:c1], wflat_d[:, c0:c1], writes=[bcast[k]])
        bones = Buf("ones")
        S.op("dve", lambda e: e.memset(onesf[:], 1.0 / 512.0), writes=[bones])
        for g in range(2):
            S.op("dve", lambda e: e.memset(kcmpT[g][0:64, :], 0.0), writes=[bkcmp[g]])
            S.op("dve", lambda e: e.memset(kcmpT[g][64:65, :], 1.0), writes=[bkcmp[g]])
            S.op("dve", lambda e: e.memset(KwT[g][64:65, :], 1.0), writes=[bKw[g][0], bKw[g][1]])
            for kv in range(2):
                S.op("dve", lambda e: e.memset(kcT[kv][g][:, 0:16], 0.0), writes=[bkcT[kv][g]])
        S.op("dve", lambda e: e.memset(vcmp[:].rearrange("p a b c -> p (a b c)"), 0.0), writes=[bvcmp])
        S.op("dve", lambda e: e.memset(Vs[:, :, :, 64:65].rearrange("p a b c -> p (a b c)"), 1.0), writes=bVs)
        S.op("dve", lambda e: e.memset(Vw[:, :, :, 64:65].rearrange("p a b c -> p (a b c)"), 1.0), writes=bVw)
        S.op("dve", lambda e: e.memset(vst[:, :, 64:65].rearrange("p a b -> p (a b)"), 1.0), writes=[bvst])
        S.op("dve", lambda e: e.memset(abuf[:].rearrange("p a b -> p (a b)"), 0.0), writes=babuf)
        S.op("dve", lambda e: e.memset(uhalo[:].rearrange("p a b -> p (a b)"), 0.0), writes=buhalo)
        S.op("dve", lambda e: e.memset(selpad2[:].rearrange("p a b -> p (a b)"), 0.0), writes=bselpad2)
        allQ = [b for g in range(2) for b in bQhi[g]]
        cp_("dve", Q[64:65, :, :], stage[64:65, 0:8].unsqueeze(2).to_broadcast([1, 8, TS]), [bstage], allQ)
        tt_("dve", rbx[0:32, 0:8], rbx[0:32, 0:8], rbx[0:32, 8:16], ALU.subtract, [bscr[1]], [bscr[1]])
        S.op("dve", lambda e: e.memset(rbx[32:33, 0:8], NEGM), reads=[bscr[1]], writes=[bscr[1]])
        mm(banks[0][0:8, :], rbx[0:33, 0:8], ohx[0:33, 0:512], True, True, [bscr[1], bscr[2]], [bbank[0]])
        tvs = scr[3]
        act(tvs[0:8, 0:512], banks[0][0:8, :], AF.Copy, [bbank[0]], [bscr[3]])
        btv = Buf("tvd")
        S.dma("pool", "tvs", tv_d.rearrange("h (r n) -> h r n", r=128),
              tvs[0:8, 0:512].unsqueeze(1).to_broadcast([8, 128, 512]), reads=[bscr[3]], writes=[btv])

        def load_bias_tables():
            tsem = iter(("ld6", "ld9", "ld10", "ld11"))

            def toeplitz3(dst_ap, rows, off, pstep, h0, nh, wbuf):
                src = bass.AP(tensor=tv_d.tensor, offset=h0 * 128 * 512 + off, ap=[[pstep, rows], [128 * 512, nh], [1, 128]])
                S.dma("sp", next(tsem), dst_ap, src, reads=[btv], writes=list(wbuf))
            ystg = ysb[:].rearrange("p a b -> p (a b)")
            toeplitz3(ystg[:, 0:1024].rearrange("p (h i) -> p h i", h=8), 128, 256, 511, 0, 8, [bysb[0], bysb[1]])
            cp_("dve", biasD[:].rearrange("p h i -> p (h i)"), ystg[:, 0:1024], [bysb[0], bysb[1]], [btab])
            toeplitz3(ystg[:, 1024:2048].rearrange("p (h i) -> p h i", h=8), 128, 256 + 128, 511, 0, 8, [bysb[2], bysb[3]])
            cp_("dve", biasO[:].rearrange("p h i -> p (h i)"), ystg[:, 1024:2048], [bysb[2], bysb[3]], [btab])
            for hh in range(2):
                stg = scr[0] if hh == 0 else scr[4]
                bst = bscr[0] if hh == 0 else bscr[4]
                toeplitz3(stg[0:16, 0:512].rearrange("p (h i) -> p h i", h=4), 16, 256 + 113, 496, hh * 4, 4, [bst])
                cp_("dve", mstack[0:16, hh * 4:hh * 4 + 4, :].rearrange("p h i -> p (h i)"), stg[0:16, 0:512], [bst], [btab])
        S.op("dve", lambda e: e.memset(stage[0:32, 0:128], NEGM), reads=[bstage], writes=[bstage])
        for h in range(8):
            S.dma("pool", "ld8", mstack[16:17, h, :], stage[0:1, 0:128], reads=[bstage], writes=[btab2])
        btab2.w = ("ld8", S.count["ld8"])

        def gen_dg(c):
            for j in range(31):
                last = j == 30
                S.op("dve", lambda e: e.tensor_scalar(out=dg[:, c, j, :], in0=identb[:],
                                                      scalar1=cvw[:, c * 31 + j:c * 31 + j + 1], scalar2=None,
                                                      op0=ALU.mult),
                     reads=[btk, bsetup], writes=[bdgc[c]] if last else [], inc=last)

        junk = scr[6][:, :].bitcast(BF16)

        xal4 = big[:].rearrange("p a b -> p (a b)").bitcast(F32)

        def src_res(tt):
            return xres[:, tt, :], [bx[tt]]

        def src_alias(tt):
            return xal4[:, tt * D:(tt + 1) * D], bbig[4 * tt:4 * tt + 4]

        def norm_stats(src, c0=0):
            for tt in range(TT):
                ap, bufs = src(tt)
                act(junk[:, 0:D], ap, AF.Square, bufs, [bscr[6], bsmall], accum_out=small[:, c0 + tt:c0 + tt + 1])
            act(small[:, c0 + 4:c0 + 8], small[:, c0:c0 + 4], AF.Sqrt, [bsmall], [bsmall], bias=EPS, scale=1.0 / D)
            S.op("dve", lambda e: e.reciprocal(out=small[:, c0 + 8:c0 + 12], in_=small[:, c0 + 4:c0 + 8]),
                 reads=[bsmall], writes=[bsmall])

        def rmsnorm_stats_all():
            norm_stats(src_res, 0)

        def load_gain(gi):
            S.dma("sp", "gl", gbc[:], gains_d[gi:gi + 1, :].partition_broadcast(128), writes=[bgbc])

        def norm_apply(src, c0=0):
            for tt in range(TT):
                ap, bufs = src(tt)
                stt_("dve", hn[:, tt % 2, :], ap, small[:, c0 + 8 + tt:c0 + 9 + tt], gbc[:], ALU.mult, ALU.mult,
                     bufs + [bsmall, bgbc], [bhn[tt % 2]])
                pb = tt % 2
                pbv = bank_bf(pb)
                for kc in range(8):
                    S.op("pe", lambda e: e.transpose(out=pbv[:, kc * 128:(kc + 1) * 128],
                                                     in_=hn[:, tt % 2, kc * 128:(kc + 1) * 128],
                                                     identity=identb[:]),
                         reads=[bhn[tt % 2], bident], writes=[bbank[pb]], acc=True, inc=(kc == 7))
                dst = hT[:, :, tt * 128:(tt + 1) * 128]
                srcv = pbv[:, 0:1024].rearrange("p (a b) -> p a b", a=8)
                if tt % 2 == 0:
                    act(dst, srcv, AF.Copy, [bbank[pb]], [bhT])
                else:
                    cp_("dve", dst, srcv, [bbank[pb]], [bhT])

        def rmsnorm_to_hT(next_gain):
            norm_stats(src_res, 0)
            norm_apply(src_res, 0)
            if next_gain is not None:
                load_gain(next_gain)

        def fm_gemm(bank, slab, KC, ncols, coff, rhs_fn, rbufs, bsl, M=128, poff=0):
            for kc in range(KC):
                mm(banks[bank][poff:poff + M, :], slab[:, kc * ncols + coff:kc * ncols + coff + M], rhs_fn(kc),
                   kc == 0, kc == KC - 1, [bsl] + rbufs, [bbank[bank]], inc=(kc == KC - 1))

        def final_norm(Tp):
            tp0 = Tp * TS
            rmsnorm_stats_all()
            for tt in range(TT):
                for half in range(2):
                    k = tt * 2 + half
                    if k < 4:
                        ob, bob = ysb[:, k, :], bysb[k]
                    else:
                        ob, bob = scr[k - 4][:, 0:TS], bscr[k - 4]
                    hsl = slice(half * 512, (half + 1) * 512)
                    stt_("dve", ob, xres[:, tt, hsl], small[:, 8 + tt:9 + tt], gbc[:, hsl], ALU.mult, ALU.mult,
                         [bx[tt], bsmall, bgbc], [bob])
                    S.dma("pool", "st%d" % k, y_d[tp0 + tt * 128:tp0 + (tt + 1) * 128, hsl], ob, reads=[bob])
                if Tp + 1 < NT:
                    S.dma("sp", "xl%d" % tt, xres[:, tt, :], x_d[tp0 + TS + tt * 128:tp0 + TS + (tt + 1) * 128, :],
                          writes=[bx[tt]])
            if Tp + 1 < NT:
                load_gain(1)

        for T in range(NT):
            t0 = T * TS
            MARKS.append((T, "1norm", S.nissued.get("pe", 0)))
            if T == 0:
                rmsnorm_to_hT(1)
            for tt in range(TT):
                S.dma("pool", "pl%d" % tt, pbf[:, tt, :], p_d[t0 + tt * 128:t0 + (tt + 1) * 128, :], writes=[bpbf[tt]])
            MARKS.append((T, "3conv", S.nissued.get("pe", 0)))
            def u_gemm(j):
                bs = (j % 2) * 2
                slA, bA = w_next("u")
                fm_gemm(bs, slA, 8, 128, 0, lambda kc: hT[:, kc, :], [bhT], bA)
                slB, bB = w_next("u")
                fm_gemm(bs + 1, slB, 8, 128, 0, lambda kc: hT[:, kc, :], [bhT], bB)
                act(scr[j % 2][:, 0:TS], banks[bs + 1][:, :], AF.Sigmoid, [bbank[bs + 1]], [bscr[j % 2]])
                tt_("dve", abuf[:, j, 30:30 + TS], banks[bs][:, :], scr[j % 2][:, 0:TS], ALU.mult,
                    [bbank[bs], bscr[j % 2]], [babuf[j]])

            def conv_mm(j):
                bk = 4 + (j % 2)
                if T == 0:
                    gen_dg(j)
                for jj in range(31):
                    mm(banks[bk][:, :], dg[:, j, jj, :], abuf[:, j, jj:jj + TS], jj == 0, jj == 30,
                       [bdgc[j], babuf[j]], [bbank[bk]], inc=(jj == 30))
                act(ysb[:, j, :], banks[bk][:, :], AF.Identity, [bbank[bk], bsetup], [bysb[j]],
                    bias=cvp[:, j * 3:j * 3 + 1])
                act(scr[2 + j][:, 0:TS], ysb[:, j, :], AF.Square, [bysb[j]], [bscr[2 + j]])
                cp_("pool", abuf[:, j, 0:30], abuf[:, j, TS:TS + 30], [babuf[j]], [babuf[j]])

            u_gemm(0)
            for j in range(4):
                if j + 1 < 4:
                    u_gemm(j + 1)
                conv_mm(j)
            for j in range(4):
                mm(banks[6][:, :], onesf[:], ysb[:, j, :], j == 0, j == 3, [bones, bysb[j]], [bbank[6]], inc=(j == 3))
            for j in range(4):
                mm(banks[7][:, :], onesf[:], scr[2 + j][:, 0:TS], j == 0, j == 3,
                   [bones, bscr[2 + j]], [bbank[7]], inc=(j == 3))
            mean = scr[0]; rstd_ = scr[1]
            cp_("dve", mean[:, 0:TS], banks[6][:, :], [bbank[6]], [bscr[0]])
            tt_("dve", scr[6][:, 0:TS], mean[:, 0:TS], mean[:, 0:TS], ALU.mult, [bscr[0]], [bscr[6]])
            tt_("dve", scr[6][:, 0:TS], banks[7][:, :], scr[6][:, 0:TS], ALU.subtract, [bbank[7], bscr[6]], [bscr[6]])
            MARKS.append((T, "4qkv", S.nissued.get("pe", 0)))
            rot = [0]

            def nbank():
                rot[0] = (rot[0] + 1) % 4
                return rot[0]
            for hp in range(4):
                sl, bsl = w_next("q")
                for half in range(2):
                    h = hp * 2 + half
                    g = h // 4
                    bk = nbank()
                    fm_gemm(bk, sl, 8, 128, half * 64, lambda kc: hT[:, kc, :], [bhT], bsl, M=64)
                    act(Q[0:64, h, :], banks[bk][0:64, :], AF.Copy, [bbank[bk]], bQlo[g], scale=0.125)
            act(scr[6][:, 0:TS], scr[6][:, 0:TS], AF.Sqrt, [bscr[6]], [bscr[6]], bias=EPS, scale=1.0)
            S.op("dve", lambda e: e.reciprocal(out=rstd_[:, 0:TS], in_=scr[6][:, 0:TS]), reads=[bscr[6]], writes=[bscr[1]])
            for j in range(4):
                tmp = scr[2 + j]; btmp = bscr[2 + j]
                tt_("dve", tmp[:, 0:TS], ysb[:, j, :], mean[:, 0:TS], ALU.subtract, [bysb[j], bscr[0]], [btmp])
                tt_("dve", tmp[:, 0:TS], tmp[:, 0:TS], rstd_[:, 0:TS], ALU.mult, [btmp, bscr[1]], [btmp])
            for idx in (0, 1, 2, 4):
                sl, bsl = w_next("kv")
                for g in range(2):
                    bk = nbank()
                    fm_gemm(bk, sl, 8, 128, g * 64, lambda kc: hT[:, kc, :], [bhT], bsl, M=64)
                    if idx in (0, 1):
                        act(kcT[idx][g][:, 16:16 + TS], banks[bk][0:64, :], AF.Copy, [bbank[bk]], [bkcT[idx][g]])
                    elif idx == 2:
                        act(KsT[g][0:64, t0:t0 + TS], banks[bk][0:64, :], AF.Copy, [bbank[bk]], [bKs[g][T]])
                    else:
                        w0 = (T % 2) * TS
                        act(KwT[g][0:64, w0:w0 + TS], banks[bk][0:64, :], AF.Copy, [bbank[bk]], [bKw[g][T % 2]])
            for j in range(4):
                tmp = scr[2 + j]; btmp = bscr[2 + j]
                act(big[:, aT_off + j, :], tmp[:, 0:TS], AF.Silu, [btmp, bsetup], [bbig[aT_off + j]],
                    bias=cvp[:, j * 3 + 2:j * 3 + 3], scale=cvp[:, j * 3 + 1:j * 3 + 2])
            for idx in (3, 5):
                sl, bsl = w_next("vtok")
                vbk = 4 if idx == 3 else 5
                pv = bank3(vbk, 4, 128)
                for tt in range(TT):
                    for kc in range(8):
                        mm(pv[:, tt, :], hT[:, kc, tt * 128:(tt + 1) * 128], sl[:, kc * 128:(kc + 1) * 128],
                           kc == 0, kc == 7, [bhT, bsl], [bbank[vbk]], inc=(kc == 7))
                if idx == 3:
                    dst = Vs[:, T * 4:T * 4 + 4, :, 0:64]
                    wb_ = [bVs[T]]
                else:
                    dst = Vw[:, (T % 2) * 4:(T % 2) * 4 + 4, :, 0:64]
                    wb_ = [bVw[T % 2]]
                act(dst, banks[vbk][:, :].rearrange("p (a g d) -> p a g d", a=4, g=2), AF.Copy, [bbank[vbk]], wb_)
            sl, bsl = w_next("ng")
            pg_ = bank3(6, 4, 24)
            for tt in range(TT):
                for kc in range(8):
                    mm(pg_[:, tt, :], hT[:, kc, tt * 128:(tt + 1) * 128], sl[:, kc * 24:(kc + 1) * 24],
                       kc == 0, kc == 7, [bhT, bsl], [bbank[6]], inc=(kc == 7))
            act(gates[:, :, :], pg_, AF.Sigmoid, [bbank[6]], bgates)
            MARKS.append((T, "5cmp", S.nissued.get("pe", 0)))
            c_lo = max(0, 32 * T - 1)
            c_hi = 32 * T + 30
            n_c = c_hi - c_lo + 1
            col0 = 16 * c_lo - t0 + 16
            for kv in range(2):
                bk = 4 + kv
                for s in range(8):
                    sl, bsl = w_next("w1")
                    for g in range(2):
                        for hc in range(2):
                            r0 = (g * 2 + hc) * 32
                            for pp in range(4):
                                pos = s * 4 + pp
                                rhs = kcT[kv][g][:, col0 + pos:col0 + pos + 16 * (n_c - 1) + 1:16]
                                mm(banks[bk][:, r0:r0 + n_c], sl[0:64, pp * 256 + hc * 128:pp * 256 + hc * 128 + 128],
                                   rhs, pos == 0 and g == 0 and hc == 0, pos == 31, [bsl, bkcT[kv][g]], [bbank[bk]],
                                   sgc=True)
                    if T == 0:
                        for hc in range(2):
                            for pp in range(4):
                                pos = s * 4 + pp
                                mm(banks[bk][:, 256 + hc:256 + hc + 1],
                                   sl[0:64, pp * 256 + hc * 128:pp * 256 + hc * 128 + 128],
                                   peT[:, kv * 32 + pos:kv * 32 + pos + 1], False, pos == 31,
                                   [bsl, bsetup], [bbank[bk]], sgc=True)
                if T == 0:
                    act(cconst[:, kv, :], banks[bk][:, 256:258], AF.Copy, [bbank[bk]], [bcconst])
                for g in range(2):
                    for hc in range(2):
                        r0 = (g * 2 + hc) * 32
                        act(hid[:, kv, g, hc, 0:n_c], banks[bk][:, r0:r0 + n_c], AF.Gelu_apprx_tanh,
                            [bbank[bk], bcconst], [bhid[kv]], bias=cconst[:, kv, hc:hc + 1])
                for g in range(2):
                    cp_("pool", kcT[kv][g][:, 0:16], kcT[kv][g][:, TS:TS + 16], [bkcT[kv][g]], [bkcT[kv][g]])
            sl, bsl = w_next("w2")
            for g in range(2):
                for hc in range(2):
                    mm(banks[6][0:64, g * 32:g * 32 + n_c], sl[:, hc * 64:(hc + 1) * 64], hid[:, 0, g, hc, 0:n_c],
                       hc == 0, hc == 1, [bsl, bhid[0]], [bbank[6]])
                act(kcmpT[g][0:64, c_lo:c_lo + n_c], banks[6][0:64, g * 32:g * 32 + n_c], AF.Copy,
                    [bbank[6]], [bkcmp[g]])
            for g in range(2):
                for hc in range(2):
                    mm(banks[7][0:n_c, g * 64:(g + 1) * 64], hid[:, 1, g, hc, 0:n_c],
                       sl[:, (2 + hc) * 64:(3 + hc) * 64], hc == 0, hc == 1, [bsl, bhid[1]], [bbank[7]])
            act(vst[0:n_c, :, 0:64], banks[7][0:n_c, 0:128].rearrange("p (g d) -> p g d", g=2), AF.Copy,
                [bbank[7]], [bvst])
            cs = c_lo
            while cs <= c_hi:
                ce = min(c_hi, (cs // 128) * 128 + 127)
                n = ce - cs + 1
                S.dma("pool", "vc", vcmp[cs % 128:cs % 128 + n, cs // 128, :, :], vst[cs - c_lo:cs - c_lo + n, :, :],
                      reads=[bvst], writes=[bvcmp])
                cs = ce + 1
            if T == 0:
                load_bias_tables()
            MARKS.append((T, "6attn", S.nissued.get("pe", 0)))
            nch = 1 if c_hi < 128 else 2
            O_c, O_s, O_w = bank3(2, 4, 65), bank3(3, 4, 65), bank3(4, 4, 65)
            IMP = bank3(5, 4, 65)

            def norm_gate(b, slot, Ob, bOb, tt, g):
                dn = f4[:, slot, 0:4]; fr = f4[:, slot, 4:8]
                ts_("dve", dn.unsqueeze(2), Ob[:, :, 64:65], 1e-30, None, ALU.add, None, [bOb], [bf4[slot]])
                S.op("dve", lambda e: e.reciprocal(out=fr, in_=dn), reads=[bf4[slot]], writes=[bf4[slot]])
                gsl = gates[:, tt, g * 12 + b:g * 12 + b + 10:3]
                tt_("dve", fr, fr, gsl, ALU.mult, [bf4[slot], bgates[tt]], [bf4[slot]])
                tt_("dve", osc[:, slot, :].rearrange("p (h d) -> p h d", h=4), Ob[:, :, 0:64],
                    fr.unsqueeze(2).to_broadcast([128, 4, 64]), ALU.mult, [bOb, bf4[slot]], [bosc[slot]])

            def stageA(tt, g, par):
                qt = T * 4 + tt
                qs = slice(tt * 128, (tt + 1) * 128)
                hs = slice(g * 4, g * 4 + 4)
                for ch in range(nch):
                    psS = bank3(6, 4, 128)
                    mm(psS, kcmpT[g][0:64, ch * 128:(ch + 1) * 128], Q[0:64, hs, qs], True, False,
                       [bkcmp[g], bQlo[g][tt]], [bbank[6]], inc=False)
                    base = 128 * ch - 8 * qt + 248
                    mm(psS, selm[0:17, base:base + 128], mstack[0:17, hs, :], False, True,
                       [bsetup, btab, btab2], [bbank[6]])
                    act(Pc[:, ch, :], banks[6][:, :], AF.Exp, [bbank[6]], [bPc[ch]])
                    yield
                    for h in range(4):
                        mm(O_c[:, h, :], Pc[:, ch, h * 128:(h + 1) * 128], vcmp[:, ch, g, :],
                           ch == 0 and h == 0, ch == nch - 1, [bPc[ch], bvcmp], [bbank[2]], sgc=True, inc=(h == 3))
                    for h in range(4):
                        mm(IMP[:, h, :], Pc[:, ch, h * 128:(h + 1) * 128], aggaug[:, ch, :],
                           ch == 0 and h == 0, ch == nch - 1, [bPc[ch], bsetup], [bbank[5]], sgc=True, inc=(h == 3))
                    yield
                den4 = tk[:, 0, 0:4]; rec4 = tk[:, 0, 8:12]
                ts_("dve", den4.unsqueeze(2), IMP[:, :, 64:65], 1e-30, None, ALU.add, None, [bbank[5]], [btk])
                S.op("dve", lambda e: e.reciprocal(out=rec4, in_=den4), reads=[btk], writes=[btk])
                impa = tk[:, 1, :]
                ts_("dve", impa, IMP[:, 0, 0:64], rec4[:, 0:1], None, ALU.mult, None, [bbank[5], btk], [btk])
                for h in range(1, 4):
                    stt_("dve", impa, IMP[:, h, 0:64], rec4[:, h:h + 1], impa, ALU.mult, ALU.add,
                         [bbank[5], btk], [btk])
                off = 62 - 2 * qt
                tt_("dve", impa, impa, vm[:, off:off + 64], ALU.mult, [btk, bsetup], [btk])
                tt_("dve", impa, impa, am[:, off:off + 64], ALU.add, [btk, bsetup], [btk])
                S.op("dve", lambda e: e.memset(impa[:, 0:1], 1e4), reads=[btk], writes=[btk])
                S.op("dve", lambda e: e.max(out=m8[:, 0:8], in_=impa), reads=[btk], writes=[btk])
                S.op("dve", lambda e: e.match_replace(out=tk[:, 2, :], in_to_replace=m8[:, 0:8], in_values=impa,
                                                      imm_value=-1e9), reads=[btk], writes=[btk])
                S.op("dve", lambda e: e.max(out=m8[:, 8:16], in_=tk[:, 2, :]), reads=[btk], writes=[btk])
                selpad = selpad2[:, par, :]; bselpad = bselpad2[par]
                ts_("dve", selpad[:, 64:128], impa, m8[:, 15:16], 1.0, ALU.is_ge, ALU.subtract, [btk], [bselpad])
                norm_gate(0, 0 if par == 0 else 3, O_c, bbank[2], tt, g)
                for _ in range(10):
                    yield
                yield "TAIL"
                tb = bank_bf(6)
                S.op("pe", lambda e: e.transpose(out=tb[:, 0:128], in_=selpad, identity=identb[:]),
                     reads=[bselpad, bident], writes=[bbank[6]])
                for h in range(4):
                    ts_("dve", Q[64:128, g * 4 + h, qs], tb[64:128, 0:128], -NEGM, b31bc[64:128, g * 4 + h:g * 4 + h + 1],
                        ALU.mult, ALU.add, [bbank[6], bsetup], [bQhi[g][tt]])

            def make_item(tt, g, par):
                qt = T * 4 + tt
                qs = slice(tt * 128, (tt + 1) * 128)
                hs = slice(g * 4, g * 4 + 4)
                jobs = [("w", kt) for kt in range(max(0, qt - 4), qt + 1)] + [("s", kt) for kt in range(0, qt + 1)]

                def emit_qk(ji):
                    br, kt = jobs[ji]
                    sbk = (0, 1, 7)[ji % 3]
                    psS = bank3(sbk, 4, 128)
                    extra = None
                    if kt == qt:
                        extra = biasD[:, hs, :]
                    elif kt == qt - 1:
                        extra = biasO[:, hs, :]
                    elif br == "w" and kt == qt - 4:
                        extra = w4[:, :].unsqueeze(1).to_broadcast([128, 4, 128])
                    if br == "s":
                        mm(psS, KsT[g][:, kt * 128:(kt + 1) * 128], Q[:, hs, qs], True, extra is None,
                           [bKs[g][kt // 4], befull[g], bQlo[g][tt], bQhi[g][tt]], [bbank[sbk]], inc=(extra is None))
                    else:
                        w0 = (kt % 8) * 128
                        mm(psS, KwT[g][0:64, w0:w0 + 128], Q[0:64, hs, qs], True, extra is None,
                           [bKw[g][(kt // 4) % 2], bQlo[g][tt]], [bbank[sbk]], inc=(extra is None))
                    if extra is not None:
                        mm(psS, identb[:], extra, False, True, [bsetup, btab], [bbank[sbk]])
                    pi = ji % NPS
                    act(Ps[:, pi, :], banks[sbk][:, :], AF.Exp, [bbank[sbk]], [bPs[pi]])

                def emit_pv(ji):
                    br, kt = jobs[ji]
                    pi = ji % NPS
                    if br == "s":
                        first, last = kt == 0, kt == qt
                        vv = Vs[:, kt, g, :]
                        vb = bVs[kt // 4]
                        Ob, bOb = O_s, bbank[3]
                    else:
                        first, last = kt == max(0, qt - 4), kt == qt
                        vv = Vw[:, kt % 8, g, :]
                        vb = bVw[(kt // 4) % 2]
                        Ob, bOb = O_w, bbank[4]
                    for h in range(4):
                        mm(Ob[:, h, :], Ps[:, pi, h * 128:(h + 1) * 128], vv, first and h == 0, last,
                           [bPs[pi], vb], [bOb], sgc=True, inc=(h == 3))

                def combine():
                    norm_gate(1, 1, O_s, bbank[3], tt, g)
                    norm_gate(2, 2, O_w, bbank[4], tt, g)
                    slots = [0 if par == 0 else 3, 1, 2]
                    po = bank3(5, 2, 128)
                    for hp in range(2):
                        for bi, sl_ in enumerate(slots):
                            mm(po[:, hp, :], osc[:, sl_, hp * 128:(hp + 1) * 128], identb[:], bi == 0, bi == 2,
                               [bosc[sl_], bsetup], [bbank[5]], inc=(bi == 2))
                    cp_("dve", big[:, oT_off + g * 2:oT_off + g * 2 + 2, qs], po, [bbank[5]],
                        [bbig[oT_off + g * 2], bbig[oT_off + g * 2 + 1]])
                return jobs, emit_qk, emit_pv, combine

            if T > 0:
                final_norm(T - 1)
            items = [(tt, g) for tt in range(TT) for g in range(2)]
            _DONE = object()

            class ItemState:
                def __init__(self, k, carry):
                    self.jobs, self.emit_qk, self.emit_pv, self.combine = make_item(items[k][0], items[k][1], k % 2)
                    self.n = len(self.jobs)
                    self.nq = 0
                    self.carry = carry

                def flush(self):
                    if self.carry is not None:
                        for _ in self.carry:
                            pass
                        self.carry = None

                def step_carry(self):
                    if self.carry is not None:
                        if next(self.carry, _DONE) is _DONE:
                            self.carry = None
                        return True
                    return False

                def ensure_qk(self, upto, block):
                    while self.nq < min(upto, self.n):
                        if self.jobs[self.nq][0] == "s" and self.carry is not None:
                            if not block:
                                return
                            self.flush()
                        self.emit_qk(self.nq)
                        self.nq += 1

            def upto_tail(gen):
                for v in gen:
                    if v == "TAIL":
                        break
                return gen

            cur = ItemState(0, upto_tail(stageA(items[0][0], items[0][1], 0)))
            cur.ensure_qk(2, False)
            for k in range(len(items)):
                genA = stageA(items[k + 1][0], items[k + 1][1], (k + 1) % 2) if k + 1 < len(items) else None
                for ji in range(cur.n):
                    cur.ensure_qk(ji + 1, True)
                    cur.ensure_qk(ji + 3, False)
                    cur.emit_pv(ji)
                    if genA is not None:
                        next(genA, None)
                cur.flush()
                if genA is not None:
                    nxt = ItemState(k + 1, upto_tail(genA))
                    nxt.ensure_qk(2, False)
                else:
                    nxt = None
                cur.combine()
                cur = nxt
            if DEBUG:
                for c in range(4):
                    S.dma("pool", "dbg", dbg["oT"][:, c * SEQ + t0:c * SEQ + t0 + TS], big[:, oT_off + c, :],
                          reads=[bbig[oT_off + c]])
                    S.dma("pool", "dbg", dbg["aT"][:, c * SEQ + t0:c * SEQ + t0 + TS], big[:, aT_off + c, :],
                          reads=[bbig[aT_off + c]])
            MARKS.append((T, "7merge", S.nissued.get("pe", 0)))
            for j in range(8):
                b0 = (j % 2) * 4
                s0, s1 = scr[(j % 2) * 2], scr[(j % 2) * 2 + 1]
                bs0, bs1 = bscr[(j % 2) * 2], bscr[(j % 2) * 2 + 1]
                slc, bc_ = w_next("ca")
                sla, ba_ = w_next("mgA")
                slb, bb_ = w_next("mgB")
                fm_gemm(b0, slc, 4, 128, 0, lambda kc: big[:, aT_off + kc, :], bbig[aT_off:aT_off + 4], bc_)
                for kc in range(4):
                    mm(banks[b0 + 1][:, :], slc[:, 512 + kc * 128:512 + (kc + 1) * 128], big[:, oT_off + kc, :],
                       kc == 0, kc == 3, [bc_] + bbig[oT_off:oT_off + 4], [bbank[b0 + 1]], inc=(kc == 3))
                fm_gemm(b0 + 2, sla, 8, 128, 0, lambda kc: hT[:, kc, :], [bhT], ba_)
                fm_gemm(b0 + 3, slb, 8, 128, 0, lambda kc: hT[:, kc, :], [bhT], bb_)
                act(s0[:, 0:TS], banks[b0 + 2][:, :], AF.Sigmoid, [bbank[b0 + 2]], [bs0])
                act(s1[:, 0:TS], banks[b0 + 3][:, :], AF.Sigmoid, [bbank[b0 + 3]], [bs1])
                tt_("dve", s0[:, 0:TS], banks[b0][:, :], s0[:, 0:TS], ALU.mult, [bbank[b0], bs0], [bs0])
                tt_("dve", s1[:, 0:TS], banks[b0 + 1][:, :], s1[:, 0:TS], ALU.mult, [bbank[b0 + 1], bs1], [bs1])
                tt_("dve", yT[:, j, :], s0[:, 0:TS], s1[:, 0:TS], ALU.add, [bs0, bs1], [bbig[j]])

            def tm_gemm(kind, nsl, lhs_fn, lbufs, evac):
                for half in range(2):
                    bb = (half % 2) * 4
                    for s in range(nsl):
                        sl, bsl = w_next(kind)
                        for tt in range(TT):
                            for k2 in range(2):
                                kc = 2 * s + k2
                                mm(banks[bb + tt][:, :], lhs_fn(kc, tt), sl[:, k2 * 512:(k2 + 1) * 512],
                                   kc == 0, kc == 2 * nsl - 1, [bsl] + lbufs(kc), [bbank[bb + tt]],
                                   inc=(tt == TT - 1 and k2 == 1))
                    for tt in range(TT):
                        evac(half, tt, bb + tt)

            def resid_add(half, tt, bk):
                tt_("dve", xres[:, tt, half * 512:(half + 1) * 512], xres[:, tt, half * 512:(half + 1) * 512],
                    banks[bk][:, :], ALU.add, [bx[tt], bbank[bk]], [bx[tt]])

            tm_gemm("out", 4, lambda kc, tt: yT[:, kc, tt * 128:(tt + 1) * 128], lambda kc: [bbig[kc]], resid_add)
            if DEBUG:
                for tt in range(TT):
                    S.dma("pool", "dbg", dbg["x1"][t0 + tt * 128:t0 + (tt + 1) * 128, :], xres[:, tt, :], reads=[bx[tt]])
            MARKS.append((T, "8norm", S.nissued.get("pe", 0)))
            pbv = bank_bf(2)
            for tt in range(TT):
                for kc in range(2):
                    S.op("pe", lambda e: e.transpose(out=pbv[:, (tt * 2 + kc) * 128:(tt * 2 + kc + 1) * 128],
                                                     in_=pbf[:, tt, kc * 128:(kc + 1) * 128], identity=identb[:]),
                         reads=[bpbf[tt], bsetup], writes=[bbank[2]], acc=True, inc=(tt == TT - 1 and kc == 1))
            act(pT[:].rearrange("p k (t q) -> p t k q", t=TT), pbv[:, 0:1024].rearrange("p (t k q) -> p t k q", t=TT, k=2),
                AF.Copy, [bbank[2]], [bpT])
            rmsnorm_to_hT(2)
            MARKS.append((T, "9ffn", S.nissued.get("pe", 0)))
            def ffn_up(i):
                par = i % 2
                for which, kind in enumerate(("upg", "upv")):
                    sl, bsl = w_next(kind)
                    bk = par * 4 + which
                    fm_gemm(bk, sl, 8, 128, 0, lambda kc: hT[:, kc, :], [bhT], bsl)
                    ci = which * NFF + i
                    ub = scr[par * 2 + which][:, :].bitcast(BF16); bub = bscr[par * 2 + which]
                    if which == 0:
                        for k3 in range(3):
                            ts_("dve", dgf[:, par, k3, :], identb[:], ffw[:, ci, k3:k3 + 1], None, ALU.mult, None,
                                [bsetup, bident], [bdgf[par]])
                    cp_("pool", ub[:, 0:2], uhalo[:, ci, :], [buhalo[ci]], [bub])
                    act(ub[:, 2:2 + TS], banks[bk][:, :], AF.Copy, [bbank[bk]], [bub])
                    cp_("pool", uhalo[:, ci, :], ub[:, TS:TS + 2], [bub], [buhalo[ci]])

            def ffn_conv(i):
                par = i % 2
                ubg = scr[par * 2][:, :].bitcast(BF16); bubg = bscr[par * 2]
                ubv = scr[par * 2 + 1][:, :].bitcast(BF16); bubv = bscr[par * 2 + 1]
                cb = par * 4 + 2
                for k3 in range(3):
                    mm(banks[cb][:, :], dgf[:, par, k3, :], ubg[:, k3:k3 + TS], k3 == 0, k3 == 2,
                       [bdgf[par], bubg], [bbank[cb]], inc=(k3 == 2))
                gl = scr[4 + par]; bgl = bscr[4 + par]
                act(gl[:, 0:TS], banks[cb][:, :], AF.Gelu_apprx_tanh, [bbank[cb], bsetup], [bgl],
                    bias=ffw[:, i, 3:4])
                cv = scr[6]; bcv = bscr[6]
                civ = NFF + i
                ts_("dve", cv[:, 0:TS], ubv[:, 2:2 + TS], ffw[:, civ, 2:3], ffw[:, civ, 3:4], ALU.mult, ALU.add,
                    [bubv, bsetup], [bcv])
                stt_("dve", cv[:, 0:TS], ubv[:, 1:1 + TS], ffw[:, civ, 1:2], cv[:, 0:TS], ALU.mult, ALU.add,
                     [bubv, bcv, bsetup], [bcv])
                stt_("dve", cv[:, 0:TS], ubv[:, 0:TS], ffw[:, civ, 0:1], cv[:, 0:TS], ALU.mult, ALU.add,
                     [bubv, bcv, bsetup], [bcv])
                tt_("dve", big[:, i, :], cv[:, 0:TS], gl[:, 0:TS], ALU.mult, [bcv, bgl], [bbig[i]])

            ffn_up(0)
            for i in range(NFF):
                if i + 1 < NFF:
                    ffn_up(i + 1)
                ffn_conv(i)
            MARKS.append((T, "10down", S.nissued.get("pe", 0)))
            tm_gemm("down", NFF // 2, lambda kc, tt: big[:, kc, tt * 128:(tt + 1) * 128], lambda kc: [bbig[kc]],
                    resid_add)
            if T + 1 < NT:
                for tt in range(TT):
                    ap, bufs = src_alias(tt)
                    S.dma("sp", "xa%d" % tt, ap, x_d[t0 + TS + tt * 128:t0 + TS + (tt + 1) * 128, :], writes=bufs)
            MARKS.append((T, "11ple", S.nissued.get("pe", 0)))
            rmsnorm_to_hT(0 if T + 1 < NT else 3)
            if T + 1 < NT:
                norm_stats(src_alias, 16)
            for half in range(2):
                hsl = slice(half * 512, (half + 1) * 512)
                sl, bsl = w_next("ple")
                for tt in range(TT):
                    for k2 in range(2):
                        mm(banks[tt][:, :], pT[:, k2, tt * 128:(tt + 1) * 128], sl[:, k2 * 512:(k2 + 1) * 512],
                           k2 == 0, k2 == 1, [bsl, bpT], [bbank[tt]], inc=(k2 == 1))
                for s in range(4):
                    sl, bsl = w_next("pgate")
                    for tt in range(TT):
                        for k2 in range(2):
                            kc = 2 * s + k2
                            mm(banks[4 + tt][:, :], hT[:, kc, tt * 128:(tt + 1) * 128], sl[:, k2 * 512:(k2 + 1) * 512],
                               kc == 0, kc == 7, [bsl, bhT], [bbank[4 + tt]], inc=(tt == TT - 1 and k2 == 1))
                for tt in range(TT):
                    act(ysb[:, tt, :], banks[tt][:, :], AF.Copy, [bbank[tt]], [bysb[tt]])
                for tt in range(TT):
                    sc_, bsc_ = scr[tt % 2], bscr[tt % 2]
                    act(sc_[:, 0:TS], banks[4 + tt][:, :], AF.Sigmoid, [bbank[4 + tt]], [bsc_])
                    tt_("dve", sc_[:, 0:TS], sc_[:, 0:TS], ysb[:, tt, :], ALU.mult, [bsc_, bysb[tt]], [bsc_])
                    tt_("dve", xres[:, tt, hsl], xres[:, tt, hsl], sc_[:, 0:TS], ALU.add, [bx[tt], bsc_], [bx[tt]])
            MARKS.append((T, "12final", S.nissued.get("pe", 0)))
            if T + 1 < NT:
                norm_apply(src_alias, 16)
                load_gain(3)
            if T + 1 == NT:
                final_norm(T)
        fin = list(bysb) + bscr[0:4]
        S.wait_all("sp", fin)
        S.wait_all("pool", fin)
        if DEBUG:
            dd = Buf()
            dd.w = ("dbg", S.count["dbg"])
            S.wait_all("pool", [dd])
    return nc


def kernel(x, p, rel_bias, norm_mix, w_in, conv_dw_w, conv_dw_b, conv_ln_g, conv_ln_b,
           w_conv_out, cmp_pe_k, cmp_pe_v, w_ck1, w_ck2, w_cv1, w_cv2, w_attn_out, w_out,
           norm_ffn, w_up, ffn_dw_w, ffn_dw_b, w_down, norm_ple, w_ple_gate, w_ple, norm_final):
    f = lambda a: np.ascontiguousarray(np.asarray(a, dtype=np.float32))
    x = f(x); p = f(p)
    wflat = build_wflat(f(w_in)[0], f(w_conv_out)[0], f(w_ck1)[0], f(w_ck2)[0], f(w_cv1)[0], f(w_cv2)[0],
                        f(w_attn_out)[0], f(w_out)[0], f(w_up)[0], f(w_down)[0], f(w_ple_gate)[0], f(w_ple)[0])
    consts = build_consts()
    gains = np.stack([f(norm_mix)[0], f(norm_ffn)[0], f(norm_ple)[0], f(norm_final)], axis=0)
    cvw = f(conv_dw_w)[0].T.reshape(4, 128, 31).transpose(1, 0, 2).reshape(128, 124)
    cvp = np.stack([f(conv_dw_b)[0], f(conv_ln_g)[0], f(conv_ln_b)[0]], axis=1).reshape(4, 128, 3)
    cvp = cvp.transpose(1, 0, 2).reshape(128, 12)
    ffw = np.concatenate([f(ffn_dw_w)[0], f(ffn_dw_b)[0][None, :]], axis=0).T.reshape(44, 128, 4)
    ffw = ffw.transpose(1, 0, 2).reshape(128, 176)
    pet = np.concatenate([f(cmp_pe_k)[0].T, f(cmp_pe_v)[0].T], axis=1)
    shared = {
        "wflat": wflat, "gains": f(gains), "relb": f(rel_bias), "cvw": f(cvw), "cvp": f(cvp), "ffw": f(ffw),
        "pet": f(pet),
    }
    for k, v in consts.items():
        shared[k] = f(v)
    nc = build_program()
    in_maps = []
    for b in range(8):
        m = dict(shared)
        m["x"] = x[b]
        m["p"] = p[0, b]
        in_maps.append(m)
    res = run_bass_kernel_spmd(nc, in_maps, core_ids=list(range(8)))
    kernel.last_results = res
    return np.stack([np.asarray(r["y"], dtype=np.float32) for r in res.results], axis=0)
```

```python
import math
import numpy as np
import concourse.bass as bass
import concourse.mybir as mybir
from concourse.bass_utils import run_bass_kernel_spmd
from contextlib import ExitStack

F32 = mybir.dt.float32
BF16 = mybir.dt.bfloat16
AF = mybir.ActivationFunctionType
ALU = mybir.AluOpType

D = 1024
SEQ = 4096
TS = 512
NT = SEQ // TS
TT = TS // 128
D_FF = 2816
NFF = D_FF // 128
NEGM = -240.0
EPS = 1e-6
SLABW = 1024
NSLOT = 12
PREF = 7
NCAST = 24
DEBUG = False
MARKS = []


class Buf:
    __slots__ = ("name", "w", "r")

    def __init__(self, name=""):
        self.name = name
        self.w = None
        self.r = {}


class Sync:
    def __init__(self, nc, es):
        self.nc = nc
        self.es = es
        self.sems = {}
        self.count = {}
        self.waited = {}
        self.nissued = {}
        self.engs = {"pe": nc.tensor, "act": nc.scalar, "dve": nc.vector,
                     "pool": nc.gpsimd, "sp": nc.sync}
        for k in self.engs:
            self.new_sem(k)
            self.waited[k] = {}

    def new_sem(self, key):
        h = self.es.enter_context(self.nc.semaphore("s_" + key))
        self.sems[key] = h
        self.count[key] = 0
        return key

    def _wait(self, ename, deps):
        e = self.engs[ename]
        wd = self.waited[ename]
        for key, val in deps.items():
            assert val <= self.count[key], ("wait on unissued increment", ename, key, val, self.count[key])
            if wd.get(key, 0) < val:
                e.wait_ge(self.sems[key], val)
                wd[key] = val

    @staticmethod
    def _deps(reads, writes, selfkey=None, acc=False):
        deps = {}

        def add(k, v):
            if deps.get(k, 0) < v:
                deps[k] = v
        for b in reads:
            if b.w is not None:
                add(*b.w)
        for b in writes:
            if b.w is not None and not (acc and b.w[0] == selfkey):
                add(*b.w)
            for k, v in b.r.items():
                if acc and k == selfkey:
                    continue
                add(k, v)
        return deps

    def op(self, ename, fn, reads=(), writes=(), acc=False, inc=True):
        deps = self._deps(reads, writes, ename, acc)
        self._wait(ename, deps)
        ins = fn(self.engs[ename])
        self.nissued[ename] = self.nissued.get(ename, 0) + 1
        if inc:
            self.count[ename] += 1
            v = self.count[ename]
            ins.then_inc(self.sems[ename], 1)
        else:
            v = self.count[ename] + 1
        for b in reads:
            if b.r.get(ename, 0) < v:
                b.r[ename] = v
        for b in writes:
            b.w = (ename, v)
            b.r = {}
        return ins

    def dma(self, qname, semkey, out, in_, reads=(), writes=(), **kw):
        deps = self._deps(reads, writes)
        self._wait(qname, deps)
        ins = self.engs[qname].dma_start(out=out, in_=in_, **kw)
        self.count[semkey] += 16
        v = self.count[semkey]
        ins.then_inc(self.sems[semkey], 16)
        for b in reads:
            if b.r.get(semkey, 0) < v:
                b.r[semkey] = v
        for b in writes:
            b.w = (semkey, v)
            b.r = {}
        return ins

    def wait_all(self, ename, bufs):
        deps = {}
        for b in bufs:
            items = list(b.r.items())
            if b.w is not None:
                items.append(b.w)
            for k, v in items:
                if deps.get(k, 0) < v:
                    deps[k] = v
        self._wait(ename, deps)


def slab_list():
    L = []
    for j in range(4):
        L.append(("u", j))
        L.append(("u", j + 4))
    for hp in range(4):
        L.append(("q", hp))
    for idx in (0, 1, 2, 4):
        L.append(("kv", idx))
    for idx in (3, 5):
        L.append(("vtok", idx))
    L.append(("ng", 0))
    for kv in range(2):
        for s in range(8):
            L.append(("w1", kv, s))
    L.append(("w2", 0))
    for j in range(8):
        L.append(("ca", j))
        L.append(("mgA", j))
        L.append(("mgB", j))
    for half in range(2):
        for s in range(4):
            L.append(("out", half, s))
    for i in range(NFF):
        L.append(("upg", i))
        L.append(("upv", i))
    for half in range(2):
        for s in range(NFF // 2):
            L.append(("down", half, s))
    for half in range(2):
        L.append(("ple", half))
        for s in range(4):
            L.append(("pgate", half, s))
    return L


SLABS = slab_list()
NSLAB = len(SLABS)
assert NSLAB % NCAST == 0, NSLAB


def _pack_fm(W, c0, ncols, KC):
    out = np.zeros((128, SLABW), np.float32)
    blk = W[:KC * 128, c0:c0 + ncols].reshape(KC, 128, ncols).transpose(1, 0, 2)
    out[:, :KC * ncols] = blk.reshape(128, KC * ncols)
    return out


def _pack_tm(W, half, s):
    out = np.zeros((128, SLABW), np.float32)
    blk = W[s * 256:(s + 1) * 256, half * 512:(half + 1) * 512].reshape(2, 128, 512).transpose(1, 0, 2)
    out[:, :] = blk.reshape(128, 1024)
    return out


def build_wflat(w_in, w_conv_out, w_ck1, w_ck2, w_cv1, w_cv2, w_attn_out, w_out, w_up, w_down,
                w_ple_gate, w_ple):
    wf = np.zeros((128, NSLAB * SLABW), np.float32)
    for i, sl in enumerate(SLABS):
        k = sl[0]
        if k == "u":
            a = _pack_fm(w_in, sl[1] * 128, 128, 8)
        elif k == "q":
            a = _pack_fm(w_in, 1024 + sl[1] * 128, 128, 8)
        elif k in ("kv", "vtok"):
            a = _pack_fm(w_in, 1536 + sl[1] * 128, 128, 8)
        elif k == "ng":
            a = _pack_fm(w_in, 2304, 24, 8)
        elif k == "w1":
            w1 = w_ck1 if sl[1] == 0 else w_cv1
            a = np.zeros((128, SLABW), np.float32)
            blk = w1[sl[2] * 256:(sl[2] + 1) * 256, :].reshape(4, 64, 256).transpose(1, 0, 2)
            a[:64, :] = blk.reshape(64, 1024)
        elif k == "w2":
            a = np.zeros((128, SLABW), np.float32)
            for kv, w2 in enumerate((w_ck2, w_cv2)):
                for hc in range(2):
                    a[:, (kv * 2 + hc) * 64:(kv * 2 + hc + 1) * 64] = w2[hc * 128:(hc + 1) * 128, :]
        elif k == "ca":
            a = np.zeros((128, SLABW), np.float32)
            a[:, 0:512] = _pack_fm(w_conv_out, sl[1] * 128, 128, 4)[:, 0:512]
            a[:, 512:1024] = _pack_fm(w_attn_out, sl[1] * 128, 128, 4)[:, 0:512]
        elif k == "mgA":
            a = _pack_fm(w_in, 2328 + sl[1] * 128, 128, 8)
        elif k == "mgB":
            a = _pack_fm(w_in, 2328 + 1024 + sl[1] * 128, 128, 8)
        elif k == "out":
            a = _pack_tm(w_out, sl[1], sl[2])
        elif k == "upg":
            a = _pack_fm(w_up, sl[1] * 128, 128, 8)
        elif k == "upv":
            a = _pack_fm(w_up, D_FF + sl[1] * 128, 128, 8)
        elif k == "down":
            a = _pack_tm(w_down, sl[1], sl[2])
        elif k == "ple":
            a = _pack_tm(w_ple, sl[1], 0)
        elif k == "pgate":
            a = _pack_tm(w_ple_gate, sl[1], sl[2])
        else:
            raise ValueError(k)
        wf[:, i * SLABW:(i + 1) * SLABW] = a
    return wf


def t5_bucket_np(n):
    n = np.maximum(n, 0)
    nf = np.maximum(n, 16).astype(np.float32)
    large = 16 + (np.log(nf / np.float32(16)) / np.float32(math.log(128 / 16)) * np.float32(16)).astype(np.int32)
    large = np.minimum(large, 31)
    return np.where(n < 16, n, large)


def build_consts():
    c = {}
    n = np.arange(-256, 256)
    oh = np.zeros((33, 512), np.float32)
    bk = t5_bucket_np(n)
    for i, nn in enumerate(n):
        if nn >= 0:
            oh[bk[i], i] = 1.0
        else:
            oh[32, i] = 1.0
    c["oh"] = oh
    k = np.arange(4096)
    c["efull"] = (k[None, :] // 64 == np.arange(64)[:, None]).astype(np.float32)
    x = np.arange(504)
    j = x - 248
    sm = np.zeros((17, 504), np.float32)
    for r in range(16):
        sm[r, j == r - 9] = 1.0
    sm[16, j >= 7] = 1.0
    c["selm"] = sm
    cc = np.arange(256)
    cs, ce = cc * 16, cc * 16 + 31
    ss = np.arange(64) * 64
    agg = ((ce[:, None] >= ss[None, :]) & (cs[:, None] <= ss[None, :] + 63)).astype(np.float32)
    agg[255, :] = 0.0
    aa = np.zeros((128, 2, 65), np.float32)
    aa[:, :, :64] = agg.reshape(2, 128, 64).transpose(1, 0, 2)
    aa[:, :, 64] = 1.0
    c["aggaug"] = aa.reshape(128, 130)
    p = np.arange(128)
    c["w4"] = np.where(p[:, None] > p[None, :], 0.0, NEGM).astype(np.float32)
    i = np.arange(128)[:, None]
    ci = (i >= 64).astype(np.int64)
    rel = (np.arange(126) - 62)[None, :]
    vm = (rel < ci - 1).astype(np.float32)
    am = np.zeros((128, 126), np.float32)
    am = np.where(rel == ci, 2e4, am)
    am = np.where(rel == ci - 1, 3e4, am)
    am = np.where(rel > ci, -1e4 - 8.0 * np.arange(126)[None, :], am)
    c["vm"] = vm.astype(np.float32)
    c["am"] = am.astype(np.float32)
    c["ident"] = np.eye(128, dtype=np.float32)
    return c


def build_program():
    nc = bass.Bass("TRN2", target_bir_lowering=False)

    def din(name, shape, dt=F32):
        return nc.dram_tensor(name, list(shape), dt, kind="ExternalInput").ap()

    x_d = din("x", [SEQ, D])
    p_d = din("p", [SEQ, 256])
    wflat_d = din("wflat", [128, NSLAB * SLABW])
    gains_d = din("gains", [4, D])
    relb_d = din("relb", [32, 8])
    cvw_d = din("cvw", [128, 4 * 31])
    cvp_d = din("cvp", [128, 12])
    ffw_d = din("ffw", [128, 44 * 4])
    pet_d = din("pet", [64, 64])
    oh_d = din("oh", [33, 512])
    efull_d = din("efull", [64, 4096])
    selm_d = din("selm", [17, 504])
    aggaug_d = din("aggaug", [128, 130])
    w4_d = din("w4", [128, 128])
    vm_d = din("vm", [128, 126])
    am_d = din("am", [128, 126])
    ident_d = din("ident", [128, 128])
    y_d = nc.dram_tensor("y", [SEQ, D], F32, kind="ExternalOutput").ap()
    wb_d = nc.dram_tensor("wb", [128, NSLAB * SLABW], BF16, kind="Internal").ap()
    tv_d = nc.dram_tensor("tvd", [8, 128 * 512], F32, kind="Internal").ap()
    dbg = {}
    if DEBUG:
        dbg["x1"] = nc.dram_tensor("dbg_x1", [SEQ, D], F32, kind="ExternalOutput").ap()
        dbg["oT"] = nc.dram_tensor("dbg_oT", [128, 4 * SEQ], BF16, kind="ExternalOutput").ap()
        dbg["aT"] = nc.dram_tensor("dbg_aT", [128, 4 * SEQ], BF16, kind="ExternalOutput").ap()

    with ExitStack() as es:
        S = Sync(nc, es)
        used = [0]

        def sb(name, shape, dt):
            n = 1
            for s_ in shape[1:]:
                n *= s_
            used[0] += n * (4 if dt == F32 else 2)
            return es.enter_context(nc.sbuf_tensor("sb_" + name, list(shape), dt))

        banks = [es.enter_context(nc.psum_tensor("bank%d" % i, [128, 512], F32)) for i in range(8)]
        bbank = [Buf("bank%d" % i) for i in range(8)]

        for k in range(NSLOT):
            S.new_sem("ws%d" % k)
            S.new_sem("wbk%d" % k)
        for k in range(NCAST):
            S.new_sem("cast%d" % k)
        for k in ("xl0", "xl1", "xl2", "xl3", "xa0", "xa1", "xa2", "xa3", "pl0", "pl1", "pl2", "pl3", "gl", "ld", "ld1", "ld2", "ld3", "ld4", "ld5", "ld6", "ld7", "ld8", "ld9", "ld10", "ld11", "ld12", "ld13", "st0", "st1", "st2", "st3", "st4", "st5", "st6", "st7", "vc",
                  "tvs", "dbg"):
            S.new_sem(k)

        xres = sb("xres", [128, TT, D], F32)
        bx = [Buf("xres%d" % t) for t in range(TT)]
        gbc = sb("gbc", [128, D], F32); bgbc = Buf("gbc")
        hT = sb("hT", [128, 8, TS], BF16); bhT = Buf("hT")
        hn = sb("hn", [128, 2, D], BF16); bhn = [Buf("hn0"), Buf("hn1")]
        abuf = sb("abuf", [128, 4, 30 + TS], BF16); babuf = [Buf("abuf%d" % j) for j in range(4)]
        ysb = sb("ysb", [128, 4, TS], F32); bysb = [Buf("ysb%d" % j) for j in range(4)]
        big = sb("big", [128, NFF, TS], BF16); bbig = [Buf("big%d" % j) for j in range(NFF)]
        Q = sb("Q", [128, 8, TS], BF16)
        bQlo = [[Buf("Qlo%d_%d" % (g, t)) for t in range(TT)] for g in range(2)]
        bQhi = [[Buf("Qhi%d_%d" % (g, t)) for t in range(TT)] for g in range(2)]
        KsT = [sb("KsT%d" % g, [128, SEQ], BF16) for g in range(2)]
        bKs = [[Buf("Ks%d_%d" % (g, t)) for t in range(NT)] for g in range(2)]
        KwT = [sb("KwT%d" % g, [65, 1024], BF16) for g in range(2)]
        bKw = [[Buf("Kw%d_%d" % (g, t)) for t in range(2)] for g in range(2)]
        Vs = sb("Vs", [128, 32, 2, 65], BF16); bVs = [Buf("Vs%d" % t) for t in range(NT)]
        Vw = sb("Vw", [128, 8, 2, 65], BF16); bVw = [Buf("Vw%d" % t) for t in range(2)]
        kcT = [[sb("kcT%d_%d" % (kv, g), [64, 16 + TS], BF16) for g in range(2)] for kv in range(2)]
        bkcT = [[Buf() for g in range(2)] for kv in range(2)]
        kcmpT = [sb("kcmpT%d" % g, [65, 256], BF16) for g in range(2)]
        bkcmp = [Buf() for g in range(2)]
        vcmp = sb("vcmp", [128, 2, 2, 65], BF16); bvcmp = Buf("vcmp")
        vst = sb("vst", [32, 2, 65], BF16); bvst = Buf("vst")
        hid = sb("hid", [128, 2, 2, 2, 32], BF16); bhid = [Buf(), Buf()]
        cconst = sb("cconst", [128, 2, 2], F32); bcconst = Buf("cconst")
        dg = sb("dg", [128, 4, 31, 128], BF16); bdgc = [Buf("dg%d" % c) for c in range(4)]
        gates = sb("gates", [128, TT, 24], F32); bgates = [Buf() for t in range(TT)]
        Pc = sb("Pc", [128, 2, 512], BF16); bPc = [Buf(), Buf()]
        NPS = 4
        Ps = sb("Ps", [128, NPS, 512], BF16); bPs = [Buf() for _ in range(NPS)]
        osc = sb("osc", [128, 4, 256], BF16); bosc = [Buf() for _ in range(4)]
        uhalo = sb("uhalo", [128, 2 * NFF, 2], BF16); buhalo = [Buf() for _ in range(2 * NFF)]
        dgf = sb("dgf", [128, 2, 6, 128], BF16); bdgf = [Buf(), Buf()]
        pT = sb("pT", [128, 2, TS], BF16); bpT = Buf("pT")
        pbf = sb("pbf", [128, TT, 256], BF16); bpbf = [Buf("pbf%d" % t) for t in range(TT)]
        biasD = sb("biasD", [128, 8, 128], BF16)
        biasO = sb("biasO", [128, 8, 128], BF16)
        mstack = sb("mstack", [17, 8, 128], BF16)
        btab = Buf("tables"); btab2 = Buf("tables2")
        selm = sb("selm", [17, 504], BF16)
        aggaug = sb("aggaug", [128, 2, 65], BF16)
        w4 = sb("w4", [128, 128], BF16)
        identb = sb("identb", [128, 128], BF16)
        onesf = sb("onesf", [128, 128], F32)
        vm = sb("vm", [128, 126], F32)
        am = sb("am", [128, 126], F32)
        cvp = sb("cvp", [128, 12], F32)
        ffw = sb("ffw", [128, 44, 4], F32)
        peT = sb("peT", [64, 64], BF16)
        b31bc = sb("b31bc", [128, 8], F32)
        small = sb("small", [128, 64], F32); bsmall = Buf("small")
        tk = sb("tk", [128, 4, 64], F32); btk = Buf("tk")
        m8 = sb("m8", [128, 16], F32)
        selpad = sb("selpad", [128, 128], BF16); bselpad = Buf("selpad")
        f4 = sb("f4", [128, 4, 8], F32); bf4 = [Buf() for _ in range(4)]
        NSCR = 7
        scr = [sb("scr%d" % i, [128, 516], F32) for i in range(NSCR)]
        bscr = [Buf("scr%d" % i) for i in range(NSCR)]
        wring = sb("wring", [128, NSLOT, SLABW], BF16)
        bslot = [Buf("slot%d" % k) for k in range(NSLOT)]
        stage = scr[5]; bstage = bscr[5]
        print("SBUF bytes/partition used:", used[0])

        yT = big
        aT_off = 8
        oT_off = 12

        bcast = [Buf("cast%d" % k) for k in range(NCAST)]
        per_cast = NSLAB // NCAST

        wstate = {"issued": 0, "next": 0}
        total_slabs = NSLAB * NT

        bwbk = [Buf("wbk%d" % i) for i in range(NSLAB)]

        def w_issue(upto):
            while wstate["issued"] < min(upto, total_slabs):
                gi = wstate["issued"]
                i = gi % NSLAB
                k = gi % NSLOT
                if gi < NSLAB or (gi < 2 * NSLAB and i % 2 == 1):
                    S.dma("pool", "ws%d" % k, wring[:, k, :], wflat_d[:, i * SLABW:(i + 1) * SLABW], writes=[bslot[k]])
                    if gi >= NSLAB or i % 2 == 0:
                        S.dma("sp", "wbk%d" % k, wb_d[:, i * SLABW:(i + 1) * SLABW], wring[:, k, :],
                              reads=[bslot[k]], writes=[bwbk[i]])
                else:
                    S.dma("sp", "ws%d" % k, wring[:, k, :], wb_d[:, i * SLABW:(i + 1) * SLABW],
                          reads=[bwbk[i]], writes=[bslot[k]])
                wstate["issued"] += 1

        def w_next(kind):
            gi = wstate["next"]
            assert SLABS[gi % NSLAB][0] == kind, (SLABS[gi % NSLAB], kind)
            w_issue(gi + 1 + PREF)
            wstate["next"] += 1
            k = gi % NSLOT
            return wring[:, k, :], bslot[k]

        def mm(out, lhsT, rhs, start, stop, reads, writes, sgc=False, inc=True):
            S.op("pe", lambda e: e.matmul(out, lhsT=lhsT, rhs=rhs, start=start, stop=stop, skip_group_check=sgc),
                 reads=reads, writes=writes, acc=True, inc=inc)

        def act(out, in_, func, reads, writes, bias=None, scale=None, accum_out=None):
            kw = {}
            if bias is not None:
                kw["bias"] = bias
            if scale is not None:
                kw["scale"] = scale
            if accum_out is not None:
                kw["accum_out"] = accum_out
            S.op("act", lambda e: e.activation(out=out, in_=in_, func=func, **kw), reads=reads, writes=writes)

        def tt_(eng, out, in0, in1, op, reads, writes):
            S.op(eng, lambda e: e.tensor_tensor(out=out, in0=in0, in1=in1, op=op), reads=reads, writes=writes)

        def ts_(eng, out, in0, s1, s2, op0, op1, reads, writes):
            if op1 is None:
                S.op(eng, lambda e: e.tensor_scalar(out=out, in0=in0, scalar1=s1, scalar2=None, op0=op0),
                     reads=reads, writes=writes)
            else:
                S.op(eng, lambda e: e.tensor_scalar(out=out, in0=in0, scalar1=s1, scalar2=s2, op0=op0, op1=op1),
                     reads=reads, writes=writes)

        def stt_(eng, out, in0, scalar, in1, op0, op1, reads, writes):
            S.op(eng, lambda e: e.scalar_tensor_tensor(out=out, in0=in0, scalar=scalar, in1=in1, op0=op0, op1=op1),
                 reads=reads, writes=writes)

        def cp_(eng, out, in_, reads, writes):
            S.op(eng, lambda e: e.tensor_copy(out=out, in_=in_), reads=reads, writes=writes)

        def bank3(b, a, n):
            return banks[b][:, 0:a * n].rearrange("p (a n) -> p a n", a=a)

        def bank_bf(b):
            return banks[b][:, :].bitcast(BF16)

        bsetup = Buf("setup")

        def ld(out, in_, writes=(bsetup,), sem="ld", **kw):
            S.dma("pool", sem, out, in_, writes=list(writes), **kw)

        for tt in range(TT):
            S.dma("sp", "xl%d" % tt, xres[:, tt, :], x_d[tt * 128:(tt + 1) * 128, :], writes=[bx[tt]])
        S.dma("sp", "gl", gbc[:], gains_d[0:1, :].partition_broadcast(128), writes=[bgbc])
        cvw = tk[:].rearrange("p a b -> p (a b)")
        ld(cvw[:, 0:124], cvw_d, writes=[btk], sem="ld1")
        rbx = scr[1]; ohx = scr[2]
        ld(rbx[0:32, 0:8], relb_d, writes=[bscr[1]], sem="ld2")
        ld(rbx[0:32, 8:16], relb_d[31:32, :].partition_broadcast(32), writes=[bscr[1]], sem="ld3")
        ld(ohx[0:33, 0:512], oh_d, writes=[bscr[2]], sem="ld4")
        S.dma("pool", "ld5", stage[64:65, 0:8], relb_d[31:32, :], writes=[bstage])
        bident = Buf("ident")
        ld(identb[:], ident_d, writes=[bident], sem="ld7")
        ld(cvp[:], cvp_d)
        ld(vm[:], vm_d)
        ld(am[:], am_d)
        ld(ffw[:].rearrange("p a b -> p (a b)"), ffw_d)
        ld(b31bc[:], relb_d[31:32, :].partition_broadcast(128))
        ld(selm[:], selm_d)
        ld(aggaug[:].rearrange("p a b -> p (a b)"), aggaug_d)
        ld(w4[:], w4_d)
        ld(peT[:], pet_d)
        bsetup.w = ("ld", S.count["ld"])
        w_issue(1 + PREF)
        befull = [Buf("efull0"), Buf("efull1")]
        for g in range(2):
            ld(KsT[g][64:128, :], efull_d, writes=[befull[g]] + bKs[g], sem="ld%d" % (12 + g))

        def cast_chunk(k):
            c0, c1 = k * per_cast * SLABW, (k + 1) * per_cast * SLABW
            S.dma("pool", "cast%d" % k, wb_d[:, c0:c1], wflat_d[:, c0:c1], writes=[bcast[k]])
        bones = Buf("ones")
        S.op("dve", lambda e: e.memset(onesf[:], 1.0 / 512.0), writes=[bones])
        for g in range(2):
            S.op("dve", lambda e: e.memset(kcmpT[g][0:64, :], 0.0), writes=[bkcmp[g]])
            S.op("dve", lambda e: e.memset(kcmpT[g][64:65, :], 1.0), writes=[bkcmp[g]])
            S.op("dve", lambda e: e.memset(KwT[g][64:65, :], 1.0), writes=[bKw[g][0], bKw[g][1]])
            for kv in range(2):
                S.op("dve", lambda e: e.memset(kcT[kv][g][:, 0:16], 0.0), writes=[bkcT[kv][g]])
        S.op("dve", lambda e: e.memset(vcmp[:].rearrange("p a b c -> p (a b c)"), 0.0), writes=[bvcmp])
        S.op("dve", lambda e: e.memset(Vs[:, :, :, 64:65].rearrange("p a b c -> p (a b c)"), 1.0), writes=bVs)
        S.op("dve", lambda e: e.memset(Vw[:, :, :, 64:65].rearrange("p a b c -> p (a b c)"), 1.0), writes=bVw)
        S.op("dve", lambda e: e.memset(vst[:, :, 64:65].rearrange("p a b -> p (a b)"), 1.0), writes=[bvst])
        S.op("dve", lambda e: e.memset(abuf[:].rearrange("p a b -> p (a b)"), 0.0), writes=babuf)
        S.op("dve", lambda e: e.memset(uhalo[:].rearrange("p a b -> p (a b)"), 0.0), writes=buhalo)
        S.op("dve", lambda e: e.memset(selpad[:], 0.0), writes=[bselpad])
        allQ = [b for g in range(2) for b in bQhi[g]]
        cp_("dve", Q[64:65, :, :], stage[64:65, 0:8].unsqueeze(2).to_broadcast([1, 8, TS]), [bstage], allQ)
        tt_("dve", rbx[0:32, 0:8], rbx[0:32, 0:8], rbx[0:32, 8:16], ALU.subtract, [bscr[1]], [bscr[1]])
        S.op("dve", lambda e: e.memset(rbx[32:33, 0:8], NEGM), reads=[bscr[1]], writes=[bscr[1]])
        mm(banks[0][0:8, :], rbx[0:33, 0:8], ohx[0:33, 0:512], True, True, [bscr[1], bscr[2]], [bbank[0]])
        tvs = scr[3]
        act(tvs[0:8, 0:512], banks[0][0:8, :], AF.Copy, [bbank[0]], [bscr[3]])
        btv = Buf("tvd")
        S.dma("pool", "tvs", tv_d.rearrange("h (r n) -> h r n", r=128),
              tvs[0:8, 0:512].unsqueeze(1).to_broadcast([8, 128, 512]), reads=[bscr[3]], writes=[btv])

        def load_bias_tables():
            tsem = iter(("ld6", "ld9", "ld10", "ld11"))

            def toeplitz3(dst_ap, rows, off, pstep, h0, nh, wbuf):
                src = bass.AP(tensor=tv_d.tensor, offset=h0 * 128 * 512 + off, ap=[[pstep, rows], [128 * 512, nh], [1, 128]])
                S.dma("sp", next(tsem), dst_ap, src, reads=[btv], writes=list(wbuf))
            ystg = ysb[:].rearrange("p a b -> p (a b)")
            toeplitz3(ystg[:, 0:1024].rearrange("p (h i) -> p h i", h=8), 128, 256, 511, 0, 8, [bysb[0], bysb[1]])
            cp_("dve", biasD[:].rearrange("p h i -> p (h i)"), ystg[:, 0:1024], [bysb[0], bysb[1]], [btab])
            toeplitz3(ystg[:, 1024:2048].rearrange("p (h i) -> p h i", h=8), 128, 256 + 128, 511, 0, 8, [bysb[2], bysb[3]])
            cp_("dve", biasO[:].rearrange("p h i -> p (h i)"), ystg[:, 1024:2048], [bysb[2], bysb[3]], [btab])
            for hh in range(2):
                stg = scr[0] if hh == 0 else scr[4]
                bst = bscr[0] if hh == 0 else bscr[4]
                toeplitz3(stg[0:16, 0:512].rearrange("p (h i) -> p h i", h=4), 16, 256 + 113, 496, hh * 4, 4, [bst])
                cp_("dve", mstack[0:16, hh * 4:hh * 4 + 4, :].rearrange("p h i -> p (h i)"), stg[0:16, 0:512], [bst], [btab])
        S.op("dve", lambda e: e.memset(stage[0:32, 0:128], NEGM), reads=[bstage], writes=[bstage])
        for h in range(8):
            S.dma("pool", "ld8", mstack[16:17, h, :], stage[0:1, 0:128], reads=[bstage], writes=[btab2])
        btab2.w = ("ld8", S.count["ld8"])

        def gen_dg(c):
            for j in range(31):
                last = j == 30
                S.op("dve", lambda e: e.tensor_scalar(out=dg[:, c, j, :], in0=identb[:],
                                                      scalar1=cvw[:, c * 31 + j:c * 31 + j + 1], scalar2=None,
                                                      op0=ALU.mult),
                     reads=[btk, bsetup], writes=[bdgc[c]] if last else [], inc=last)

        junk = scr[6][:, :].bitcast(BF16)

        xal4 = big[:].rearrange("p a b -> p (a b)").bitcast(F32)

        def src_res(tt):
            return xres[:, tt, :], [bx[tt]]

        def src_alias(tt):
            return xal4[:, tt * D:(tt + 1) * D], bbig[4 * tt:4 * tt + 4]

        def norm_stats(src, c0=0):
            for tt in range(TT):
                ap, bufs = src(tt)
                act(junk[:, 0:D], ap, AF.Square, bufs, [bscr[6], bsmall], accum_out=small[:, c0 + tt:c0 + tt + 1])
            act(small[:, c0 + 4:c0 + 8], small[:, c0:c0 + 4], AF.Sqrt, [bsmall], [bsmall], bias=EPS, scale=1.0 / D)
            S.op("dve", lambda e: e.reciprocal(out=small[:, c0 + 8:c0 + 12], in_=small[:, c0 + 4:c0 + 8]),
                 reads=[bsmall], writes=[bsmall])

        def rmsnorm_stats_all():
            norm_stats(src_res, 0)

        def load_gain(gi):
            S.dma("sp", "gl", gbc[:], gains_d[gi:gi + 1, :].partition_broadcast(128), writes=[bgbc])

        def norm_apply(src, c0=0):
            for tt in range(TT):
                ap, bufs = src(tt)
                stt_("dve", hn[:, tt % 2, :], ap, small[:, c0 + 8 + tt:c0 + 9 + tt], gbc[:], ALU.mult, ALU.mult,
                     bufs + [bsmall, bgbc], [bhn[tt % 2]])
                pb = tt % 2
                pbv = bank_bf(pb)
                for kc in range(8):
                    S.op("pe", lambda e: e.transpose(out=pbv[:, kc * 128:(kc + 1) * 128],
                                                     in_=hn[:, tt % 2, kc * 128:(kc + 1) * 128],
                                                     identity=identb[:]),
                         reads=[bhn[tt % 2], bident], writes=[bbank[pb]], acc=True, inc=(kc == 7))
                dst = hT[:, :, tt * 128:(tt + 1) * 128]
                srcv = pbv[:, 0:1024].rearrange("p (a b) -> p a b", a=8)
                if tt % 2 == 0:
                    act(dst, srcv, AF.Copy, [bbank[pb]], [bhT])
                else:
                    cp_("dve", dst, srcv, [bbank[pb]], [bhT])

        def rmsnorm_to_hT(next_gain):
            norm_stats(src_res, 0)
            norm_apply(src_res, 0)
            if next_gain is not None:
                load_gain(next_gain)

        def fm_gemm(bank, slab, KC, ncols, coff, rhs_fn, rbufs, bsl, M=128, poff=0):
            for kc in range(KC):
                mm(banks[bank][poff:poff + M, :], slab[:, kc * ncols + coff:kc * ncols + coff + M], rhs_fn(kc),
                   kc == 0, kc == KC - 1, [bsl] + rbufs, [bbank[bank]], inc=(kc == KC - 1))

        def final_norm(Tp):
            tp0 = Tp * TS
            rmsnorm_stats_all()
            for tt in range(TT):
                for half in range(2):
                    k = tt * 2 + half
                    if k < 4:
                        ob, bob = ysb[:, k, :], bysb[k]
                    else:
                        ob, bob = scr[k - 4][:, 0:TS], bscr[k - 4]
                    hsl = slice(half * 512, (half + 1) * 512)
                    stt_("dve", ob, xres[:, tt, hsl], small[:, 8 + tt:9 + tt], gbc[:, hsl], ALU.mult, ALU.mult,
                         [bx[tt], bsmall, bgbc], [bob])
                    S.dma("pool", "st%d" % k, y_d[tp0 + tt * 128:tp0 + (tt + 1) * 128, hsl], ob, reads=[bob])
                if Tp + 1 < NT:
                    S.dma("sp", "xl%d" % tt, xres[:, tt, :], x_d[tp0 + TS + tt * 128:tp0 + TS + (tt + 1) * 128, :],
                          writes=[bx[tt]])
            if Tp + 1 < NT:
                load_gain(1)

        for T in range(NT):
            t0 = T * TS
            MARKS.append((T, "1norm", S.nissued.get("pe", 0)))
            if T == 0:
                rmsnorm_to_hT(1)
            for tt in range(TT):
                S.dma("pool", "pl%d" % tt, pbf[:, tt, :], p_d[t0 + tt * 128:t0 + (tt + 1) * 128, :], writes=[bpbf[tt]])
            MARKS.append((T, "3conv", S.nissued.get("pe", 0)))
            def u_gemm(j):
                bs = (j % 2) * 2
                slA, bA = w_next("u")
                fm_gemm(bs, slA, 8, 128, 0, lambda kc: hT[:, kc, :], [bhT], bA)
                slB, bB = w_next("u")
                fm_gemm(bs + 1, slB, 8, 128, 0, lambda kc: hT[:, kc, :], [bhT], bB)
                act(scr[j % 2][:, 0:TS], banks[bs + 1][:, :], AF.Sigmoid, [bbank[bs + 1]], [bscr[j % 2]])
                tt_("dve", abuf[:, j, 30:30 + TS], banks[bs][:, :], scr[j % 2][:, 0:TS], ALU.mult,
                    [bbank[bs], bscr[j % 2]], [babuf[j]])

            def conv_mm(j):
                bk = 4 + (j % 2)
                if T == 0:
                    gen_dg(j)
                for jj in range(31):
                    mm(banks[bk][:, :], dg[:, j, jj, :], abuf[:, j, jj:jj + TS], jj == 0, jj == 30,
                       [bdgc[j], babuf[j]], [bbank[bk]], inc=(jj == 30))
                act(ysb[:, j, :], banks[bk][:, :], AF.Identity, [bbank[bk], bsetup], [bysb[j]],
                    bias=cvp[:, j * 3:j * 3 + 1])
                act(scr[2 + j][:, 0:TS], ysb[:, j, :], AF.Square, [bysb[j]], [bscr[2 + j]])
                cp_("pool", abuf[:, j, 0:30], abuf[:, j, TS:TS + 30], [babuf[j]], [babuf[j]])

            u_gemm(0)
            for j in range(4):
                if j + 1 < 4:
                    u_gemm(j + 1)
                conv_mm(j)
            for j in range(4):
                mm(banks[6][:, :], onesf[:], ysb[:, j, :], j == 0, j == 3, [bones, bysb[j]], [bbank[6]], inc=(j == 3))
            for j in range(4):
                mm(banks[7][:, :], onesf[:], scr[2 + j][:, 0:TS], j == 0, j == 3,
                   [bones, bscr[2 + j]], [bbank[7]], inc=(j == 3))
            mean = scr[0]; rstd_ = scr[1]
            cp_("dve", mean[:, 0:TS], banks[6][:, :], [bbank[6]], [bscr[0]])
            tt_("dve", scr[6][:, 0:TS], mean[:, 0:TS], mean[:, 0:TS], ALU.mult, [bscr[0]], [bscr[6]])
            tt_("dve", scr[6][:, 0:TS], banks[7][:, :], scr[6][:, 0:TS], ALU.subtract, [bbank[7], bscr[6]], [bscr[6]])
            MARKS.append((T, "4qkv", S.nissued.get("pe", 0)))
            rot = [0]

            def nbank():
                rot[0] = (rot[0] + 1) % 4
                return rot[0]
            for hp in range(4):
                sl, bsl = w_next("q")
                for half in range(2):
                    h = hp * 2 + half
                    g = h // 4
                    bk = nbank()
                    fm_gemm(bk, sl, 8, 128, half * 64, lambda kc: hT[:, kc, :], [bhT], bsl, M=64)
                    act(Q[0:64, h, :], banks[bk][0:64, :], AF.Copy, [bbank[bk]], bQlo[g], scale=0.125)
            act(scr[6][:, 0:TS], scr[6][:, 0:TS], AF.Sqrt, [bscr[6]], [bscr[6]], bias=EPS, scale=1.0)
            S.op("dve", lambda e: e.reciprocal(out=rstd_[:, 0:TS], in_=scr[6][:, 0:TS]), reads=[bscr[6]], writes=[bscr[1]])
            for j in range(4):
                tmp = scr[2 + j]; btmp = bscr[2 + j]
                tt_("dve", tmp[:, 0:TS], ysb[:, j, :], mean[:, 0:TS], ALU.subtract, [bysb[j], bscr[0]], [btmp])
                tt_("dve", tmp[:, 0:TS], tmp[:, 0:TS], rstd_[:, 0:TS], ALU.mult, [btmp, bscr[1]], [btmp])
            for idx in (0, 1, 2, 4):
                sl, bsl = w_next("kv")
                for g in range(2):
                    bk = nbank()
                    fm_gemm(bk, sl, 8, 128, g * 64, lambda kc: hT[:, kc, :], [bhT], bsl, M=64)
                    if idx in (0, 1):
                        act(kcT[idx][g][:, 16:16 + TS], banks[bk][0:64, :], AF.Copy, [bbank[bk]], [bkcT[idx][g]])
                    elif idx == 2:
                        act(KsT[g][0:64, t0:t0 + TS], banks[bk][0:64, :], AF.Copy, [bbank[bk]], [bKs[g][T]])
                    else:
                        w0 = (T % 2) * TS
                        act(KwT[g][0:64, w0:w0 + TS], banks[bk][0:64, :], AF.Copy, [bbank[bk]], [bKw[g][T % 2]])
            for j in range(4):
                tmp = scr[2 + j]; btmp = bscr[2 + j]
                act(big[:, aT_off + j, :], tmp[:, 0:TS], AF.Silu, [btmp, bsetup], [bbig[aT_off + j]],
                    bias=cvp[:, j * 3 + 2:j * 3 + 3], scale=cvp[:, j * 3 + 1:j * 3 + 2])
            for idx in (3, 5):
                sl, bsl = w_next("vtok")
                vbk = 4 if idx == 3 else 5
                pv = bank3(vbk, 4, 128)
                for tt in range(TT):
                    for kc in range(8):
                        mm(pv[:, tt, :], hT[:, kc, tt * 128:(tt + 1) * 128], sl[:, kc * 128:(kc + 1) * 128],
                           kc == 0, kc == 7, [bhT, bsl], [bbank[vbk]], inc=(kc == 7))
                if idx == 3:
                    dst = Vs[:, T * 4:T * 4 + 4, :, 0:64]
                    wb_ = [bVs[T]]
                else:
                    dst = Vw[:, (T % 2) * 4:(T % 2) * 4 + 4, :, 0:64]
                    wb_ = [bVw[T % 2]]
                act(dst, banks[vbk][:, :].rearrange("p (a g d) -> p a g d", a=4, g=2), AF.Copy, [bbank[vbk]], wb_)
            sl, bsl = w_next("ng")
            pg_ = bank3(6, 4, 24)
            for tt in range(TT):
                for kc in range(8):
                    mm(pg_[:, tt, :], hT[:, kc, tt * 128:(tt + 1) * 128], sl[:, kc * 24:(kc + 1) * 24],
                       kc == 0, kc == 7, [bhT, bsl], [bbank[6]], inc=(kc == 7))
            act(gates[:, :, :], pg_, AF.Sigmoid, [bbank[6]], bgates)
            MARKS.append((T, "5cmp", S.nissued.get("pe", 0)))
            c_lo = max(0, 32 * T - 1)
            c_hi = 32 * T + 30
            n_c = c_hi - c_lo + 1
            col0 = 16 * c_lo - t0 + 16
            for kv in range(2):
                bk = 4 + kv
                for s in range(8):
                    sl, bsl = w_next("w1")
                    for g in range(2):
                        for hc in range(2):
                            r0 = (g * 2 + hc) * 32
                            for pp in range(4):
                                pos = s * 4 + pp
                                rhs = kcT[kv][g][:, col0 + pos:col0 + pos + 16 * (n_c - 1) + 1:16]
                                mm(banks[bk][:, r0:r0 + n_c], sl[0:64, pp * 256 + hc * 128:pp * 256 + hc * 128 + 128],
                                   rhs, pos == 0 and g == 0 and hc == 0, pos == 31, [bsl, bkcT[kv][g]], [bbank[bk]],
                                   sgc=True)
                    if T == 0:
                        for hc in range(2):
                            for pp in range(4):
                                pos = s * 4 + pp
                                mm(banks[bk][:, 256 + hc:256 + hc + 1],
                                   sl[0:64, pp * 256 + hc * 128:pp * 256 + hc * 128 + 128],
                                   peT[:, kv * 32 + pos:kv * 32 + pos + 1], False, pos == 31,
                                   [bsl, bsetup], [bbank[bk]], sgc=True)
                if T == 0:
                    act(cconst[:, kv, :], banks[bk][:, 256:258], AF.Copy, [bbank[bk]], [bcconst])
                for g in range(2):
                    for hc in range(2):
                        r0 = (g * 2 + hc) * 32
                        act(hid[:, kv, g, hc, 0:n_c], banks[bk][:, r0:r0 + n_c], AF.Gelu_apprx_tanh,
                            [bbank[bk], bcconst], [bhid[kv]], bias=cconst[:, kv, hc:hc + 1])
                for g in range(2):
                    cp_("pool", kcT[kv][g][:, 0:16], kcT[kv][g][:, TS:TS + 16], [bkcT[kv][g]], [bkcT[kv][g]])
            sl, bsl = w_next("w2")
            for g in range(2):
                for hc in range(2):
                    mm(banks[6][0:64, g * 32:g * 32 + n_c], sl[:, hc * 64:(hc + 1) * 64], hid[:, 0, g, hc, 0:n_c],
                       hc == 0, hc == 1, [bsl, bhid[0]], [bbank[6]])
                act(kcmpT[g][0:64, c_lo:c_lo + n_c], banks[6][0:64, g * 32:g * 32 + n_c], AF.Copy,
                    [bbank[6]], [bkcmp[g]])
            for g in range(2):
                for hc in range(2):
                    mm(banks[7][0:n_c, g * 64:(g + 1) * 64], hid[:, 1, g, hc, 0:n_c],
                       sl[:, (2 + hc) * 64:(3 + hc) * 64], hc == 0, hc == 1, [bsl, bhid[1]], [bbank[7]])
            act(vst[0:n_c, :, 0:64], banks[7][0:n_c, 0:128].rearrange("p (g d) -> p g d", g=2), AF.Copy,
                [bbank[7]], [bvst])
            cs = c_lo
            while cs <= c_hi:
                ce = min(c_hi, (cs // 128) * 128 + 127)
                n = ce - cs + 1
                S.dma("pool", "vc", vcmp[cs % 128:cs % 128 + n, cs // 128, :, :], vst[cs - c_lo:cs - c_lo + n, :, :],
                      reads=[bvst], writes=[bvcmp])
                cs = ce + 1
            if T == 0:
                load_bias_tables()
            MARKS.append((T, "6attn", S.nissued.get("pe", 0)))
            nch = 1 if c_hi < 128 else 2
            O_c, O_s, O_w = bank3(2, 4, 65), bank3(3, 4, 65), bank3(4, 4, 65)
            IMP = bank3(5, 4, 65)

            def norm_gate(b, slot, Ob, bOb, tt, g):
                dn = f4[:, slot, 0:4]; fr = f4[:, slot, 4:8]
                ts_("dve", dn.unsqueeze(2), Ob[:, :, 64:65], 1e-30, None, ALU.add, None, [bOb], [bf4[slot]])
                S.op("dve", lambda e: e.reciprocal(out=fr, in_=dn), reads=[bf4[slot]], writes=[bf4[slot]])
                gsl = gates[:, tt, g * 12 + b:g * 12 + b + 10:3]
                tt_("dve", fr, fr, gsl, ALU.mult, [bf4[slot], bgates[tt]], [bf4[slot]])
                tt_("dve", osc[:, slot, :].rearrange("p (h d) -> p h d", h=4), Ob[:, :, 0:64],
                    fr.unsqueeze(2).to_broadcast([128, 4, 64]), ALU.mult, [bOb, bf4[slot]], [bosc[slot]])

            def stageA(tt, g, par):
                qt = T * 4 + tt
                qs = slice(tt * 128, (tt + 1) * 128)
                hs = slice(g * 4, g * 4 + 4)
                for ch in range(nch):
                    psS = bank3(6, 4, 128)
                    mm(psS, kcmpT[g][0:65, ch * 128:(ch + 1) * 128], Q[0:65, hs, qs], True, False,
                       [bkcmp[g], bQlo[g][tt], bQhi[g][tt]], [bbank[6]], inc=False)
                    base = 128 * ch - 8 * qt + 248
                    mm(psS, selm[0:17, base:base + 128], mstack[0:17, hs, :], False, True,
                       [bsetup, btab, btab2], [bbank[6]])
                    act(Pc[:, ch, :], banks[6][:, :], AF.Exp, [bbank[6]], [bPc[ch]])
                    yield
                    for h in range(4):
                        mm(O_c[:, h, :], Pc[:, ch, h * 128:(h + 1) * 128], vcmp[:, ch, g, :],
                           ch == 0 and h == 0, ch == nch - 1, [bPc[ch], bvcmp], [bbank[2]], sgc=True, inc=(h == 3))
                    for h in range(4):
                        mm(IMP[:, h, :], Pc[:, ch, h * 128:(h + 1) * 128], aggaug[:, ch, :],
                           ch == 0 and h == 0, ch == nch - 1, [bPc[ch], bsetup], [bbank[5]], sgc=True, inc=(h == 3))
                    yield
                den4 = tk[:, 0, 0:4]; rec4 = tk[:, 0, 8:12]
                ts_("dve", den4.unsqueeze(2), IMP[:, :, 64:65], 1e-30, None, ALU.add, None, [bbank[5]], [btk])
                S.op("dve", lambda e: e.reciprocal(out=rec4, in_=den4), reads=[btk], writes=[btk])
                impa = tk[:, 1, :]
                ts_("dve", impa, IMP[:, 0, 0:64], rec4[:, 0:1], None, ALU.mult, None, [bbank[5], btk], [btk])
                for h in range(1, 4):
                    stt_("dve", impa, IMP[:, h, 0:64], rec4[:, h:h + 1], impa, ALU.mult, ALU.add,
                         [bbank[5], btk], [btk])
                off = 62 - 2 * qt
                tt_("dve", impa, impa, vm[:, off:off + 64], ALU.mult, [btk, bsetup], [btk])
                tt_("dve", impa, impa, am[:, off:off + 64], ALU.add, [btk, bsetup], [btk])
                S.op("dve", lambda e: e.memset(impa[:, 0:1], 1e4), reads=[btk], writes=[btk])
                S.op("dve", lambda e: e.max(out=m8[:, 0:8], in_=impa), reads=[btk], writes=[btk])
                S.op("dve", lambda e: e.match_replace(out=tk[:, 2, :], in_to_replace=m8[:, 0:8], in_values=impa,
                                                      imm_value=-1e9), reads=[btk], writes=[btk])
                S.op("dve", lambda e: e.max(out=m8[:, 8:16], in_=tk[:, 2, :]), reads=[btk], writes=[btk])
                ts_("dve", selpad[:, 64:128], impa, m8[:, 15:16], 1.0, ALU.is_ge, ALU.subtract, [btk], [bselpad])
                norm_gate(0, 0 if par == 0 else 3, O_c, bbank[2], tt, g)
                for _ in range(10):
                    yield
                tb = bank_bf(5)
                S.op("pe", lambda e: e.transpose(out=tb[:, 0:128], in_=selpad[:, :], identity=identb[:]),
                     reads=[bselpad, bsetup], writes=[bbank[5]])
                for h in range(4):
                    ts_("dve", Q[64:128, g * 4 + h, qs], tb[64:128, 0:128], -NEGM, b31bc[64:128, g * 4 + h:g * 4 + h + 1],
                        ALU.mult, ALU.add, [bbank[5], bsetup], [bQhi[g][tt]])

            def make_item(tt, g, par):
                qt = T * 4 + tt
                qs = slice(tt * 128, (tt + 1) * 128)
                hs = slice(g * 4, g * 4 + 4)
                jobs = [("w", kt) for kt in range(max(0, qt - 4), qt + 1)] + [("s", kt) for kt in range(0, qt + 1)]

                def emit_qk(ji):
                    br, kt = jobs[ji]
                    sbk = (0, 1, 7)[ji % 3]
                    psS = bank3(sbk, 4, 128)
                    extra = None
                    if kt == qt:
                        extra = biasD[:, hs, :]
                    elif kt == qt - 1:
                        extra = biasO[:, hs, :]
                    elif br == "w" and kt == qt - 4:
                        extra = w4[:, :].unsqueeze(1).to_broadcast([128, 4, 128])
                    if br == "s":
                        mm(psS, KsT[g][:, kt * 128:(kt + 1) * 128], Q[:, hs, qs], True, extra is None,
                           [bKs[g][kt // 4], befull[g], bQlo[g][tt], bQhi[g][tt]], [bbank[sbk]], inc=(extra is None))
                    else:
                        w0 = (kt % 8) * 128
                        mm(psS, KwT[g][0:65, w0:w0 + 128], Q[0:65, hs, qs], True, extra is None,
                           [bKw[g][(kt // 4) % 2], bQlo[g][tt], bQhi[g][tt]], [bbank[sbk]], inc=(extra is None))
                    if extra is not None:
                        mm(psS, identb[:], extra, False, True, [bsetup, btab], [bbank[sbk]])
                    pi = ji % NPS
                    act(Ps[:, pi, :], banks[sbk][:, :], AF.Exp, [bbank[sbk]], [bPs[pi]])

                def emit_pv(ji):
                    br, kt = jobs[ji]
                    pi = ji % NPS
                    if br == "s":
                        first, last = kt == 0, kt == qt
                        vv = Vs[:, kt, g, :]
                        vb = bVs[kt // 4]
                        Ob, bOb = O_s, bbank[3]
                    else:
                        first, last = kt == max(0, qt - 4), kt == qt
                        vv = Vw[:, kt % 8, g, :]
                        vb = bVw[(kt // 4) % 2]
                        Ob, bOb = O_w, bbank[4]
                    for h in range(4):
                        mm(Ob[:, h, :], Ps[:, pi, h * 128:(h + 1) * 128], vv, first and h == 0, last,
                           [bPs[pi], vb], [bOb], sgc=True, inc=(h == 3))

                def combine():
                    norm_gate(1, 1, O_s, bbank[3], tt, g)
                    norm_gate(2, 2, O_w, bbank[4], tt, g)
                    slots = [0 if par == 0 else 3, 1, 2]
                    po = bank3(5, 2, 128)
                    for hp in range(2):
                        for bi, sl_ in enumerate(slots):
                            mm(po[:, hp, :], osc[:, sl_, hp * 128:(hp + 1) * 128], identb[:], bi == 0, bi == 2,
                               [bosc[sl_], bsetup], [bbank[5]], inc=(bi == 2))
                    cp_("dve", big[:, oT_off + g * 2:oT_off + g * 2 + 2, qs], po, [bbank[5]],
                        [bbig[oT_off + g * 2], bbig[oT_off + g * 2 + 1]])
                return jobs, emit_qk, emit_pv, combine

            if T > 0:
                final_norm(T - 1)
            items = [(tt, g) for tt in range(TT) for g in range(2)]
            for _ in stageA(items[0][0], items[0][1], 0):
                pass
            cur = make_item(items[0][0], items[0][1], 0)
            cur[1](0)
            if len(cur[0]) > 1:
                cur[1](1)
            for k in range(len(items)):
                jobs, emit_qk, emit_pv, combine = cur
                genA = stageA(items[k + 1][0], items[k + 1][1], (k + 1) % 2) if k + 1 < len(items) else None
                for ji in range(len(jobs)):
                    if ji + 2 < len(jobs):
                        emit_qk(ji + 2)
                    emit_pv(ji)
                    if genA is not None:
                        next(genA, None)
                if genA is not None:
                    for _ in genA:
                        pass
                    nxt = make_item(items[k + 1][0], items[k + 1][1], (k + 1) % 2)
                    nxt[1](0)
                    if len(nxt[0]) > 1:
                        nxt[1](1)
                else:
                    nxt = None
                combine()
                cur = nxt
            if DEBUG:
                for c in range(4):
                    S.dma("pool", "dbg", dbg["oT"][:, c * SEQ + t0:c * SEQ + t0 + TS], big[:, oT_off + c, :],
                          reads=[bbig[oT_off + c]])
                    S.dma("pool", "dbg", dbg["aT"][:, c * SEQ + t0:c * SEQ + t0 + TS], big[:, aT_off + c, :],
                          reads=[bbig[aT_off + c]])
            MARKS.append((T, "7merge", S.nissued.get("pe", 0)))
            for j in range(8):
                b0 = (j % 2) * 4
                s0, s1 = scr[(j % 2) * 2], scr[(j % 2) * 2 + 1]
                bs0, bs1 = bscr[(j % 2) * 2], bscr[(j % 2) * 2 + 1]
                slc, bc_ = w_next("ca")
                sla, ba_ = w_next("mgA")
                slb, bb_ = w_next("mgB")
                fm_gemm(b0, slc, 4, 128, 0, lambda kc: big[:, aT_off + kc, :], bbig[aT_off:aT_off + 4], bc_)
                for kc in range(4):
                    mm(banks[b0 + 1][:, :], slc[:, 512 + kc * 128:512 + (kc + 1) * 128], big[:, oT_off + kc, :],
                       kc == 0, kc == 3, [bc_] + bbig[oT_off:oT_off + 4], [bbank[b0 + 1]], inc=(kc == 3))
                fm_gemm(b0 + 2, sla, 8, 128, 0, lambda kc: hT[:, kc, :], [bhT], ba_)
                fm_gemm(b0 + 3, slb, 8, 128, 0, lambda kc: hT[:, kc, :], [bhT], bb_)
                act(s0[:, 0:TS], banks[b0 + 2][:, :], AF.Sigmoid, [bbank[b0 + 2]], [bs0])
                act(s1[:, 0:TS], banks[b0 + 3][:, :], AF.Sigmoid, [bbank[b0 + 3]], [bs1])
                tt_("dve", s0[:, 0:TS], banks[b0][:, :], s0[:, 0:TS], ALU.mult, [bbank[b0], bs0], [bs0])
                tt_("dve", s1[:, 0:TS], banks[b0 + 1][:, :], s1[:, 0:TS], ALU.mult, [bbank[b0 + 1], bs1], [bs1])
                tt_("dve", yT[:, j, :], s0[:, 0:TS], s1[:, 0:TS], ALU.add, [bs0, bs1], [bbig[j]])

            def tm_gemm(kind, nsl, lhs_fn, lbufs, evac):
                for half in range(2):
                    bb = (half % 2) * 4
                    for s in range(nsl):
                        sl, bsl = w_next(kind)
                        for tt in range(TT):
                            for k2 in range(2):
                                kc = 2 * s + k2
                                mm(banks[bb + tt][:, :], lhs_fn(kc, tt), sl[:, k2 * 512:(k2 + 1) * 512],
                                   kc == 0, kc == 2 * nsl - 1, [bsl] + lbufs(kc), [bbank[bb + tt]],
                                   inc=(tt == TT - 1 and k2 == 1))
                    for tt in range(TT):
                        evac(half, tt, bb + tt)

            def resid_add(half, tt, bk):
                tt_("dve", xres[:, tt, half * 512:(half + 1) * 512], xres[:, tt, half * 512:(half + 1) * 512],
                    banks[bk][:, :], ALU.add, [bx[tt], bbank[bk]], [bx[tt]])

            tm_gemm("out", 4, lambda kc, tt: yT[:, kc, tt * 128:(tt + 1) * 128], lambda kc: [bbig[kc]], resid_add)
            if DEBUG:
                for tt in range(TT):
                    S.dma("pool", "dbg", dbg["x1"][t0 + tt * 128:t0 + (tt + 1) * 128, :], xres[:, tt, :], reads=[bx[tt]])
            MARKS.append((T, "8norm", S.nissued.get("pe", 0)))
            pbv = bank_bf(2)
            for tt in range(TT):
                for kc in range(2):
                    S.op("pe", lambda e: e.transpose(out=pbv[:, (tt * 2 + kc) * 128:(tt * 2 + kc + 1) * 128],
                                                     in_=pbf[:, tt, kc * 128:(kc + 1) * 128], identity=identb[:]),
                         reads=[bpbf[tt], bsetup], writes=[bbank[2]], acc=True, inc=(tt == TT - 1 and kc == 1))
            act(pT[:].rearrange("p k (t q) -> p t k q", t=TT), pbv[:, 0:1024].rearrange("p (t k q) -> p t k q", t=TT, k=2),
                AF.Copy, [bbank[2]], [bpT])
            rmsnorm_to_hT(2)
            MARKS.append((T, "9ffn", S.nissued.get("pe", 0)))
            def ffn_up(i):
                par = i % 2
                for which, kind in enumerate(("upg", "upv")):
                    sl, bsl = w_next(kind)
                    bk = par * 4 + which
                    fm_gemm(bk, sl, 8, 128, 0, lambda kc: hT[:, kc, :], [bhT], bsl)
                    ci = which * NFF + i
                    ub = scr[par * 2 + which][:, :].bitcast(BF16); bub = bscr[par * 2 + which]
                    if which == 0:
                        for k3 in range(3):
                            ts_("dve", dgf[:, par, k3, :], identb[:], ffw[:, ci, k3:k3 + 1], None, ALU.mult, None,
                                [bsetup, bident], [bdgf[par]])
                    cp_("pool", ub[:, 0:2], uhalo[:, ci, :], [buhalo[ci]], [bub])
                    act(ub[:, 2:2 + TS], banks[bk][:, :], AF.Copy, [bbank[bk]], [bub])
                    cp_("pool", uhalo[:, ci, :], ub[:, TS:TS + 2], [bub], [buhalo[ci]])

            def ffn_conv(i):
                par = i % 2
                ubg = scr[par * 2][:, :].bitcast(BF16); bubg = bscr[par * 2]
                ubv = scr[par * 2 + 1][:, :].bitcast(BF16); bubv = bscr[par * 2 + 1]
                cb = par * 4 + 2
                for k3 in range(3):
                    mm(banks[cb][:, :], dgf[:, par, k3, :], ubg[:, k3:k3 + TS], k3 == 0, k3 == 2,
                       [bdgf[par], bubg], [bbank[cb]], inc=(k3 == 2))
                gl = scr[4 + par]; bgl = bscr[4 + par]
                act(gl[:, 0:TS], banks[cb][:, :], AF.Gelu_apprx_tanh, [bbank[cb], bsetup], [bgl],
                    bias=ffw[:, i, 3:4])
                cv = scr[6]; bcv = bscr[6]
                civ = NFF + i
                ts_("dve", cv[:, 0:TS], ubv[:, 2:2 + TS], ffw[:, civ, 2:3], ffw[:, civ, 3:4], ALU.mult, ALU.add,
                    [bubv, bsetup], [bcv])
                stt_("dve", cv[:, 0:TS], ubv[:, 1:1 + TS], ffw[:, civ, 1:2], cv[:, 0:TS], ALU.mult, ALU.add,
                     [bubv, bcv, bsetup], [bcv])
                stt_("dve", cv[:, 0:TS], ubv[:, 0:TS], ffw[:, civ, 0:1], cv[:, 0:TS], ALU.mult, ALU.add,
                     [bubv, bcv, bsetup], [bcv])
                tt_("dve", big[:, i, :], cv[:, 0:TS], gl[:, 0:TS], ALU.mult, [bcv, bgl], [bbig[i]])

            ffn_up(0)
            for i in range(NFF):
                if i + 1 < NFF:
                    ffn_up(i + 1)
                ffn_conv(i)
            MARKS.append((T, "10down", S.nissued.get("pe", 0)))
            tm_gemm("down", NFF // 2, lambda kc, tt: big[:, kc, tt * 128:(tt + 1) * 128], lambda kc: [bbig[kc]],
                    resid_add)
            if T + 1 < NT:
                for tt in range(TT):
                    ap, bufs = src_alias(tt)
                    S.dma("sp", "xa%d" % tt, ap, x_d[t0 + TS + tt * 128:t0 + TS + (tt + 1) * 128, :], writes=bufs)
            MARKS.append((T, "11ple", S.nissued.get("pe", 0)))
            rmsnorm_to_hT(0 if T + 1 < NT else 3)
            if T + 1 < NT:
                norm_stats(src_alias, 16)
            for half in range(2):
                hsl = slice(half * 512, (half + 1) * 512)
                sl, bsl = w_next("ple")
                for tt in range(TT):
                    for k2 in range(2):
                        mm(banks[tt][:, :], pT[:, k2, tt * 128:(tt + 1) * 128], sl[:, k2 * 512:(k2 + 1) * 512],
                           k2 == 0, k2 == 1, [bsl, bpT], [bbank[tt]], inc=(k2 == 1))
                for s in range(4):
                    sl, bsl = w_next("pgate")
                    for tt in range(TT):
                        for k2 in range(2):
                            kc = 2 * s + k2
                            mm(banks[4 + tt][:, :], hT[:, kc, tt * 128:(tt + 1) * 128], sl[:, k2 * 512:(k2 + 1) * 512],
                               kc == 0, kc == 7, [bsl, bhT], [bbank[4 + tt]], inc=(tt == TT - 1 and k2 == 1))
                for tt in range(TT):
                    act(ysb[:, tt, :], banks[tt][:, :], AF.Copy, [bbank[tt]], [bysb[tt]])
                for tt in range(TT):
                    sc_, bsc_ = scr[tt % 2], bscr[tt % 2]
                    act(sc_[:, 0:TS], banks[4 + tt][:, :], AF.Sigmoid, [bbank[4 + tt]], [bsc_])
                    tt_("dve", sc_[:, 0:TS], sc_[:, 0:TS], ysb[:, tt, :], ALU.mult, [bsc_, bysb[tt]], [bsc_])
                    tt_("dve", xres[:, tt, hsl], xres[:, tt, hsl], sc_[:, 0:TS], ALU.add, [bx[tt], bsc_], [bx[tt]])
            MARKS.append((T, "12final", S.nissued.get("pe", 0)))
            if T + 1 < NT:
                norm_apply(src_alias, 16)
                load_gain(3)
            if T + 1 == NT:
                final_norm(T)
        fin = list(bysb) + bscr[0:4]
        S.wait_all("sp", fin)
        S.wait_all("pool", fin)
        if DEBUG:
            dd = Buf()
            dd.w = ("dbg", S.count["dbg"])
            S.wait_all("pool", [dd])
    return nc


def kernel(x, p, rel_bias, norm_mix, w_in, conv_dw_w, conv_dw_b, conv_ln_g, conv_ln_b,
           w_conv_out, cmp_pe_k, cmp_pe_v, w_ck1, w_ck2, w_cv1, w_cv2, w_attn_out, w_out,
           norm_ffn, w_up, ffn_dw_w, ffn_dw_b, w_down, norm_ple, w_ple_gate, w_ple, norm_final):
    f = lambda a: np.ascontiguousarray(np.asarray(a, dtype=np.float32))
    x = f(x); p = f(p)
    wflat = build_wflat(f(w_in)[0], f(w_conv_out)[0], f(w_ck1)[0], f(w_ck2)[0], f(w_cv1)[0], f(w_cv2)[0],
                        f(w_attn_out)[0], f(w_out)[0], f(w_up)[0], f(w_down)[0], f(w_ple_gate)[0], f(w_ple)[0])
    consts = build_consts()
    gains = np.stack([f(norm_mix)[0], f(norm_ffn)[0], f(norm_ple)[0], f(norm_final)], axis=0)
    cvw = f(conv_dw_w)[0].T.reshape(4, 128, 31).transpose(1, 0, 2).reshape(128, 124)
    cvp = np.stack([f(conv_dw_b)[0], f(conv_ln_g)[0], f(conv_ln_b)[0]], axis=1).reshape(4, 128, 3)
    cvp = cvp.transpose(1, 0, 2).reshape(128, 12)
    ffw = np.concatenate([f(ffn_dw_w)[0], f(ffn_dw_b)[0][None, :]], axis=0).T.reshape(44, 128, 4)
    ffw = ffw.transpose(1, 0, 2).reshape(128, 176)
    pet = np.concatenate([f(cmp_pe_k)[0].T, f(cmp_pe_v)[0].T], axis=1)
    shared = {
        "wflat": wflat, "gains": f(gains), "relb": f(rel_bias), "cvw": f(cvw), "cvp": f(cvp), "ffw": f(ffw),
        "pet": f(pet),
    }
    for k, v in consts.items():
        shared[k] = f(v)
    nc = build_program()
    in_maps = []
    for b in range(8):
        m = dict(shared)
        m["x"] = x[b]
        m["p"] = p[0, b]
        in_maps.append(m)
    res = run_bass_kernel_spmd(nc, in_maps, core_ids=list(range(8)))
    kernel.last_results = res
    return np.stack([np.asarray(r["y"], dtype=np.float32) for r in res.results], axis=0)
```

```python
import math
import numpy as np
import concourse.bass as bass
import concourse.mybir as mybir
from concourse.bass_utils import run_bass_kernel_spmd
from contextlib import ExitStack

F32 = mybir.dt.float32
BF16 = mybir.dt.bfloat16
AF = mybir.ActivationFunctionType
ALU = mybir.AluOpType

D = 1024
SEQ = 4096
TS = 512
NT = SEQ // TS
TT = TS // 128
D_FF = 2816
NFF = D_FF // 128
NEGM = -240.0
EPS = 1e-6
SLABW = 1024
NSLOT = 12
PREF = 7
NCAST = 24
DEBUG = False
MARKS = []


class Buf:
    __slots__ = ("name", "w", "r")

    def __init__(self, name=""):
        self.name = name
        self.w = None
        self.r = {}


class Sync:
    def __init__(self, nc, es):
        self.nc = nc
        self.es = es
        self.sems = {}
        self.count = {}
        self.waited = {}
        self.nissued = {}
        self.engs = {"pe": nc.tensor, "act": nc.scalar, "dve": nc.vector,
                     "pool": nc.gpsimd, "sp": nc.sync}
        for k in self.engs:
            self.new_sem(k)
            self.waited[k] = {}

    def new_sem(self, key):
        h = self.es.enter_context(self.nc.semaphore("s_" + key))
        self.sems[key] = h
        self.count[key] = 0
        return key

    def _wait(self, ename, deps):
        e = self.engs[ename]
        wd = self.waited[ename]
        for key, val in deps.items():
            assert val <= self.count[key], ("wait on unissued increment", ename, key, val, self.count[key])
            if wd.get(key, 0) < val:
                e.wait_ge(self.sems[key], val)
                wd[key] = val

    @staticmethod
    def _deps(reads, writes, selfkey=None, acc=False):
        deps = {}

        def add(k, v):
            if deps.get(k, 0) < v:
                deps[k] = v
        for b in reads:
            if b.w is not None:
                add(*b.w)
        for b in writes:
            if b.w is not None and not (acc and b.w[0] == selfkey):
                add(*b.w)
            for k, v in b.r.items():
                if acc and k == selfkey:
                    continue
                add(k, v)
        return deps

    def op(self, ename, fn, reads=(), writes=(), acc=False, inc=True):
        deps = self._deps(reads, writes, ename, acc)
        self._wait(ename, deps)
        ins = fn(self.engs[ename])
        self.nissued[ename] = self.nissued.get(ename, 0) + 1
        if inc:
            self.count[ename] += 1
            v = self.count[ename]
            ins.then_inc(self.sems[ename], 1)
        else:
            v = self.count[ename] + 1
        for b in reads:
            if b.r.get(ename, 0) < v:
                b.r[ename] = v
        for b in writes:
            b.w = (ename, v)
            b.r = {}
        return ins

    def dma(self, qname, semkey, out, in_, reads=(), writes=(), **kw):
        deps = self._deps(reads, writes)
        self._wait(qname, deps)
        ins = self.engs[qname].dma_start(out=out, in_=in_, **kw)
        self.count[semkey] += 16
        v = self.count[semkey]
        ins.then_inc(self.sems[semkey], 16)
        for b in reads:
            if b.r.get(semkey, 0) < v:
                b.r[semkey] = v
        for b in writes:
            b.w = (semkey, v)
            b.r = {}
        return ins

    def wait_all(self, ename, bufs):
        deps = {}
        for b in bufs:
            items = list(b.r.items())
            if b.w is not None:
                items.append(b.w)
            for k, v in items:
                if deps.get(k, 0) < v:
                    deps[k] = v
        self._wait(ename, deps)


def slab_list():
    L = []
    for j in range(4):
        L.append(("u", j))
        L.append(("u", j + 4))
    for hp in range(4):
        L.append(("q", hp))
    for idx in (0, 1, 2, 4):
        L.append(("kv", idx))
    for idx in (3, 5):
        L.append(("vtok", idx))
    L.append(("ng", 0))
    for kv in range(2):
        for s in range(8):
            L.append(("w1", kv, s))
    L.append(("w2", 0))
    for j in range(8):
        L.append(("ca", j))
        L.append(("mgA", j))
        L.append(("mgB", j))
    for half in range(2):
        for s in range(4):
            L.append(("out", half, s))
    for i in range(NFF):
        L.append(("upg", i))
        L.append(("upv", i))
    for half in range(2):
        for s in range(NFF // 2):
            L.append(("down", half, s))
    for half in range(2):
        L.append(("ple", half))
        for s in range(4):
            L.append(("pgate", half, s))
    return L


SLABS = slab_list()
NSLAB = len(SLABS)
assert NSLAB % NCAST == 0, NSLAB


def _pack_fm(W, c0, ncols, KC):
    out = np.zeros((128, SLABW), np.float32)
    blk = W[:KC * 128, c0:c0 + ncols].reshape(KC, 128, ncols).transpose(1, 0, 2)
    out[:, :KC * ncols] = blk.reshape(128, KC * ncols)
    return out


def _pack_tm(W, half, s):
    out = np.zeros((128, SLABW), np.float32)
    blk = W[s * 256:(s + 1) * 256, half * 512:(half + 1) * 512].reshape(2, 128, 512).transpose(1, 0, 2)
    out[:, :] = blk.reshape(128, 1024)
    return out


def build_wflat(w_in, w_conv_out, w_ck1, w_ck2, w_cv1, w_cv2, w_attn_out, w_out, w_up, w_down,
                w_ple_gate, w_ple):
    wf = np.zeros((128, NSLAB * SLABW), np.float32)
    for i, sl in enumerate(SLABS):
        k = sl[0]
        if k == "u":
            a = _pack_fm(w_in, sl[1] * 128, 128, 8)
        elif k == "q":
            a = _pack_fm(w_in, 1024 + sl[1] * 128, 128, 8)
        elif k in ("kv", "vtok"):
            a = _pack_fm(w_in, 1536 + sl[1] * 128, 128, 8)
        elif k == "ng":
            a = _pack_fm(w_in, 2304, 24, 8)
        elif k == "w1":
            w1 = w_ck1 if sl[1] == 0 else w_cv1
            a = np.zeros((128, SLABW), np.float32)
            blk = w1[sl[2] * 256:(sl[2] + 1) * 256, :].reshape(4, 64, 256).transpose(1, 0, 2)
            a[:64, :] = blk.reshape(64, 1024)
        elif k == "w2":
            a = np.zeros((128, SLABW), np.float32)
            for kv, w2 in enumerate((w_ck2, w_cv2)):
                for hc in range(2):
                    a[:, (kv * 2 + hc) * 64:(kv * 2 + hc + 1) * 64] = w2[hc * 128:(hc + 1) * 128, :]
        elif k == "ca":
            a = np.zeros((128, SLABW), np.float32)
            a[:, 0:512] = _pack_fm(w_conv_out, sl[1] * 128, 128, 4)[:, 0:512]
            a[:, 512:1024] = _pack_fm(w_attn_out, sl[1] * 128, 128, 4)[:, 0:512]
        elif k == "mgA":
            a = _pack_fm(w_in, 2328 + sl[1] * 128, 128, 8)
        elif k == "mgB":
            a = _pack_fm(w_in, 2328 + 1024 + sl[1] * 128, 128, 8)
        elif k == "out":
            a = _pack_tm(w_out, sl[1], sl[2])
        elif k == "upg":
            a = _pack_fm(w_up, sl[1] * 128, 128, 8)
        elif k == "upv":
            a = _pack_fm(w_up, D_FF + sl[1] * 128, 128, 8)
        elif k == "down":
            a = _pack_tm(w_down, sl[1], sl[2])
        elif k == "ple":
            a = _pack_tm(w_ple, sl[1], 0)
        elif k == "pgate":
            a = _pack_tm(w_ple_gate, sl[1], sl[2])
        else:
            raise ValueError(k)
        wf[:, i * SLABW:(i + 1) * SLABW] = a
    return wf


def t5_bucket_np(n):
    n = np.maximum(n, 0)
    nf = np.maximum(n, 16).astype(np.float32)
    large = 16 + (np.log(nf / np.float32(16)) / np.float32(math.log(128 / 16)) * np.float32(16)).astype(np.int32)
    large = np.minimum(large, 31)
    return np.where(n < 16, n, large)


def build_consts():
    c = {}
    n = np.arange(-256, 256)
    oh = np.zeros((33, 512), np.float32)
    bk = t5_bucket_np(n)
    for i, nn in enumerate(n):
        if nn >= 0:
            oh[bk[i], i] = 1.0
        else:
            oh[32, i] = 1.0
    c["oh"] = oh
    k = np.arange(4096)
    c["efull"] = (k[None, :] // 64 == np.arange(64)[:, None]).astype(np.float32)
    x = np.arange(504)
    j = x - 248
    sm = np.zeros((17, 504), np.float32)
    for r in range(16):
        sm[r, j == r - 9] = 1.0
    sm[16, j >= 7] = 1.0
    c["selm"] = sm
    cc = np.arange(256)
    cs, ce = cc * 16, cc * 16 + 31
    ss = np.arange(64) * 64
    agg = ((ce[:, None] >= ss[None, :]) & (cs[:, None] <= ss[None, :] + 63)).astype(np.float32)
    agg[255, :] = 0.0
    aa = np.zeros((128, 2, 65), np.float32)
    aa[:, :, :64] = agg.reshape(2, 128, 64).transpose(1, 0, 2)
    aa[:, :, 64] = 1.0
    c["aggaug"] = aa.reshape(128, 130)
    p = np.arange(128)
    c["w4"] = np.where(p[:, None] > p[None, :], 0.0, NEGM).astype(np.float32)
    i = np.arange(128)[:, None]
    ci = (i >= 64).astype(np.int64)
    rel = (np.arange(126) - 62)[None, :]
    vm = (rel < ci - 1).astype(np.float32)
    am = np.zeros((128, 126), np.float32)
    am = np.where(rel == ci, 2e4, am)
    am = np.where(rel == ci - 1, 3e4, am)
    am = np.where(rel > ci, -1e4 - 8.0 * np.arange(126)[None, :], am)
    c["vm"] = vm.astype(np.float32)
    c["am"] = am.astype(np.float32)
    c["ident"] = np.eye(128, dtype=np.float32)
    return c


def build_program():
    nc = bass.Bass("TRN2", target_bir_lowering=False)

    def din(name, shape, dt=F32):
        return nc.dram_tensor(name, list(shape), dt, kind="ExternalInput").ap()

    x_d = din("x", [SEQ, D])
    p_d = din("p", [SEQ, 256])
    wflat_d = din("wflat", [128, NSLAB * SLABW])
    gains_d = din("gains", [4, D])
    relb_d = din("relb", [32, 8])
    cvw_d = din("cvw", [128, 4 * 31])
    cvp_d = din("cvp", [128, 12])
    ffw_d = din("ffw", [128, 44 * 4])
    pet_d = din("pet", [64, 64])
    oh_d = din("oh", [33, 512])
    efull_d = din("efull", [64, 4096])
    selm_d = din("selm", [17, 504])
    aggaug_d = din("aggaug", [128, 130])
    w4_d = din("w4", [128, 128])
    vm_d = din("vm", [128, 126])
    am_d = din("am", [128, 126])
    ident_d = din("ident", [128, 128])
    y_d = nc.dram_tensor("y", [SEQ, D], F32, kind="ExternalOutput").ap()
    wb_d = nc.dram_tensor("wb", [128, NSLAB * SLABW], BF16, kind="Internal").ap()
    tv_d = nc.dram_tensor("tvd", [8, 128 * 512], F32, kind="Internal").ap()
    dbg = {}
    if DEBUG:
        dbg["x1"] = nc.dram_tensor("dbg_x1", [SEQ, D], F32, kind="ExternalOutput").ap()
        dbg["oT"] = nc.dram_tensor("dbg_oT", [128, 4 * SEQ], BF16, kind="ExternalOutput").ap()
        dbg["aT"] = nc.dram_tensor("dbg_aT", [128, 4 * SEQ], BF16, kind="ExternalOutput").ap()

    with ExitStack() as es:
        S = Sync(nc, es)
        used = [0]

        def sb(name, shape, dt):
            n = 1
            for s_ in shape[1:]:
                n *= s_
            used[0] += n * (4 if dt == F32 else 2)
            return es.enter_context(nc.sbuf_tensor("sb_" + name, list(shape), dt))

        banks = [es.enter_context(nc.psum_tensor("bank%d" % i, [128, 512], F32)) for i in range(8)]
        bbank = [Buf("bank%d" % i) for i in range(8)]

        for k in range(NSLOT):
            S.new_sem("ws%d" % k)
            S.new_sem("wbk%d" % k)
        for k in range(NCAST):
            S.new_sem("cast%d" % k)
        for k in ("xl0", "xl1", "xl2", "xl3", "xa0", "xa1", "xa2", "xa3", "pl0", "pl1", "pl2", "pl3", "gl", "ld", "ld1", "ld2", "ld3", "ld4", "ld5", "ld6", "ld7", "ld8", "ld9", "ld10", "ld11", "ld12", "ld13", "st0", "st1", "st2", "st3", "st4", "st5", "st6", "st7", "vc",
                  "tvs", "dbg"):
            S.new_sem(k)

        xres = sb("xres", [128, TT, D], F32)
        bx = [Buf("xres%d" % t) for t in range(TT)]
        gbc = sb("gbc", [128, D], F32); bgbc = Buf("gbc")
        hT = sb("hT", [128, 8, TS], BF16); bhT = Buf("hT")
        hn = sb("hn", [128, 2, D], BF16); bhn = [Buf("hn0"), Buf("hn1")]
        abuf = sb("abuf", [128, 4, 30 + TS], BF16); babuf = [Buf("abuf%d" % j) for j in range(4)]
        ysb = sb("ysb", [128, 4, TS], F32); bysb = [Buf("ysb%d" % j) for j in range(4)]
        big = sb("big", [128, NFF, TS], BF16); bbig = [Buf("big%d" % j) for j in range(NFF)]
        Q = sb("Q", [128, 8, TS], BF16)
        bQlo = [[Buf("Qlo%d_%d" % (g, t)) for t in range(TT)] for g in range(2)]
        bQhi = [[Buf("Qhi%d_%d" % (g, t)) for t in range(TT)] for g in range(2)]
        KsT = [sb("KsT%d" % g, [128, SEQ], BF16) for g in range(2)]
        bKs = [[Buf("Ks%d_%d" % (g, t)) for t in range(NT)] for g in range(2)]
        KwT = [sb("KwT%d" % g, [65, 1024], BF16) for g in range(2)]
        bKw = [[Buf("Kw%d_%d" % (g, t)) for t in range(2)] for g in range(2)]
        Vs = sb("Vs", [128, 32, 2, 65], BF16); bVs = [Buf("Vs%d" % t) for t in range(NT)]
        Vw = sb("Vw", [128, 8, 2, 65], BF16); bVw = [Buf("Vw%d" % t) for t in range(2)]
        kcT = [[sb("kcT%d_%d" % (kv, g), [64, 16 + TS], BF16) for g in range(2)] for kv in range(2)]
        bkcT = [[Buf() for g in range(2)] for kv in range(2)]
        kcmpT = [sb("kcmpT%d" % g, [65, 256], BF16) for g in range(2)]
        bkcmp = [Buf() for g in range(2)]
        vcmp = sb("vcmp", [128, 2, 2, 65], BF16); bvcmp = Buf("vcmp")
        vst = sb("vst", [32, 2, 65], BF16); bvst = Buf("vst")
        hid = sb("hid", [128, 2, 2, 2, 32], BF16); bhid = [Buf(), Buf()]
        cconst = sb("cconst", [128, 2, 2], F32); bcconst = Buf("cconst")
        dg = sb("dg", [128, 4, 31, 128], BF16); bdgc = [Buf("dg%d" % c) for c in range(4)]
        gates = sb("gates", [128, TT, 24], F32); bgates = [Buf() for t in range(TT)]
        Pc = sb("Pc", [128, 2, 512], BF16); bPc = [Buf(), Buf()]
        NPS = 4
        Ps = sb("Ps", [128, NPS, 512], BF16); bPs = [Buf() for _ in range(NPS)]
        osc = sb("osc", [128, 4, 256], BF16); bosc = [Buf() for _ in range(4)]
        uhalo = sb("uhalo", [128, 2 * NFF, 2], BF16); buhalo = [Buf() for _ in range(2 * NFF)]
        dgf = sb("dgf", [128, 2, 6, 128], BF16); bdgf = [Buf(), Buf()]
        pT = sb("pT", [128, 2, TS], BF16); bpT = Buf("pT")
        pbf = sb("pbf", [128, TT, 256], BF16); bpbf = [Buf("pbf%d" % t) for t in range(TT)]
        biasD = sb("biasD", [128, 8, 128], BF16)
        biasO = sb("biasO", [128, 8, 128], BF16)
        mstack = sb("mstack", [17, 8, 128], BF16)
        btab = Buf("tables"); btab2 = Buf("tables2")
        selm = sb("selm", [17, 504], BF16)
        aggaug = sb("aggaug", [128, 2, 65], BF16)
        w4 = sb("w4", [128, 128], BF16)
        identb = sb("identb", [128, 128], BF16)
        onesf = sb("onesf", [128, 128], F32)
        onesb = sb("onesb", [128, 128], BF16)
        vm = sb("vm", [128, 126], F32)
        am = sb("am", [128, 126], F32)
        cvp = sb("cvp", [128, 12], F32)
        ffw = sb("ffw", [128, 44, 4], F32)
        peT = sb("peT", [64, 64], BF16)
        b31bc = sb("b31bc", [128, 8], F32)
        small = sb("small", [128, 64], F32); bsmall = Buf("small")
        tk = sb("tk", [128, 4, 64], F32); btk = Buf("tk")
        m8 = sb("m8", [128, 16], F32)
        selpad = sb("selpad", [128, 128], BF16); bselpad = Buf("selpad")
        f4 = sb("f4", [128, 4, 8], F32); bf4 = [Buf() for _ in range(4)]
        NSCR = 7
        scr = [sb("scr%d" % i, [128, 516], F32) for i in range(NSCR)]
        bscr = [Buf("scr%d" % i) for i in range(NSCR)]
        wring = sb("wring", [128, NSLOT, SLABW], BF16)
        bslot = [Buf("slot%d" % k) for k in range(NSLOT)]
        stage = scr[5]; bstage = bscr[5]
        print("SBUF bytes/partition used:", used[0])

        yT = big
        aT_off = 8
        oT_off = 12

        bcast = [Buf("cast%d" % k) for k in range(NCAST)]
        per_cast = NSLAB // NCAST

        wstate = {"issued": 0, "next": 0}
        total_slabs = NSLAB * NT

        bwbk = [Buf("wbk%d" % i) for i in range(NSLAB)]

        def w_issue(upto):
            while wstate["issued"] < min(upto, total_slabs):
                gi = wstate["issued"]
                i = gi % NSLAB
                k = gi % NSLOT
                if gi < NSLAB or (gi < 2 * NSLAB and i % 2 == 1):
                    S.dma("pool", "ws%d" % k, wring[:, k, :], wflat_d[:, i * SLABW:(i + 1) * SLABW], writes=[bslot[k]])
                    if gi >= NSLAB or i % 2 == 0:
                        S.dma("sp", "wbk%d" % k, wb_d[:, i * SLABW:(i + 1) * SLABW], wring[:, k, :],
                              reads=[bslot[k]], writes=[bwbk[i]])
                else:
                    S.dma("sp", "ws%d" % k, wring[:, k, :], wb_d[:, i * SLABW:(i + 1) * SLABW],
                          reads=[bwbk[i]], writes=[bslot[k]])
                wstate["issued"] += 1

        def w_next(kind):
            gi = wstate["next"]
            assert SLABS[gi % NSLAB][0] == kind, (SLABS[gi % NSLAB], kind)
            w_issue(gi + 1 + PREF)
            wstate["next"] += 1
            k = gi % NSLOT
            return wring[:, k, :], bslot[k]

        def mm(out, lhsT, rhs, start, stop, reads, writes, sgc=False, inc=True):
            S.op("pe", lambda e: e.matmul(out, lhsT=lhsT, rhs=rhs, start=start, stop=stop, skip_group_check=sgc),
                 reads=reads, writes=writes, acc=True, inc=inc)

        def act(out, in_, func, reads, writes, bias=None, scale=None, accum_out=None):
            kw = {}
            if bias is not None:
                kw["bias"] = bias
            if scale is not None:
                kw["scale"] = scale
            if accum_out is not None:
                kw["accum_out"] = accum_out
            S.op("act", lambda e: e.activation(out=out, in_=in_, func=func, **kw), reads=reads, writes=writes)

        def tt_(eng, out, in0, in1, op, reads, writes):
            S.op(eng, lambda e: e.tensor_tensor(out=out, in0=in0, in1=in1, op=op), reads=reads, writes=writes)

        def ts_(eng, out, in0, s1, s2, op0, op1, reads, writes):
            if op1 is None:
                S.op(eng, lambda e: e.tensor_scalar(out=out, in0=in0, scalar1=s1, scalar2=None, op0=op0),
                     reads=reads, writes=writes)
            else:
                S.op(eng, lambda e: e.tensor_scalar(out=out, in0=in0, scalar1=s1, scalar2=s2, op0=op0, op1=op1),
                     reads=reads, writes=writes)

        def stt_(eng, out, in0, scalar, in1, op0, op1, reads, writes):
            S.op(eng, lambda e: e.scalar_tensor_tensor(out=out, in0=in0, scalar=scalar, in1=in1, op0=op0, op1=op1),
                 reads=reads, writes=writes)

        def cp_(eng, out, in_, reads, writes):
            S.op(eng, lambda e: e.tensor_copy(out=out, in_=in_), reads=reads, writes=writes)

        def bank3(b, a, n):
            return banks[b][:, 0:a * n].rearrange("p (a n) -> p a n", a=a)

        def bank_bf(b):
            return banks[b][:, :].bitcast(BF16)

        bsetup = Buf("setup")

        def ld(out, in_, writes=(bsetup,), sem="ld", **kw):
            S.dma("pool", sem, out, in_, writes=list(writes), **kw)

        for tt in range(TT):
            S.dma("sp", "xl%d" % tt, xres[:, tt, :], x_d[tt * 128:(tt + 1) * 128, :], writes=[bx[tt]])
        S.dma("sp", "gl", gbc[:], gains_d[0:1, :].partition_broadcast(128), writes=[bgbc])
        cvw = tk[:].rearrange("p a b -> p (a b)")
        ld(cvw[:, 0:124], cvw_d, writes=[btk], sem="ld1")
        rbx = scr[1]; ohx = scr[2]
        ld(rbx[0:32, 0:8], relb_d, writes=[bscr[1]], sem="ld2")
        ld(rbx[0:32, 8:16], relb_d[31:32, :].partition_broadcast(32), writes=[bscr[1]], sem="ld3")
        ld(ohx[0:33, 0:512], oh_d, writes=[bscr[2]], sem="ld4")
        S.dma("pool", "ld5", stage[64:65, 0:8], relb_d[31:32, :], writes=[bstage])
        bident = Buf("ident")
        ld(identb[:], ident_d, writes=[bident], sem="ld7")
        ld(cvp[:], cvp_d)
        ld(vm[:], vm_d)
        ld(am[:], am_d)
        ld(ffw[:].rearrange("p a b -> p (a b)"), ffw_d)
        ld(b31bc[:], relb_d[31:32, :].partition_broadcast(128))
        ld(selm[:], selm_d)
        ld(aggaug[:].rearrange("p a b -> p (a b)"), aggaug_d)
        ld(w4[:], w4_d)
        ld(peT[:], pet_d)
        bsetup.w = ("ld", S.count["ld"])
        w_issue(1 + PREF)
        befull = [Buf("efull0"), Buf("efull1")]
        for g in range(2):
            ld(KsT[g][64:128, :], efull_d, writes=[befull[g]] + bKs[g], sem="ld%d" % (12 + g))

        def cast_chunk(k):
            c0, c1 = k * per_cast * SLABW, (k + 1) * per_cast * SLABW
            S.dma("pool", "cast%d" % k, wb_d[:, c0:c1], wflat_d[:, c0:c1], writes=[bcast[k]])
        bones = Buf("ones")
        S.op("dve", lambda e: e.memset(onesf[:], 1.0 / 512.0), writes=[bones])
        S.op("dve", lambda e: e.memset(onesb[:], 1.0 / 512.0), writes=[bones])
        for g in range(2):
            S.op("dve", lambda e: e.memset(kcmpT[g][0:64, :], 0.0), writes=[bkcmp[g]])
            S.op("dve", lambda e: e.memset(kcmpT[g][64:65, :], 1.0), writes=[bkcmp[g]])
            S.op("dve", lambda e: e.memset(KwT[g][64:65, :], 1.0), writes=[bKw[g][0], bKw[g][1]])
            for kv in range(2):
                S.op("dve", lambda e: e.memset(kcT[kv][g][:, 0:16], 0.0), writes=[bkcT[kv][g]])
        S.op("dve", lambda e: e.memset(vcmp[:].rearrange("p a b c -> p (a b c)"), 0.0), writes=[bvcmp])
        S.op("dve", lambda e: e.memset(Vs[:, :, :, 64:65].rearrange("p a b c -> p (a b c)"), 1.0), writes=bVs)
        S.op("dve", lambda e: e.memset(Vw[:, :, :, 64:65].rearrange("p a b c -> p (a b c)"), 1.0), writes=bVw)
        S.op("dve", lambda e: e.memset(vst[:, :, 64:65].rearrange("p a b -> p (a b)"), 1.0), writes=[bvst])
        S.op("dve", lambda e: e.memset(abuf[:].rearrange("p a b -> p (a b)"), 0.0), writes=babuf)
        S.op("dve", lambda e: e.memset(uhalo[:].rearrange("p a b -> p (a b)"), 0.0), writes=buhalo)
        S.op("dve", lambda e: e.memset(selpad[:], 0.0), writes=[bselpad])
        allQ = [b for g in range(2) for b in bQhi[g]]
        cp_("dve", Q[64:65, :, :], stage[64:65, 0:8].unsqueeze(2).to_broadcast([1, 8, TS]), [bstage], allQ)
        tt_("dve", rbx[0:32, 0:8], rbx[0:32, 0:8], rbx[0:32, 8:16], ALU.subtract, [bscr[1]], [bscr[1]])
        S.op("dve", lambda e: e.memset(rbx[32:33, 0:8], NEGM), reads=[bscr[1]], writes=[bscr[1]])
        mm(banks[0][0:8, :], rbx[0:33, 0:8], ohx[0:33, 0:512], True, True, [bscr[1], bscr[2]], [bbank[0]])
        tvs = scr[3]
        act(tvs[0:8, 0:512], banks[0][0:8, :], AF.Copy, [bbank[0]], [bscr[3]])
        btv = Buf("tvd")
        S.dma("pool", "tvs", tv_d.rearrange("h (r n) -> h r n", r=128),
              tvs[0:8, 0:512].unsqueeze(1).to_broadcast([8, 128, 512]), reads=[bscr[3]], writes=[btv])

        def load_bias_tables():
            tsem = iter(("ld6", "ld9", "ld10", "ld11"))

            def toeplitz3(dst_ap, rows, off, pstep, h0, nh, wbuf):
                src = bass.AP(tensor=tv_d.tensor, offset=h0 * 128 * 512 + off, ap=[[pstep, rows], [128 * 512, nh], [1, 128]])
                S.dma("sp", next(tsem), dst_ap, src, reads=[btv], writes=list(wbuf))
            ystg = ysb[:].rearrange("p a b -> p (a b)")
            toeplitz3(ystg[:, 0:1024].rearrange("p (h i) -> p h i", h=8), 128, 256, 511, 0, 8, [bysb[0], bysb[1]])
            cp_("dve", biasD[:].rearrange("p h i -> p (h i)"), ystg[:, 0:1024], [bysb[0], bysb[1]], [btab])
            toeplitz3(ystg[:, 1024:2048].rearrange("p (h i) -> p h i", h=8), 128, 256 + 128, 511, 0, 8, [bysb[2], bysb[3]])
            cp_("dve", biasO[:].rearrange("p h i -> p (h i)"), ystg[:, 1024:2048], [bysb[2], bysb[3]], [btab])
            for hh in range(2):
                stg = scr[0] if hh == 0 else scr[4]
                bst = bscr[0] if hh == 0 else bscr[4]
                toeplitz3(stg[0:16, 0:512].rearrange("p (h i) -> p h i", h=4), 16, 256 + 113, 496, hh * 4, 4, [bst])
                cp_("dve", mstack[0:16, hh * 4:hh * 4 + 4, :].rearrange("p h i -> p (h i)"), stg[0:16, 0:512], [bst], [btab])
        S.op("dve", lambda e: e.memset(stage[0:32, 0:128], NEGM), reads=[bstage], writes=[bstage])
        for h in range(8):
            S.dma("pool", "ld8", mstack[16:17, h, :], stage[0:1, 0:128], reads=[bstage], writes=[btab2])
        btab2.w = ("ld8", S.count["ld8"])

        def gen_dg(c):
            for j in range(31):
                last = j == 30
                S.op("dve", lambda e: e.tensor_scalar(out=dg[:, c, j, :], in0=identb[:],
                                                      scalar1=cvw[:, c * 31 + j:c * 31 + j + 1], scalar2=None,
                                                      op0=ALU.mult),
                     reads=[btk, bsetup], writes=[bdgc[c]] if last else [], inc=last)

        junk = scr[6][:, :].bitcast(BF16)

        xal4 = big[:].rearrange("p a b -> p (a b)").bitcast(F32)

        def src_res(tt):
            return xres[:, tt, :], [bx[tt]]

        def src_alias(tt):
            return xal4[:, tt * D:(tt + 1) * D], bbig[4 * tt:4 * tt + 4]

        def norm_stats(src, c0=0):
            for tt in range(TT):
                ap, bufs = src(tt)
                act(junk[:, 0:D], ap, AF.Square, bufs, [bscr[6], bsmall], accum_out=small[:, c0 + tt:c0 + tt + 1])
            act(small[:, c0 + 4:c0 + 8], small[:, c0:c0 + 4], AF.Sqrt, [bsmall], [bsmall], bias=EPS, scale=1.0 / D)
            S.op("dve", lambda e: e.reciprocal(out=small[:, c0 + 8:c0 + 12], in_=small[:, c0 + 4:c0 + 8]),
                 reads=[bsmall], writes=[bsmall])

        def rmsnorm_stats_all():
            norm_stats(src_res, 0)

        def load_gain(gi):
            S.dma("sp", "gl", gbc[:], gains_d[gi:gi + 1, :].partition_broadcast(128), writes=[bgbc])

        def norm_apply(src, c0=0):
            for tt in range(TT):
                ap, bufs = src(tt)
                stt_("dve", hn[:, tt % 2, :], ap, small[:, c0 + 8 + tt:c0 + 9 + tt], gbc[:], ALU.mult, ALU.mult,
                     bufs + [bsmall, bgbc], [bhn[tt % 2]])
                pb = tt % 2
                pbv = bank_bf(pb)
                for kc in range(8):
                    S.op("pe", lambda e: e.transpose(out=pbv[:, kc * 128:(kc + 1) * 128],
                                                     in_=hn[:, tt % 2, kc * 128:(kc + 1) * 128],
                                                     identity=identb[:]),
                         reads=[bhn[tt % 2], bident], writes=[bbank[pb]], acc=True, inc=(kc == 7))
                dst = hT[:, :, tt * 128:(tt + 1) * 128]
                srcv = pbv[:, 0:1024].rearrange("p (a b) -> p a b", a=8)
                if tt % 2 == 0:
                    act(dst, srcv, AF.Copy, [bbank[pb]], [bhT])
                else:
                    cp_("dve", dst, srcv, [bbank[pb]], [bhT])

        def rmsnorm_to_hT(next_gain):
            norm_stats(src_res, 0)
            norm_apply(src_res, 0)
            if next_gain is not None:
                load_gain(next_gain)

        def fm_gemm(bank, slab, KC, ncols, coff, rhs_fn, rbufs, bsl, M=128, poff=0):
            for kc in range(KC):
                mm(banks[bank][poff:poff + M, :], slab[:, kc * ncols + coff:kc * ncols + coff + M], rhs_fn(kc),
                   kc == 0, kc == KC - 1, [bsl] + rbufs, [bbank[bank]], inc=(kc == KC - 1))

        def final_norm(Tp):
            tp0 = Tp * TS
            rmsnorm_stats_all()
            for tt in range(TT):
                for half in range(2):
                    k = tt * 2 + half
                    if k < 4:
                        ob, bob = ysb[:, k, :], bysb[k]
                    else:
                        ob, bob = scr[k - 4][:, 0:TS], bscr[k - 4]
                    hsl = slice(half * 512, (half + 1) * 512)
                    stt_("dve", ob, xres[:, tt, hsl], small[:, 8 + tt:9 + tt], gbc[:, hsl], ALU.mult, ALU.mult,
                         [bx[tt], bsmall, bgbc], [bob])
                    S.dma("pool", "st%d" % k, y_d[tp0 + tt * 128:tp0 + (tt + 1) * 128, hsl], ob, reads=[bob])
                if Tp + 1 < NT:
                    S.dma("sp", "xl%d" % tt, xres[:, tt, :], x_d[tp0 + TS + tt * 128:tp0 + TS + (tt + 1) * 128, :],
                          writes=[bx[tt]])
            if Tp + 1 < NT:
                load_gain(1)

        for T in range(NT):
            t0 = T * TS
            MARKS.append((T, "1norm", S.nissued.get("pe", 0)))
            if T == 0:
                rmsnorm_to_hT(1)
            for tt in range(TT):
                S.dma("pool", "pl%d" % tt, pbf[:, tt, :], p_d[t0 + tt * 128:t0 + (tt + 1) * 128, :], writes=[bpbf[tt]])
            MARKS.append((T, "3conv", S.nissued.get("pe", 0)))
            def u_gemm(j):
                bs = (j % 2) * 2
                slA, bA = w_next("u")
                fm_gemm(bs, slA, 8, 128, 0, lambda kc: hT[:, kc, :], [bhT], bA)
                slB, bB = w_next("u")
                fm_gemm(bs + 1, slB, 8, 128, 0, lambda kc: hT[:, kc, :], [bhT], bB)
                act(scr[j % 2][:, 0:TS], banks[bs + 1][:, :], AF.Sigmoid, [bbank[bs + 1]], [bscr[j % 2]])
                tt_("dve", abuf[:, j, 30:30 + TS], banks[bs][:, :], scr[j % 2][:, 0:TS], ALU.mult,
                    [bbank[bs], bscr[j % 2]], [babuf[j]])

            def conv_mm(j):
                bk = 4 + (j % 2)
                if T == 0:
                    gen_dg(j)
                for jj in range(31):
                    mm(banks[bk][:, :], dg[:, j, jj, :], abuf[:, j, jj:jj + TS], jj == 0, jj == 30,
                       [bdgc[j], babuf[j]], [bbank[bk]], inc=(jj == 30))
                act(ysb[:, j, :], banks[bk][:, :], AF.Identity, [bbank[bk], bsetup], [bysb[j]],
                    bias=cvp[:, j * 3:j * 3 + 1])
                act(scr[2 + j][:, :].bitcast(BF16)[:, 0:TS], ysb[:, j, :], AF.Square, [bysb[j]], [bscr[2 + j]])
                cp_("pool", abuf[:, j, 0:30], abuf[:, j, TS:TS + 30], [babuf[j]], [babuf[j]])

            u_gemm(0)
            for j in range(4):
                if j + 1 < 4:
                    u_gemm(j + 1)
                conv_mm(j)
            for j in range(4):
                mm(banks[6][:, :], onesf[:], ysb[:, j, :], j == 0, j == 3, [bones, bysb[j]], [bbank[6]], inc=(j == 3))
            for j in range(4):
                mm(banks[7][:, :], onesb[:], scr[2 + j][:, :].bitcast(BF16)[:, 0:TS], j == 0, j == 3,
                   [bones, bscr[2 + j]], [bbank[7]], inc=(j == 3))
            mean = scr[0]; rstd_ = scr[1]
            cp_("dve", mean[:, 0:TS], banks[6][:, :], [bbank[6]], [bscr[0]])
            tt_("dve", scr[6][:, 0:TS], mean[:, 0:TS], mean[:, 0:TS], ALU.mult, [bscr[0]], [bscr[6]])
            tt_("dve", scr[6][:, 0:TS], banks[7][:, :], scr[6][:, 0:TS], ALU.subtract, [bbank[7], bscr[6]], [bscr[6]])
            MARKS.append((T, "4qkv", S.nissued.get("pe", 0)))
            rot = [0]

            def nbank():
                rot[0] = (rot[0] + 1) % 4
                return rot[0]
            for hp in range(4):
                sl, bsl = w_next("q")
                for half in range(2):
                    h = hp * 2 + half
                    g = h // 4
                    bk = nbank()
                    fm_gemm(bk, sl, 8, 128, half * 64, lambda kc: hT[:, kc, :], [bhT], bsl, M=64)
                    act(Q[0:64, h, :], banks[bk][0:64, :], AF.Copy, [bbank[bk]], bQlo[g], scale=0.125)
            act(scr[6][:, 0:TS], scr[6][:, 0:TS], AF.Sqrt, [bscr[6]], [bscr[6]], bias=EPS, scale=1.0)
            S.op("dve", lambda e: e.reciprocal(out=rstd_[:, 0:TS], in_=scr[6][:, 0:TS]), reads=[bscr[6]], writes=[bscr[1]])
            for j in range(4):
                tmp = scr[2 + j]; btmp = bscr[2 + j]
                tt_("dve", tmp[:, 0:TS], ysb[:, j, :], mean[:, 0:TS], ALU.subtract, [bysb[j], bscr[0]], [btmp])
                tt_("dve", tmp[:, 0:TS], tmp[:, 0:TS], rstd_[:, 0:TS], ALU.mult, [btmp, bscr[1]], [btmp])
            for idx in (0, 1, 2, 4):
                sl, bsl = w_next("kv")
                for g in range(2):
                    bk = nbank()
                    fm_gemm(bk, sl, 8, 128, g * 64, lambda kc: hT[:, kc, :], [bhT], bsl, M=64)
                    if idx in (0, 1):
                        act(kcT[idx][g][:, 16:16 + TS], banks[bk][0:64, :], AF.Copy, [bbank[bk]], [bkcT[idx][g]])
                    elif idx == 2:
                        act(KsT[g][0:64, t0:t0 + TS], banks[bk][0:64, :], AF.Copy, [bbank[bk]], [bKs[g][T]])
                    else:
                        w0 = (T % 2) * TS
                        act(KwT[g][0:64, w0:w0 + TS], banks[bk][0:64, :], AF.Copy, [bbank[bk]], [bKw[g][T % 2]])
            for j in range(4):
                tmp = scr[2 + j]; btmp = bscr[2 + j]
                act(big[:, aT_off + j, :], tmp[:, 0:TS], AF.Silu, [btmp, bsetup], [bbig[aT_off + j]],
                    bias=cvp[:, j * 3 + 2:j * 3 + 3], scale=cvp[:, j * 3 + 1:j * 3 + 2])
            for idx in (3, 5):
                sl, bsl = w_next("vtok")
                vbk = 4 if idx == 3 else 5
                pv = bank3(vbk, 4, 128)
                for tt in range(TT):
                    for kc in range(8):
                        mm(pv[:, tt, :], hT[:, kc, tt * 128:(tt + 1) * 128], sl[:, kc * 128:(kc + 1) * 128],
                           kc == 0, kc == 7, [bhT, bsl], [bbank[vbk]], inc=(kc == 7))
                if idx == 3:
                    dst = Vs[:, T * 4:T * 4 + 4, :, 0:64]
                    wb_ = [bVs[T]]
                else:
                    dst = Vw[:, (T % 2) * 4:(T % 2) * 4 + 4, :, 0:64]
                    wb_ = [bVw[T % 2]]
                act(dst, banks[vbk][:, :].rearrange("p (a g d) -> p a g d", a=4, g=2), AF.Copy, [bbank[vbk]], wb_)
            sl, bsl = w_next("ng")
            pg_ = bank3(6, 4, 24)
            for tt in range(TT):
                for kc in range(8):
                    mm(pg_[:, tt, :], hT[:, kc, tt * 128:(tt + 1) * 128], sl[:, kc * 24:(kc + 1) * 24],
                       kc == 0, kc == 7, [bhT, bsl], [bbank[6]], inc=(kc == 7))
            act(gates[:, :, :], pg_, AF.Sigmoid, [bbank[6]], bgates)
            if T == 0:
                load_bias_tables()
            MARKS.append((T, "5cmp", S.nissued.get("pe", 0)))
            c_lo = max(0, 32 * T - 1)
            c_hi = 32 * T + 30
            n_c = c_hi - c_lo + 1
            col0 = 16 * c_lo - t0 + 16
            for kv in range(2):
                bk = 4 + kv
                for s in range(8):
                    sl, bsl = w_next("w1")
                    for g in range(2):
                        for hc in range(2):
                            r0 = (g * 2 + hc) * 32
                            for pp in range(4):
                                pos = s * 4 + pp
                                rhs = kcT[kv][g][:, col0 + pos:col0 + pos + 16 * (n_c - 1) + 1:16]
                                mm(banks[bk][:, r0:r0 + n_c], sl[0:64, pp * 256 + hc * 128:pp * 256 + hc * 128 + 128],
                                   rhs, pos == 0 and g == 0 and hc == 0, pos == 31, [bsl, bkcT[kv][g]], [bbank[bk]],
                                   sgc=True)
                    if T == 0:
                        for hc in range(2):
                            for pp in range(4):
                                pos = s * 4 + pp
                                mm(banks[bk][:, 256 + hc:256 + hc + 1],
                                   sl[0:64, pp * 256 + hc * 128:pp * 256 + hc * 128 + 128],
                                   peT[:, kv * 32 + pos:kv * 32 + pos + 1], False, pos == 31,
                                   [bsl, bsetup], [bbank[bk]], sgc=True)
                if T == 0:
                    act(cconst[:, kv, :], banks[bk][:, 256:258], AF.Copy, [bbank[bk]], [bcconst])
                for g in range(2):
                    for hc in range(2):
                        r0 = (g * 2 + hc) * 32
                        act(hid[:, kv, g, hc, 0:n_c], banks[bk][:, r0:r0 + n_c], AF.Gelu_apprx_tanh,
                            [bbank[bk], bcconst], [bhid[kv]], bias=cconst[:, kv, hc:hc + 1])
                for g in range(2):
                    cp_("pool", kcT[kv][g][:, 0:16], kcT[kv][g][:, TS:TS + 16], [bkcT[kv][g]], [bkcT[kv][g]])
            sl, bsl = w_next("w2")
            for g in range(2):
                for hc in range(2):
                    mm(banks[6][0:64, g * 32:g * 32 + n_c], sl[:, hc * 64:(hc + 1) * 64], hid[:, 0, g, hc, 0:n_c],
                       hc == 0, hc == 1, [bsl, bhid[0]], [bbank[6]])
                act(kcmpT[g][0:64, c_lo:c_lo + n_c], banks[6][0:64, g * 32:g * 32 + n_c], AF.Copy,
                    [bbank[6]], [bkcmp[g]])
            for g in range(2):
                for hc in range(2):
                    mm(banks[7][0:n_c, g * 64:(g + 1) * 64], hid[:, 1, g, hc, 0:n_c],
                       sl[:, (2 + hc) * 64:(3 + hc) * 64], hc == 0, hc == 1, [bsl, bhid[1]], [bbank[7]])
            act(vst[0:n_c, :, 0:64], banks[7][0:n_c, 0:128].rearrange("p (g d) -> p g d", g=2), AF.Copy,
                [bbank[7]], [bvst])
            cs = c_lo
            while cs <= c_hi:
                ce = min(c_hi, (cs // 128) * 128 + 127)
                n = ce - cs + 1
                S.dma("pool", "vc", vcmp[cs % 128:cs % 128 + n, cs // 128, :, :], vst[cs - c_lo:cs - c_lo + n, :, :],
                      reads=[bvst], writes=[bvcmp])
                cs = ce + 1
            MARKS.append((T, "6attn", S.nissued.get("pe", 0)))
            nch = 1 if c_hi < 128 else 2
            O_c, O_s, O_w = bank3(2, 4, 65), bank3(3, 4, 65), bank3(4, 4, 65)
            IMP = bank3(5, 4, 65)

            def norm_gate(b, slot, Ob, bOb, tt, g):
                dn = f4[:, slot, 0:4]; fr = f4[:, slot, 4:8]
                ts_("dve", dn.unsqueeze(2), Ob[:, :, 64:65], 1e-30, None, ALU.add, None, [bOb], [bf4[slot]])
                S.op("dve", lambda e: e.reciprocal(out=fr, in_=dn), reads=[bf4[slot]], writes=[bf4[slot]])
                gsl = gates[:, tt, g * 12 + b:g * 12 + b + 10:3]
                tt_("dve", fr, fr, gsl, ALU.mult, [bf4[slot], bgates[tt]], [bf4[slot]])
                tt_("dve", osc[:, slot, :].rearrange("p (h d) -> p h d", h=4), Ob[:, :, 0:64],
                    fr.unsqueeze(2).to_broadcast([128, 4, 64]), ALU.mult, [bOb, bf4[slot]], [bosc[slot]])

            def stageA(tt, g, par):
                qt = T * 4 + tt
                qs = slice(tt * 128, (tt + 1) * 128)
                hs = slice(g * 4, g * 4 + 4)
                for ch in range(nch):
                    psS = bank3(6, 4, 128)
                    mm(psS, kcmpT[g][0:65, ch * 128:(ch + 1) * 128], Q[0:65, hs, qs], True, False,
                       [bkcmp[g], bQlo[g][tt], bQhi[g][tt]], [bbank[6]], inc=False)
                    base = 128 * ch - 8 * qt + 248
                    mm(psS, selm[0:17, base:base + 128], mstack[0:17, hs, :], False, True,
                       [bsetup, btab, btab2], [bbank[6]])
                    act(Pc[:, ch, :], banks[6][:, :], AF.Exp, [bbank[6]], [bPc[ch]])
                    yield
                    for h in range(4):
                        mm(O_c[:, h, :], Pc[:, ch, h * 128:(h + 1) * 128], vcmp[:, ch, g, :],
                           ch == 0 and h == 0, ch == nch - 1, [bPc[ch], bvcmp], [bbank[2]], sgc=True, inc=(h == 3))
                    for h in range(4):
                        mm(IMP[:, h, :], Pc[:, ch, h * 128:(h + 1) * 128], aggaug[:, ch, :],
                           ch == 0 and h == 0, ch == nch - 1, [bPc[ch], bsetup], [bbank[5]], sgc=True, inc=(h == 3))
                    yield
                den4 = tk[:, 0, 0:4]; rec4 = tk[:, 0, 8:12]
                ts_("dve", den4.unsqueeze(2), IMP[:, :, 64:65], 1e-30, None, ALU.add, None, [bbank[5]], [btk])
                S.op("dve", lambda e: e.reciprocal(out=rec4, in_=den4), reads=[btk], writes=[btk])
                impa = tk[:, 1, :]
                ts_("dve", impa, IMP[:, 0, 0:64], rec4[:, 0:1], None, ALU.mult, None, [bbank[5], btk], [btk])
                for h in range(1, 4):
                    stt_("dve", impa, IMP[:, h, 0:64], rec4[:, h:h + 1], impa, ALU.mult, ALU.add,
                         [bbank[5], btk], [btk])
                off = 62 - 2 * qt
                tt_("dve", impa, impa, vm[:, off:off + 64], ALU.mult, [btk, bsetup], [btk])
                tt_("dve", impa, impa, am[:, off:off + 64], ALU.add, [btk, bsetup], [btk])
                S.op("dve", lambda e: e.memset(impa[:, 0:1], 1e4), reads=[btk], writes=[btk])
                S.op("dve", lambda e: e.max(out=m8[:, 0:8], in_=impa), reads=[btk], writes=[btk])
                S.op("dve", lambda e: e.match_replace(out=tk[:, 2, :], in_to_replace=m8[:, 0:8], in_values=impa,
                                                      imm_value=-1e9), reads=[btk], writes=[btk])
                S.op("dve", lambda e: e.max(out=m8[:, 8:16], in_=tk[:, 2, :]), reads=[btk], writes=[btk])
                ts_("dve", selpad[:, 64:128], impa, m8[:, 15:16], 1.0, ALU.is_ge, ALU.subtract, [btk], [bselpad])
                norm_gate(0, 0 if par == 0 else 3, O_c, bbank[2], tt, g)
                for _ in range(10):
                    yield
                tb = bank_bf(5)
                S.op("pe", lambda e: e.transpose(out=tb[:, 0:128], in_=selpad[:, :], identity=identb[:]),
                     reads=[bselpad, bsetup], writes=[bbank[5]])
                for h in range(4):
                    ts_("dve", Q[64:128, g * 4 + h, qs], tb[64:128, 0:128], -NEGM, b31bc[64:128, g * 4 + h:g * 4 + h + 1],
                        ALU.mult, ALU.add, [bbank[5], bsetup], [bQhi[g][tt]])

            def make_item(tt, g, par):
                qt = T * 4 + tt
                qs = slice(tt * 128, (tt + 1) * 128)
                hs = slice(g * 4, g * 4 + 4)
                jobs = [("w", kt) for kt in range(max(0, qt - 4), qt + 1)] + [("s", kt) for kt in range(0, qt + 1)]

                def emit_qk(ji):
                    br, kt = jobs[ji]
                    sbk = (0, 1, 7)[ji % 3]
                    psS = bank3(sbk, 4, 128)
                    extra = None
                    if kt == qt:
                        extra = biasD[:, hs, :]
                    elif kt == qt - 1:
                        extra = biasO[:, hs, :]
                    elif br == "w" and kt == qt - 4:
                        extra = w4[:, :].unsqueeze(1).to_broadcast([128, 4, 128])
                    if br == "s":
                        mm(psS, KsT[g][:, kt * 128:(kt + 1) * 128], Q[:, hs, qs], True, extra is None,
                           [bKs[g][kt // 4], befull[g], bQlo[g][tt], bQhi[g][tt]], [bbank[sbk]], inc=(extra is None))
                    else:
                        w0 = (kt % 8) * 128
                        mm(psS, KwT[g][0:65, w0:w0 + 128], Q[0:65, hs, qs], True, extra is None,
                           [bKw[g][(kt // 4) % 2], bQlo[g][tt], bQhi[g][tt]], [bbank[sbk]], inc=(extra is None))
                    if extra is not None:
                        mm(psS, identb[:], extra, False, True, [bsetup, btab], [bbank[sbk]])
                    pi = ji % NPS
                    act(Ps[:, pi, :], banks[sbk][:, :], AF.Exp, [bbank[sbk]], [bPs[pi]])

                def emit_pv(ji):
                    br, kt = jobs[ji]
                    pi = ji % NPS
                    if br == "s":
                        first, last = kt == 0, kt == qt
                        vv = Vs[:, kt, g, :]
                        vb = bVs[kt // 4]
                        Ob, bOb = O_s, bbank[3]
                    else:
                        first, last = kt == max(0, qt - 4), kt == qt
                        vv = Vw[:, kt % 8, g, :]
                        vb = bVw[(kt // 4) % 2]
                        Ob, bOb = O_w, bbank[4]
                    for h in range(4):
                        mm(Ob[:, h, :], Ps[:, pi, h * 128:(h + 1) * 128], vv, first and h == 0, last,
                           [bPs[pi], vb], [bOb], sgc=True, inc=(h == 3))

                def combine():
                    norm_gate(1, 1, O_s, bbank[3], tt, g)
                    norm_gate(2, 2, O_w, bbank[4], tt, g)
                    slots = [0 if par == 0 else 3, 1, 2]
                    po = bank3(5, 2, 128)
                    for hp in range(2):
                        for bi, sl_ in enumerate(slots):
                            mm(po[:, hp, :], osc[:, sl_, hp * 128:(hp + 1) * 128], identb[:], bi == 0, bi == 2,
                               [bosc[sl_], bsetup], [bbank[5]], inc=(bi == 2))
                    cp_("dve", big[:, oT_off + g * 2:oT_off + g * 2 + 2, qs], po, [bbank[5]],
                        [bbig[oT_off + g * 2], bbig[oT_off + g * 2 + 1]])
                return jobs, emit_qk, emit_pv, combine

            if T > 0:
                final_norm(T - 1)
            items = [(tt, g) for tt in range(TT) for g in range(2)]
            for _ in stageA(items[0][0], items[0][1], 0):
                pass
            cur = make_item(items[0][0], items[0][1], 0)
            cur[1](0)
            if len(cur[0]) > 1:
                cur[1](1)
            for k in range(len(items)):
                jobs, emit_qk, emit_pv, combine = cur
                genA = stageA(items[k + 1][0], items[k + 1][1], (k + 1) % 2) if k + 1 < len(items) else None
                for ji in range(len(jobs)):
                    if ji + 2 < len(jobs):
                        emit_qk(ji + 2)
                    emit_pv(ji)
                    if genA is not None:
                        next(genA, None)
                if genA is not None:
                    for _ in genA:
                        pass
                    nxt = make_item(items[k + 1][0], items[k + 1][1], (k + 1) % 2)
                    nxt[1](0)
                    if len(nxt[0]) > 1:
                        nxt[1](1)
                else:
                    nxt = None
                combine()
                cur = nxt
            if DEBUG:
                for c in range(4):
                    S.dma("pool", "dbg", dbg["oT"][:, c * SEQ + t0:c * SEQ + t0 + TS], big[:, oT_off + c, :],
                          reads=[bbig[oT_off + c]])
                    S.dma("pool", "dbg", dbg["aT"][:, c * SEQ + t0:c * SEQ + t0 + TS], big[:, aT_off + c, :],
                          reads=[bbig[aT_off + c]])
            MARKS.append((T, "7merge", S.nissued.get("pe", 0)))
            for j in range(8):
                b0 = (j % 2) * 4
                s0, s1 = scr[(j % 2) * 2], scr[(j % 2) * 2 + 1]
                bs0, bs1 = bscr[(j % 2) * 2], bscr[(j % 2) * 2 + 1]
                slc, bc_ = w_next("ca")
                sla, ba_ = w_next("mgA")
                slb, bb_ = w_next("mgB")
                fm_gemm(b0, slc, 4, 128, 0, lambda kc: big[:, aT_off + kc, :], bbig[aT_off:aT_off + 4], bc_)
                for kc in range(4):
                    mm(banks[b0 + 1][:, :], slc[:, 512 + kc * 128:512 + (kc + 1) * 128], big[:, oT_off + kc, :],
                       kc == 0, kc == 3, [bc_] + bbig[oT_off:oT_off + 4], [bbank[b0 + 1]], inc=(kc == 3))
                fm_gemm(b0 + 2, sla, 8, 128, 0, lambda kc: hT[:, kc, :], [bhT], ba_)
                fm_gemm(b0 + 3, slb, 8, 128, 0, lambda kc: hT[:, kc, :], [bhT], bb_)
                act(s0[:, 0:TS], banks[b0 + 2][:, :], AF.Sigmoid, [bbank[b0 + 2]], [bs0])
                act(s1[:, 0:TS], banks[b0 + 3][:, :], AF.Sigmoid, [bbank[b0 + 3]], [bs1])
                tt_("dve", s0[:, 0:TS], banks[b0][:, :], s0[:, 0:TS], ALU.mult, [bbank[b0], bs0], [bs0])
                tt_("dve", s1[:, 0:TS], banks[b0 + 1][:, :], s1[:, 0:TS], ALU.mult, [bbank[b0 + 1], bs1], [bs1])
                tt_("dve", yT[:, j, :], s0[:, 0:TS], s1[:, 0:TS], ALU.add, [bs0, bs1], [bbig[j]])

            def tm_gemm(kind, nsl, lhs_fn, lbufs, evac):
                for half in range(2):
                    bb = (half % 2) * 4
                    for s in range(nsl):
                        sl, bsl = w_next(kind)
                        for tt in range(TT):
                            for k2 in range(2):
                                kc = 2 * s + k2
                                mm(banks[bb + tt][:, :], lhs_fn(kc, tt), sl[:, k2 * 512:(k2 + 1) * 512],
                                   kc == 0, kc == 2 * nsl - 1, [bsl] + lbufs(kc), [bbank[bb + tt]],
                                   inc=(tt == TT - 1 and k2 == 1))
                    for tt in range(TT):
                        evac(half, tt, bb + tt)

            def resid_add(half, tt, bk):
                tt_("dve", xres[:, tt, half * 512:(half + 1) * 512], xres[:, tt, half * 512:(half + 1) * 512],
                    banks[bk][:, :], ALU.add, [bx[tt], bbank[bk]], [bx[tt]])

            tm_gemm("out", 4, lambda kc, tt: yT[:, kc, tt * 128:(tt + 1) * 128], lambda kc: [bbig[kc]], resid_add)
            if DEBUG:
                for tt in range(TT):
                    S.dma("pool", "dbg", dbg["x1"][t0 + tt * 128:t0 + (tt + 1) * 128, :], xres[:, tt, :], reads=[bx[tt]])
            MARKS.append((T, "8norm", S.nissued.get("pe", 0)))
            pbv = bank_bf(2)
            for tt in range(TT):
                for kc in range(2):
                    S.op("pe", lambda e: e.transpose(out=pbv[:, (tt * 2 + kc) * 128:(tt * 2 + kc + 1) * 128],
                                                     in_=pbf[:, tt, kc * 128:(kc + 1) * 128], identity=identb[:]),
                         reads=[bpbf[tt], bsetup], writes=[bbank[2]], acc=True, inc=(tt == TT - 1 and kc == 1))
            act(pT[:].rearrange("p k (t q) -> p t k q", t=TT), pbv[:, 0:1024].rearrange("p (t k q) -> p t k q", t=TT, k=2),
                AF.Copy, [bbank[2]], [bpT])
            rmsnorm_to_hT(2)
            MARKS.append((T, "9ffn", S.nissued.get("pe", 0)))
            def ffn_up(i):
                par = i % 2
                for which, kind in enumerate(("upg", "upv")):
                    sl, bsl = w_next(kind)
                    bk = par * 4 + which
                    fm_gemm(bk, sl, 8, 128, 0, lambda kc: hT[:, kc, :], [bhT], bsl)
                    ci = which * NFF + i
                    ub = scr[par * 2 + which][:, :].bitcast(BF16); bub = bscr[par * 2 + which]
                    if which == 0:
                        for k3 in range(3):
                            ts_("dve", dgf[:, par, k3, :], identb[:], ffw[:, ci, k3:k3 + 1], None, ALU.mult, None,
                                [bsetup, bident], [bdgf[par]])
                    cp_("pool", ub[:, 0:2], uhalo[:, ci, :], [buhalo[ci]], [bub])
                    act(ub[:, 2:2 + TS], banks[bk][:, :], AF.Copy, [bbank[bk]], [bub])
                    cp_("pool", uhalo[:, ci, :], ub[:, TS:TS + 2], [bub], [buhalo[ci]])

            def ffn_conv(i):
                par = i % 2
                ubg = scr[par * 2][:, :].bitcast(BF16); bubg = bscr[par * 2]
                ubv = scr[par * 2 + 1][:, :].bitcast(BF16); bubv = bscr[par * 2 + 1]
                cb = par * 4 + 2
                for k3 in range(3):
                    mm(banks[cb][:, :], dgf[:, par, k3, :], ubg[:, k3:k3 + TS], k3 == 0, k3 == 2,
                       [bdgf[par], bubg], [bbank[cb]], inc=(k3 == 2))
                gl = scr[4 + par]; bgl = bscr[4 + par]
                act(gl[:, 0:TS], banks[cb][:, :], AF.Gelu_apprx_tanh, [bbank[cb], bsetup], [bgl],
                    bias=ffw[:, i, 3:4])
                cv = scr[6]; bcv = bscr[6]
                civ = NFF + i
                ts_("dve", cv[:, 0:TS], ubv[:, 2:2 + TS], ffw[:, civ, 2:3], ffw[:, civ, 3:4], ALU.mult, ALU.add,
                    [bubv, bsetup], [bcv])
                stt_("dve", cv[:, 0:TS], ubv[:, 1:1 + TS], ffw[:, civ, 1:2], cv[:, 0:TS], ALU.mult, ALU.add,
                     [bubv, bcv, bsetup], [bcv])
                stt_("dve", cv[:, 0:TS], ubv[:, 0:TS], ffw[:, civ, 0:1], cv[:, 0:TS], ALU.mult, ALU.add,
                     [bubv, bcv, bsetup], [bcv])
                tt_("dve", big[:, i, :], cv[:, 0:TS], gl[:, 0:TS], ALU.mult, [bcv, bgl], [bbig[i]])

            ffn_up(0)
            for i in range(NFF):
                if i + 1 < NFF:
                    ffn_up(i + 1)
                ffn_conv(i)
            MARKS.append((T, "10down", S.nissued.get("pe", 0)))
            tm_gemm("down", NFF // 2, lambda kc, tt: big[:, kc, tt * 128:(tt + 1) * 128], lambda kc: [bbig[kc]],
                    resid_add)
            if T + 1 < NT:
                for tt in range(TT):
                    ap, bufs = src_alias(tt)
                    S.dma("sp", "xa%d" % tt, ap, x_d[t0 + TS + tt * 128:t0 + TS + (tt + 1) * 128, :], writes=bufs)
            MARKS.append((T, "11ple", S.nissued.get("pe", 0)))
            rmsnorm_to_hT(0 if T + 1 < NT else 3)
            if T + 1 < NT:
                norm_stats(src_alias, 16)
            for half in range(2):
                hsl = slice(half * 512, (half + 1) * 512)
                sl, bsl = w_next("ple")
                for tt in range(TT):
                    for k2 in range(2):
                        mm(banks[tt][:, :], pT[:, k2, tt * 128:(tt + 1) * 128], sl[:, k2 * 512:(k2 + 1) * 512],
                           k2 == 0, k2 == 1, [bsl, bpT], [bbank[tt]], inc=(k2 == 1))
                for s in range(4):
                    sl, bsl = w_next("pgate")
                    for tt in range(TT):
                        for k2 in range(2):
                            kc = 2 * s + k2
                            mm(banks[4 + tt][:, :], hT[:, kc, tt * 128:(tt + 1) * 128], sl[:, k2 * 512:(k2 + 1) * 512],
                               kc == 0, kc == 7, [bsl, bhT], [bbank[4 + tt]], inc=(tt == TT - 1 and k2 == 1))
                for tt in range(TT):
                    act(ysb[:, tt, :], banks[tt][:, :], AF.Copy, [bbank[tt]], [bysb[tt]])
                for tt in range(TT):
                    sc_, bsc_ = scr[tt % 2], bscr[tt % 2]
                    act(sc_[:, 0:TS], banks[4 + tt][:, :], AF.Sigmoid, [bbank[4 + tt]], [bsc_])
                    tt_("dve", sc_[:, 0:TS], sc_[:, 0:TS], ysb[:, tt, :], ALU.mult, [bsc_, bysb[tt]], [bsc_])
                    tt_("dve", xres[:, tt, hsl], xres[:, tt, hsl], sc_[:, 0:TS], ALU.add, [bx[tt], bsc_], [bx[tt]])
            MARKS.append((T, "12final", S.nissued.get("pe", 0)))
            if T + 1 < NT:
                norm_apply(src_alias, 16)
                load_gain(3)
            if T + 1 == NT:
                final_norm(T)
        fin = list(bysb) + bscr[0:4]
        S.wait_all("sp", fin)
        S.wait_all("pool", fin)
        if DEBUG:
            dd = Buf()
            dd.w = ("dbg", S.count["dbg"])
            S.wait_all("pool", [dd])
    return nc


def kernel(x, p, rel_bias, norm_mix, w_in, conv_dw_w, conv_dw_b, conv_ln_g, conv_ln_b,
           w_conv_out, cmp_pe_k, cmp_pe_v, w_ck1, w_ck2, w_cv1, w_cv2, w_attn_out, w_out,
           norm_ffn, w_up, ffn_dw_w, ffn_dw_b, w_down, norm_ple, w_ple_gate, w_ple, norm_final):
    f = lambda a: np.ascontiguousarray(np.asarray(a, dtype=np.float32))
    x = f(x); p = f(p)
    wflat = build_wflat(f(w_in)[0], f(w_conv_out)[0], f(w_ck1)[0], f(w_ck2)[0], f(w_cv1)[0], f(w_cv2)[0],
                        f(w_attn_out)[0], f(w_out)[0], f(w_up)[0], f(w_down)[0], f(w_ple_gate)[0], f(w_ple)[0])
    consts = build_consts()
    gains = np.stack([f(norm_mix)[0], f(norm_ffn)[0], f(norm_ple)[0], f(norm_final)], axis=0)
    cvw = f(conv_dw_w)[0].T.reshape(4, 128, 31).transpose(1, 0, 2).reshape(128, 124)
    cvp = np.stack([f(conv_dw_b)[0], f(conv_ln_g)[0], f(conv_ln_b)[0]], axis=1).reshape(4, 128, 3)
    cvp = cvp.transpose(1, 0, 2).reshape(128, 12)
    ffw = np.concatenate([f(ffn_dw_w)[0], f(ffn_dw_b)[0][None, :]], axis=0).T.reshape(44, 128, 4)
    ffw = ffw.transpose(1, 0, 2).reshape(128, 176)
    pet = np.concatenate([f(cmp_pe_k)[0].T, f(cmp_pe_v)[0].T], axis=1)
    shared = {
        "wflat": wflat, "gains": f(gains), "relb": f(rel_bias), "cvw": f(cvw), "cvp": f(cvp), "ffw": f(ffw),
        "pet": f(pet),
    }
    for k, v in consts.items():
        shared[k] = f(v)
    nc = build_program()
    in_maps = []
    for b in range(8):
        m = dict(shared)
        m["x"] = x[b]
        m["p"] = p[0, b]
        in_maps.append(m)
    res = run_bass_kernel_spmd(nc, in_maps, core_ids=list(range(8)))
    kernel.last_results = res
    return np.stack([np.asarray(r["y"], dtype=np.float32) for r in res.results], axis=0)
```
